# Optimizing a Trainium2 kernel written in Bass

```python
import math
import jax, jax.numpy as jnp
from jax import lax
import numpy as np

D_MODEL = 1024
BATCH = 8
SEQ = 4096
DEPTH = 1
DEC_BATCH = 8
DEC_SEQ = 32
PAST_LEN = 2048

CHUNK = 64
Q_BLOCK = 128
EPS = 1e-6

GLA_HEADS = 4
GLA_DK = D_MODEL // 2 // GLA_HEADS
GLA_DV = D_MODEL // GLA_HEADS
GLA_KEY = GLA_HEADS * GLA_DK
GLA_VAL = GLA_HEADS * GLA_DV
GLA_RANK = 16
GLA_TAU = 16.0

DIFF_HEADS = 8
DIFF_HD = D_MODEL // (2 * DIFF_HEADS)
DIFF_VD = 2 * DIFF_HD
DIFF_QK = DIFF_HEADS * 2 * DIFF_HD
DIFF_VAL = DIFF_HEADS * DIFF_VD
ROT_DIM = DIFF_HD // 4
ROPE_THETA = 500000.0

IN_SIZES = (GLA_KEY, GLA_KEY, GLA_VAL, GLA_RANK, GLA_VAL,
            DIFF_QK, DIFF_QK, DIFF_VAL, DIFF_VAL,
            D_MODEL, D_MODEL)
IN_TOTAL = sum(IN_SIZES)
IN_SPLITS = tuple(int(s) for s in np.cumsum(IN_SIZES)[:-1])

kernel_name = 'hybrid_gla_diffattn_stream_step'

F32 = jnp.float32


def rmsnorm(x, g):
    xf = x.astype(F32)
    y = xf * lax.rsqrt(jnp.mean(xf * xf, axis=-1, keepdims=True) + EPS) * g.astype(F32)
    return y.astype(x.dtype)


def rope(x, pos):
    half = ROT_DIM // 2
    inv = ROPE_THETA ** (-jnp.arange(0, ROT_DIM, 2, dtype=F32) / ROT_DIM)
    ang = pos.astype(F32)[:, None] * inv[None, :]
    cos = jnp.cos(ang)[:, None, None, :]
    sin = jnp.sin(ang)[:, None, None, :]
    xr = x[..., :ROT_DIM].astype(F32)
    x1, x2 = xr[..., :half], xr[..., half:]
    rot = jnp.concatenate([x1 * cos - x2 * sin, x2 * cos + x1 * sin], axis=-1)
    return jnp.concatenate([rot.astype(x.dtype), x[..., ROT_DIM:]], axis=-1)


def in_proj(xn, w_in, gla_w_a2, gla_b_a):
    B, T = xn.shape[:2]
    u = xn @ w_in
    gq, gk, gv, gr, gg, dq, dk, dv, dg, ma, mb = jnp.split(u, IN_SPLITS, axis=-1)
    log_a = jax.nn.log_sigmoid((gr @ gla_w_a2 + gla_b_a).astype(F32)) / GLA_TAU
    return (gq.reshape(B, T, GLA_HEADS, GLA_DK), gk.reshape(B, T, GLA_HEADS, GLA_DK),
            gv.reshape(B, T, GLA_HEADS, GLA_DV), log_a.reshape(B, T, GLA_HEADS, GLA_DK), gg,
            dq.reshape(B, T, DIFF_HEADS, 2, DIFF_HD), dk.reshape(B, T, DIFF_HEADS, 2, DIFF_HD),
            dv.reshape(B, T, DIFF_HEADS, DIFF_VD), dg, ma, mb)


def gla_block(state, q, k, v, log_a):
    C = q.shape[1]
    b = jnp.cumsum(log_a, axis=1)
    qf = q.astype(F32) * (GLA_DK ** -0.5)
    kf = k.astype(F32)
    vf = v.astype(F32)
    o_inter = jnp.einsum('bthd,bhde->bthe', qf * jnp.exp(b), state)
    causal = jnp.tril(jnp.ones((C, C), dtype=bool))[None, :, :, None, None]
    diff = b[:, :, None] - b[:, None, :]
    decay = jnp.exp(jnp.where(causal, diff, -jnp.inf))
    A = jnp.einsum('bthd,bshd,btshd->bhts', qf, kf, decay)
    o_intra = jnp.einsum('bhts,bshe->bthe', A, vf)
    b_last = b[:, -1]
    k_dec = kf * jnp.exp(b_last[:, None] - b)
    new_state = jnp.exp(b_last)[..., None] * state + jnp.einsum('bshd,bshe->bhde', k_dec, vf)
    return new_state, o_inter + o_intra


def gla_prompt(q, k, v, log_a):
    B, T = q.shape[:2]
    nc = T // CHUNK

    def to_chunks(a):
        return a.reshape(B, nc, CHUNK, *a.shape[2:]).swapaxes(0, 1)

    state0 = jnp.zeros((B, GLA_HEADS, GLA_DK, GLA_DV), F32)
    s_fin, o = lax.scan(lambda s, inp: gla_block(s, *inp), state0,
                        (to_chunks(q), to_chunks(k), to_chunks(v), to_chunks(log_a)))
    return o.swapaxes(0, 1).reshape(B, T, GLA_HEADS, GLA_DV), s_fin


def diff_attend(q, k, v, mask, lam):
    s = jnp.einsum('bqhmd,bkhmd->bhmqk', q, k).astype(F32) * (DIFF_HD ** -0.5)
    if mask is not None:
        s = jnp.where(mask, s, -jnp.inf)
    p = jax.nn.softmax(s, axis=-1)
    attn = p[:, :, 0] - lam * p[:, :, 1]
    return jnp.einsum('bhqk,bkhe->bqhe', attn, v.astype(F32))


def diff_prompt(q, k, v, lam):
    B, T = q.shape[:2]
    nb = T // Q_BLOCK
    qb = q.reshape(B, nb, Q_BLOCK, *q.shape[2:]).swapaxes(0, 1)
    k_chunk = jnp.arange(T, dtype=jnp.int32) // CHUNK

    def blk(args):
        q_i, i = args
        q_pos = i * Q_BLOCK + jnp.arange(Q_BLOCK, dtype=jnp.int32)
        mask = (q_pos // CHUNK)[:, None] >= k_chunk[None, :]
        return diff_attend(q_i, k, v, mask, lam)

    ob = lax.map(blk, (qb, jnp.arange(nb, dtype=jnp.int32)))
    return ob.swapaxes(0, 1).reshape(B, T, DIFF_HEADS, DIFF_VD)


def out_merge(x, o_gla, o_diff, gg, dg, ma, mb, gla_norm_g, diff_subln_g, lambda_init,
              w_proj_gla, w_proj_diff, w_out, post_norm_g):
    B, T = x.shape[:2]
    yg = rmsnorm(o_gla.astype(x.dtype), gla_norm_g).reshape(B, T, GLA_VAL) * jax.nn.silu(gg)
    yd = (rmsnorm(o_diff.astype(x.dtype), diff_subln_g) * (1.0 - lambda_init)).reshape(B, T, DIFF_VAL) * jax.nn.silu(dg)
    merged = jax.nn.sigmoid(ma) * (yg @ w_proj_gla) + jax.nn.sigmoid(mb) * (yd @ w_proj_diff)
    return x + rmsnorm(merged @ w_out, post_norm_g)


def setup_inputs(seed: int = 0) -> dict:
    key = jax.random.key(seed)
    ks = jax.random.split(key, 20)
    n = jax.random.normal
    return {
        'x_prompt': n(ks[0], (BATCH, SEQ, D_MODEL), F32),
        'x_sample': n(ks[1], (DEC_BATCH, DEC_SEQ, D_MODEL), F32),
        'cache_diff_k': n(ks[2], (DEPTH, DEC_BATCH, PAST_LEN, DIFF_HEADS, DIFF_VD), F32),
        'cache_diff_v': n(ks[3], (DEPTH, DEC_BATCH, PAST_LEN, DIFF_HEADS, DIFF_VD), F32),
        'state_gla': n(ks[4], (DEPTH, DEC_BATCH, GLA_HEADS, GLA_DK, GLA_DV), F32),
        'pre_norm_g': 1.0 + 0.05 * n(ks[5], (DEPTH, D_MODEL), F32),
        'w_in': n(ks[6], (DEPTH, D_MODEL, IN_TOTAL), F32) * D_MODEL ** -0.5,
        'gla_w_a2': n(ks[7], (DEPTH, GLA_RANK, GLA_KEY), F32) * GLA_RANK ** -0.5,
        'gla_b_a': 0.1 * n(ks[8], (DEPTH, GLA_KEY), F32),
        'gla_norm_g': 1.0 + 0.05 * n(ks[9], (DEPTH, GLA_DV), F32),
        'diff_lambda_q1': 0.1 * n(ks[10], (DEPTH, DIFF_HD), F32),
        'diff_lambda_k1': 0.1 * n(ks[11], (DEPTH, DIFF_HD), F32),
        'diff_lambda_q2': 0.1 * n(ks[12], (DEPTH, DIFF_HD), F32),
        'diff_lambda_k2': 0.1 * n(ks[13], (DEPTH, DIFF_HD), F32),
        'diff_subln_g': 1.0 + 0.05 * n(ks[14], (DEPTH, DIFF_VD), F32),
        'w_proj_gla': n(ks[15], (DEPTH, GLA_VAL, D_MODEL), F32) * GLA_VAL ** -0.5,
        'w_proj_diff': n(ks[16], (DEPTH, DIFF_VAL, D_MODEL), F32) * DIFF_VAL ** -0.5,
        'w_out': n(ks[17], (DEPTH, D_MODEL, D_MODEL), F32) * D_MODEL ** -0.5,
        'post_norm_g': 1.0 + 0.05 * n(ks[18], (DEPTH, D_MODEL), F32),
    }


def reference(x_prompt, x_sample, cache_diff_k, cache_diff_v, state_gla, pre_norm_g, w_in,
              gla_w_a2, gla_b_a, gla_norm_g, diff_lambda_q1, diff_lambda_k1, diff_lambda_q2,
              diff_lambda_k2, diff_subln_g, w_proj_gla, w_proj_diff, w_out, post_norm_g):
    hp, hs = x_prompt, x_sample
    kp_l, vp_l, sp_l, ks_l, vs_l, ss_l = [], [], [], [], [], []
    for l in range(DEPTH):
        lambda_init = 0.8 - 0.6 * math.exp(-0.3 * l)
        lam = (jnp.exp(jnp.sum(diff_lambda_q1[l].astype(F32) * diff_lambda_k1[l].astype(F32)))
               - jnp.exp(jnp.sum(diff_lambda_q2[l].astype(F32) * diff_lambda_k2[l].astype(F32)))
               + lambda_init)
        merge_args = (gla_norm_g[l], diff_subln_g[l], lambda_init, w_proj_gla[l], w_proj_diff[l],
                      w_out[l], post_norm_g[l])

        B, T = hp.shape[:2]
        gq, gk, gv, la, gg, dq, dk, dv, dg, ma, mb = in_proj(rmsnorm(hp, pre_norm_g[l]), w_in[l], gla_w_a2[l], gla_b_a[l])
        pos = jnp.arange(T, dtype=jnp.int32)
        dq, dk = rope(dq, pos), rope(dk, pos)
        o_gla, s_p = gla_prompt(gq, gk, gv, la)
        o_diff = diff_prompt(dq, dk, dv, lam)
        hp = out_merge(hp, o_gla, o_diff, gg, dg, ma, mb, *merge_args)
        kp_l.append(dk.reshape(B, T, DIFF_HEADS, DIFF_VD))
        vp_l.append(dv)
        sp_l.append(s_p)

        Bs, Ts = hs.shape[:2]
        gq, gk, gv, la, gg, dq, dk, dv, dg, ma, mb = in_proj(rmsnorm(hs, pre_norm_g[l]), w_in[l], gla_w_a2[l], gla_b_a[l])
        pos_s = PAST_LEN + jnp.arange(Ts, dtype=jnp.int32)
        dq, dk = rope(dq, pos_s), rope(dk, pos_s)
        s_s, o_gla_s = gla_block(state_gla[l].astype(F32), gq, gk, gv, la)
        past_k = cache_diff_k[l].reshape(Bs, -1, DIFF_HEADS, 2, DIFF_HD).astype(dk.dtype)
        k_all = jnp.concatenate([past_k, dk], axis=1)
        v_all = jnp.concatenate([cache_diff_v[l].astype(dv.dtype), dv], axis=1)
        o_diff_s = diff_attend(dq, k_all, v_all, None, lam)
        hs = out_merge(hs, o_gla_s, o_diff_s, gg, dg, ma, mb, *merge_args)
        ks_l.append(dk.reshape(Bs, Ts, DIFF_HEADS, DIFF_VD))
        vs_l.append(dv)
        ss_l.append(s_s)

    return (hp, hs, jnp.stack(kp_l), jnp.stack(vp_l), jnp.stack(sp_l),
            jnp.stack(ks_l), jnp.stack(vs_l), jnp.stack(ss_l))
```

```python
import math
from contextlib import ExitStack
import numpy as np
import concourse.bass as bass
import concourse.mybir as mybir
from concourse.bass_utils import run_bass_kernel_spmd

F32 = mybir.dt.float32
BF16 = mybir.dt.bfloat16
AF = mybir.ActivationFunctionType
ALU = mybir.AluOpType
AX = mybir.AxisListType

T = 4096
TS = 32
TT = T + TS
PAST = 2048
D = 1024
C_GQ, C_GK, C_GV, C_GR, C_GG = 0, 512, 1024, 2048, 2064
C_DQ, C_DK, C_DV, C_DG, C_MA, C_MB = 3088, 4112, 5136, 6160, 7184, 8208
IN_TOTAL = 9232
EPS = 1e-6
TILES = [(i * 128, 128) for i in range(32)] + [(T, TS)]
BLOCKS = [(i * 512, 512) for i in range(8)] + [(T, TS)]
NQB = 16
DBG = {'nh': 8, 'stage': 9, 'nqb': 16, 'skip': '', 'qb0': 0}


class Buf:
    __slots__ = ('w', 'r', 'excl')

    def __init__(self, excl=False):
        self.w = None
        self.r = {}
        self.excl = excl


class Sched:
    def __init__(self, nc, esems):
        self.nc = nc
        self.names = ['pe', 'act', 'dve', 'pool', 'sp']
        self.q = {k: [] for k in self.names}
        self.esem = esems
        self.cnt = {k: 0 for k in self.names}
        self.seen = {k: {} for k in self.names}
        self.pend = {k: ([], []) for k in self.names}
        self.last = {}

    def _waits(self, eng, toks):
        ws = []
        for t in toks:
            if t is None:
                continue
            if eng == 'pe' and t[2] == 'e_pe':
                continue
            if t[2] == 'e_' + eng and t[1] <= self.cnt[eng] - 3:
                continue
            if self.seen[eng].get(t[2], 0) < t[1]:
                self.seen[eng][t[2]] = t[1]
                ws.append(t)
        return ws

    def _deps(self, reads, writes):
        toks = []
        for b in reads:
            toks.append(b.w)
            if b.excl:
                toks.extend(b.r.values())
        for b in writes:
            toks.append(b.w)
            toks.extend(b.r.values())
        return toks

    def _commit(self, eng, tok, reads, writes):
        pr, pw = self.pend[eng]
        pr.extend(reads)
        pw.extend(writes)
        if tok is not None:
            for b in pr:
                if b.excl:
                    b.w = tok
                    b.r = {}
                else:
                    b.r[tok[2]] = tok
            for b in pw:
                b.w = tok
                b.r = {}
            self.pend[eng] = ([], [])
            self.last[tok[2]] = tok

    def op(self, eng, fn, reads=(), writes=(), signal=True, extra=()):
        ws = self._waits(eng, self._deps(reads, writes) + list(extra))
        tok = None
        if signal:
            self.cnt[eng] += 1
            tok = (self.esem[eng], self.cnt[eng], 'e_' + eng)
        self.q[eng].append((ws, fn, tok, 1))
        self._commit(eng, tok, reads, writes)
        return tok

    def dma(self, eng, fn, dsem, reads=(), writes=(), extra=()):
        ws = self._waits(eng, self._deps(reads, writes) + list(extra))
        dsem[1] += 16
        tok = (dsem[0], dsem[1], dsem[2])
        self.q[eng].append((ws, fn, tok, 16))
        for b in reads:
            b.r[tok[2]] = tok
        for b in writes:
            b.w = tok
            b.r = {}
        self.last[tok[2]] = tok
        return tok

    def barrier(self):
        toks = list(self.last.values())
        for e in self.names:
            ws = self._waits(e, toks)
            if ws:
                self.q[e].append((ws, None, None, 0))

    def emit(self, block):
        def mk(name):
            def body(e):
                for ws, fn, tok, inc in self.q[name]:
                    for w in ws:
                        e.wait_ge(w[0], w[1])
                    if fn is None:
                        continue
                    ins = fn(e)
                    if tok is not None:
                        ins.then_inc(tok[0], inc)
            return body
        block.tensor(mk('pe'))
        block.scalar(mk('act'))
        block.vector(mk('dve'))
        block.gpsimd(mk('pool'))
        block.sync(mk('sp'))


def build(stop=None):
    nc = bass.Bass("TRN2", target_bir_lowering=False)

    def din(name, shape, dt=F32):
        return nc.dram_tensor(name, shape, dt, kind="ExternalInput").ap()

    def dout(name, shape, dt=F32):
        return nc.dram_tensor(name, shape, dt, kind="ExternalOutput").ap()

    xp = din("xp", [T, D])
    xs = din("xs", [TS, D])
    ck = din("ck", [PAST, D])
    cv = din("cv", [PAST, D])
    sg = din("sg", [4, 128, 256])
    wl = din("wl", [128, 8, IN_TOTAL])
    wa2 = din("wa2", [16, 512])
    ba = din("ba", [128, 4])
    g_pre = din("g_pre", [1, D])
    g_post = din("g_post", [1, D])
    g_gla = din("g_gla", [128, 2])
    g_sub = din("g_sub", [128, 1])
    lam4 = din("lam4", [1, 256])
    wpg = din("wpg", [128, 8, D])
    wpd = din("wpd", [128, 8, D])
    wo = din("wo", [128, 8, D])
    rope = din("rope", [TT, 32])
    yp = dout("yp", [T, D])
    ys = dout("ys", [TS, D])
    nkp = dout("nkp", [T, D])
    nvp = dout("nvp", [T, D])
    ngp = dout("ngp", [4, 128, 256])
    nks = dout("nks", [TS, D])
    nvs = dout("nvs", [TS, D])
    ngs = dout("ngs", [4, 128, 256])
    if stop is None:
        ygT_d = nc.dram_tensor("ygT_scr", [D, TT], BF16).ap()
        ydT_d = nc.dram_tensor("ydT_scr", [D, TT], BF16).ap()
    else:
        ygT_d = dout("ygT_scr", [D, TT], BF16)
        ydT_d = dout("ydT_scr", [D, TT], BF16)
        dbg_xnT = dout("dbg_xnT", [128, 8, TT], BF16)

    def xrows(t0, n):
        return xp[t0:t0 + n, :] if t0 < T else xs[0:n, :]

    with ExitStack() as es:
        ARENA = 106200
        arena = es.enter_context(nc.sbuf_tensor("arena", [128, ARENA], BF16))
        aoff = [0]

        def sb(name, shape, dt, stack=None):
            n = 1
            for d_ in shape[1:]:
                n *= d_
            nb = n * (4 if dt == F32 else 2)
            nb = (nb + 31) // 32 * 32
            o = aoff[0]
            aoff[0] += nb // 2
            assert aoff[0] <= ARENA, (name, aoff[0])
            ap = arena[:, o:o + nb // 2]
            if dt == F32:
                ap = ap.bitcast(F32)
            ap = ap[:, 0:n]
            if len(shape) == 3:
                ap = ap.rearrange("p (a b) -> p a b", a=shape[1])
            elif len(shape) == 4:
                ap = ap.rearrange("p (a b c) -> p a b c", a=shape[1], b=shape[2])
            return ap

        def sem(name):
            return es.enter_context(nc.semaphore(name))

        nds = [0]

        def dsem():
            nds[0] += 1
            return [sem("ds%d" % nds[0]), 0, "ds%d" % nds[0]]

        esems = {k: sem("es_" + k) for k in ['pe', 'act', 'dve', 'pool', 'sp']}
        S = Sched(nc, esems)
        psum_all = es.enter_context(nc.psum_tensor("psum_all", [128, 8, 512], F32))
        banks = [psum_all[:, i, :] for i in range(8)]
        BK = [Buf(excl=True) for _ in range(8)]
        bankbuf = [[BK[i], BK[i]] for i in range(8)]

        ident = sb("ident", [128, 128], BF16)
        identf = sb("identf", [128, 128], F32)
        ones_bf = sb("ones_bf", [128, 128], BF16)
        ones_f = sb("ones_f", [128, 128], F32)
        trimask = sb("trimask", [128, 128], BF16)
        dmask = sb("dmask", [128, 2, 2, 256], BF16)
        g_bc = sb("g_bc", [128, D], F32)
        gpost_bc = sb("gpost_bc", [128, D], F32)
        ba_t = sb("ba_t", [128, 4], F32)
        nba_t = sb("nba_t", [128, 4], F32)
        wa2_bf = sb("wa2_bf", [128, 512], BF16)
        ggla_t = sb("ggla_t", [128, 2], F32)
        gsub_t = sb("gsub_t", [128, 1], F32)
        gsub08 = sb("gsub08", [128, 1], F32)
        lam_t = sb("lam_t", [128, 256], F32)
        lam_p = sb("lam_p", [128, 128], F32)
        lam_s = sb("lam_s", [128, 2], F32)
        lam_e = sb("lam_e", [128, 2], F32)
        neg_lam = sb("neg_lam", [128, 1], F32)
        rope_t = sb("rope_t", [128, 33, 32], F32)
        junk = sb("junk", [128, D], BF16)
        junkb = Buf()
        xn_tok = [sb("xn_tok%d" % i, [128, D], BF16) for i in range(2)]
        xnb = [Buf() for _ in range(2)]
        nstat2 = sb("nstat", [128, 8], F32)
        nstb2 = [Buf(), Buf()]
        mark_const = aoff[0]
        xnT = sb("xnT", [128, 8, TT], BF16)
        CB = Buf()
        XNT = Buf()

        d_c = dsem()
        cons = [
            ('sp', g_bc[:], g_pre.partition_broadcast(128)),
            ('sp', gpost_bc[:], g_post.partition_broadcast(128)),
            ('sp', ba_t[:], ba),
            ('sp', ggla_t[:], g_gla),
            ('sp', gsub_t[:], g_sub),
            ('sp', lam_t[:], lam4.partition_broadcast(128)),
            ('sp', rope_t[:, 0:32, :], rope[0:T, :].rearrange("(n p) c -> p n c", p=128)),
            ('sp', rope_t[0:TS, 32, :], rope[T:TT, :]),
            ('pool', wa2_bf[0:16, :], wa2),
        ]
        d_c2 = dsem()
        for eng, o, i in cons:
            S.dma(eng, lambda e, o=o, i=i: e.dma_start(out=o, in_=i), d_c if eng == 'sp' else d_c2, writes=[CB])
        P = lambda fn, **kw: S.op('pool', fn, **kw)
        P(lambda e: e.memset(ones_bf[:], 1.0), writes=[CB])
        P(lambda e: e.memset(ones_f[:], 1.0), writes=[CB])
        P(lambda e: e.affine_select(out=ident[:], in_=ones_bf[:], pattern=[[-1, 128]], compare_op=ALU.is_equal,
                                    fill=0.0, base=0, channel_multiplier=1), reads=[CB], writes=[CB])
        P(lambda e: e.affine_select(out=identf[:], in_=ones_f[:], pattern=[[-1, 128]], compare_op=ALU.is_equal,
                                    fill=0.0, base=0, channel_multiplier=1), reads=[CB], writes=[CB])
        P(lambda e: e.affine_select(out=trimask[:], in_=ones_bf[:], pattern=[[1, 128]], compare_op=ALU.is_ge,
                                    fill=0.0, base=0, channel_multiplier=-1), reads=[CB], writes=[CB])
        P(lambda e: e.memset(dmask[:], 1.0), writes=[CB])
        P(lambda e: e.memset(dmask[64:128, 0, :, 0:64], 0.0), reads=[CB], writes=[CB])
        P(lambda e: e.memset(dmask[0:64, 1, :, 0:128], 0.0), reads=[CB], writes=[CB])
        P(lambda e: e.memset(dmask[64:128, 1, :, 0:192], 0.0), reads=[CB], writes=[CB])
        V = lambda fn, **kw: S.op('dve', fn, **kw)
        A = lambda fn, **kw: S.op('act', fn, **kw)
        V(lambda e: e.tensor_scalar(out=nba_t[:], in0=ba_t[:], scalar1=-1.0, scalar2=None, op0=ALU.mult), reads=[CB], writes=[CB])
        V(lambda e: e.tensor_scalar(out=gsub08[:], in0=gsub_t[:], scalar1=0.8, scalar2=None, op0=ALU.mult), reads=[CB], writes=[CB])
        lam_v = lam_t[:].rearrange("p (a b d) -> p a b d", a=2, b=2)
        V(lambda e: e.tensor_tensor(out=lam_p[:].rearrange("p (a d) -> p a d", a=2), in0=lam_v[:, :, 0, :], in1=lam_v[:, :, 1, :],
                                    op=ALU.mult), reads=[CB], writes=[CB])
        V(lambda e: e.tensor_reduce(out=lam_s[:], in_=lam_p[:].rearrange("p (a d) -> p a d", a=2), axis=AX.X, op=ALU.add),
          reads=[CB], writes=[CB])
        A(lambda e: e.activation(out=lam_e[:], in_=lam_s[:], func=AF.Exp), reads=[CB], writes=[CB])
        V(lambda e: e.scalar_tensor_tensor(out=neg_lam[:], in0=lam_e[:, 1:2], scalar=-0.2, in1=lam_e[:, 0:1],
                                           op0=ALU.add, op1=ALU.subtract), reads=[CB], writes=[CB])
        S.barrier()

        def mm(out, lhsT, rhs, start, stop, reads, writes, last):
            return S.op('pe', lambda e: e.matmul(out, lhsT=lhsT, rhs=rhs, start=start, stop=stop),
                        reads=reads, writes=writes, signal=last)

        def tr(out, in_, idn, reads, writes, last=True):
            return S.op('pe', lambda e: e.transpose(out=out, in_=in_, identity=idn), reads=reads, writes=writes, signal=last)

        def act(out, in_, func, reads, writes, **kw):
            return S.op('act', lambda e: e.activation(out=out, in_=in_, func=func, **kw), reads=reads, writes=writes)

        def tt(eng, out, in0, in1, op, reads, writes):
            return S.op(eng, lambda e: e.tensor_tensor(out=out, in0=in0, in1=in1, op=op), reads=reads, writes=writes)

        def ts(eng, out, in0, s1, op0, reads, writes, s2=None, op1=None):
            if op1 is None:
                return S.op(eng, lambda e: e.tensor_scalar(out=out, in0=in0, scalar1=s1, scalar2=None, op0=op0),
                            reads=reads, writes=writes)
            return S.op(eng, lambda e: e.tensor_scalar(out=out, in0=in0, scalar1=s1, scalar2=s2, op0=op0, op1=op1),
                        reads=reads, writes=writes)

        def stt(eng, out, in0, scalar, in1, op0, op1, reads, writes):
            return S.op(eng, lambda e: e.scalar_tensor_tensor(out=out, in0=in0, scalar=scalar, in1=in1, op0=op0, op1=op1),
                        reads=reads, writes=writes)

        def cp(eng, out, in_, reads, writes):
            if eng == 'act':
                return S.op('act', lambda e: e.copy(out=out, in_=in_), reads=reads, writes=writes)
            return S.op(eng, lambda e: e.tensor_copy(out=out, in_=in_), reads=reads, writes=writes)

        NW = 3
        wslot = [sb("wslot%d" % i, [128, 8, 384], BF16) for i in range(NW)]
        wbuf = [Buf() for _ in range(NW)]
        wsem = [dsem() for _ in range(NW)]
        wctr = [0]

        def load_w(cols):
            i = wctr[0] % NW
            wctr[0] += 1
            off = 0
            for c0, n in cols:
                S.dma('pool', lambda e, o=wslot[i][:, :, off:off + n], s=wl[:, :, c0:c0 + n]: e.dma_start(out=o, in_=s),
                      wsem[i], writes=[wbuf[i]])
                off += n
            return wslot[i], wbuf[i]

        prot = [0]

        def next_bank(lo, hi):
            b = lo + prot[0] % (hi - lo)
            prot[0] += 1
            return b

        mark_mid = aoff[0]
        xin = [sb("xin%d" % i, [128, D], F32) for i in range(2)]
        xinb = [Buf() for _ in range(2)]
        xsem = [dsem() for _ in range(2)]
        def norm_a(i, n, xtile, xbuf):
            k = i % 2
            nstat, nstb = nstat2[:, 4 * k:4 * k + 4], nstb2[k]
            act(junk[0:n, :], xtile[0:n, :], AF.Square, [xbuf], [junkb, nstb], accum_out=nstat[0:n, 0:1])
            act(nstat[0:n, 1:2], nstat[0:n, 0:1], AF.Ln, [nstb], [nstb], scale=1.0 / D, bias=EPS)
            act(nstat[0:n, 2:3], nstat[0:n, 1:2], AF.Exp, [nstb], [nstb], scale=-0.5)

        def norm_b(i, n, xtile, xbuf):
            k = i % 2
            nstat, nstb = nstat2[:, 4 * k:4 * k + 4], nstb2[k]
            stt('dve', xn_tok[k][0:n, :], xtile[0:n, :], nstat[0:n, 2:3], g_bc[0:n, :], ALU.mult, ALU.mult,
                [xbuf, nstb], [xnb[k]])

        def norm_c(i, n, dstT, dstbuf, dcol):
            k = i % 2
            bk = next_bank(0, 2)
            pT = banks[bk][:].bitcast(BF16).rearrange("p (k t) -> p k t", k=8)
            for kc in range(8):
                tr(pT[:, kc, 0:n], xn_tok[k][0:n, kc * 128:(kc + 1) * 128], ident[0:n, 0:n], [xnb[k]], [bankbuf[bk][0]],
                   last=(kc == 7))
            cp('dve', dstT[:, :, dcol:dcol + n], pT[:, :, 0:n], [bankbuf[bk][0]], [dstbuf])

        def norm_tile(i, t0, n, xtile, xbuf, dstT, dstbuf, dcol):
            norm_a(i, n, xtile, xbuf)
            norm_b(i, n, xtile, xbuf)
            norm_c(i, n, dstT, dstbuf, dcol)

        NTL0 = len(TILES)
        for st_ in range(NTL0 + 2):
            if st_ < NTL0:
                t0, n = TILES[st_]
                k = st_ % 2
                S.dma('sp', lambda e, o=xin[k][0:n, :], s=xrows(t0, n): e.dma_start(out=o, in_=s), xsem[k], writes=[xinb[k]])
                norm_a(st_, n, xin[k], xinb[k])
            if 0 <= st_ - 1 < NTL0:
                t0, n = TILES[st_ - 1]
                norm_b(st_ - 1, n, xin[(st_ - 1) % 2], xinb[(st_ - 1) % 2])
            if 0 <= st_ - 2 < NTL0:
                t0, n = TILES[st_ - 2]
                norm_c(st_ - 2, n, xnT, XNT, t0)
        S.barrier()
        if stop is not None:
            dd = dsem()
            S.dma('sp', lambda e: e.dma_start(out=dbg_xnT, in_=xnT), dd, reads=[XNT])
            S.barrier()

        aoff[0] = mark_mid
        with ExitStack() as gs:
          if stop is None or stop >= 2:
            grT = sb("grT", [128, TT], BF16, gs)
            GRT = Buf()
            qdT = sb("qdT", [128, TT], BF16, gs)
            kdT = sb("kdT", [128, TT], BF16, gs)
            vtok = sb("vtok", [128, 33, 256], BF16, gs)
            gsil = sb("gsil", [128, 2, TT], BF16, gs)
            ncl = sb("ncl", [128, 33], F32, gs)
            ebl = sb("ebl", [128, 33], F32, gs)
            HB = Buf()
            NT = 2
            tmpA = [sb("gtA%d" % i, [128, 512], F32, gs) for i in range(NT)]
            tmpB = [sb("gtB%d" % i, [128, 512], F32, gs) for i in range(NT)]
            tmpC = [sb("gtC%d" % i, [128, 512], F32, gs) for i in range(NT)]
            tmpD = [sb("gtD%d" % i, [128, 512], F32, gs) for i in range(NT)]
            tAb = [Buf() for _ in range(NT)]
            tBb = [Buf() for _ in range(NT)]
            tCb = [Buf() for _ in range(NT)]
            tDb = [Buf() for _ in range(NT)]
            Sst = [sb("Sst%d" % i, [128, 256], F32, gs) for i in range(2)]
            Sbf = [sb("Sbf%d" % i, [128, 256], BF16, gs) for i in range(2)]
            Sb = [Buf() for _ in range(2)]
            Sbb = [Buf() for _ in range(2)]
            Am = [sb("Am%d" % i, [128, 128], BF16, gs) for i in range(2)]
            Amb = [Buf() for _ in range(2)]
            kd2 = [sb("kd2_%d" % i, [128, 128], BF16, gs) for i in range(2)]
            kd2b = [Buf() for _ in range(2)]
            Uc = [sb("Uc%d" % i, [128, 256], F32, gs) for i in range(2)]
            Ucb = [Buf() for _ in range(2)]
            sqg = [sb("sqg%d" % i, [128, 2, 512], BF16, gs) for i in range(2)]
            sqgb = [Buf() for _ in range(2)]
            rsg = [sb("rsg%d" % i, [128, 512], F32, gs) for i in range(2)]
            rsgb = [Buf() for _ in range(2)]
            ygo = [sb("ygo%d" % i, [128, 2, 512], BF16, gs) for i in range(2)]
            ygob = [Buf() for _ in range(2)]
            ygsem = [dsem() for _ in range(2)]
            ssem = dsem()
            sosem = [dsem() for _ in range(2)]

            wt, wb_ = load_w([(C_GR, 16)])
            for (t0, n) in BLOCKS:
                bk = next_bank(0, 4)
                for kc in range(8):
                    mm(banks[bk][0:16, 0:n], wt[:, kc, 0:16], xnT[:, kc, t0:t0 + n], kc == 0, kc == 7,
                       [wb_, XNT], [bankbuf[bk][0]], kc == 7)
                cp('act', grT[0:16, t0:t0 + n], banks[bk][0:16, 0:n], [bankbuf[bk][0]], [GRT])

            for h in range(4):
                if h == 0:
                    gpre = (load_w([(C_GQ + h * 128, 128), (C_GK + h * 128, 128)]), load_w([(C_GG + h * 256, 256)]),
                            load_w([(C_GV + h * 256, 256)]))
                (wq, wqb), (wg_, wgb), (wv, wvb) = gpre
                for bi, (t0, n) in enumerate(BLOCKS):
                    k = bi % NT
                    bk = next_bank(0, 4)
                    mm(banks[bk][:, 0:n], wa2_bf[0:16, h * 128:(h + 1) * 128], grT[0:16, t0:t0 + n], True, True,
                       [GRT, CB], [bankbuf[bk][0]], True)
                    act(tmpA[k][:, 0:n], banks[bk][:, 0:n], AF.Exp, [bankbuf[bk][0]], [tAb[k]], scale=-1.0, bias=nba_t[:, h:h + 1])
                    act(tmpA[k][:, 0:n], tmpA[k][:, 0:n], AF.Ln, [tAb[k]], [tAb[k]], bias=1.0)
                    for c0 in range(0, n, 128):
                        cn = min(128, n - c0)
                        ci = (t0 + c0) // 128
                        S.op('dve', lambda e, o=tmpB[k][:, c0:c0 + cn], d1=tmpA[k][:, c0:c0 + cn], d0=ones_bf[:, 0:cn]:
                             e.tensor_tensor_scan(out=o, data0=d0, data1=d1, initial=0.0, op0=ALU.mult, op1=ALU.add),
                             reads=[tAb[k]], writes=[tBb[k]])
                        ts('dve', ncl[:, ci:ci + 1], tmpB[k][:, c0 + cn - 1:c0 + cn], -1.0 / 16, ALU.mult, [tBb[k]], [HB])
                    act(tmpC[k][:, 0:n], tmpB[k][:, 0:n], AF.Exp, [tBb[k]], [tCb[k]], scale=-1.0 / 16)
                    act(tmpD[k][:, 0:n], tmpB[k][:, 0:n], AF.Exp, [tBb[k]], [tDb[k]], scale=1.0 / 16)
                    bk = next_bank(0, 4)
                    for kc in range(8):
                        mm(banks[bk][:, 0:n], wq[:, kc, 0:128], xnT[:, kc, t0:t0 + n], kc == 0, kc == 7, [wqb, XNT],
                           [bankbuf[bk][0]], kc == 7)
                    stt('dve', qdT[:, t0:t0 + n], banks[bk][:, 0:n], 128 ** -0.5, tmpC[k][:, 0:n], ALU.mult, ALU.mult,
                        [bankbuf[bk][0], tCb[k]], [HB])
                    bk = next_bank(0, 4)
                    for kc in range(8):
                        mm(banks[bk][:, 0:n], wq[:, kc, 128:256], xnT[:, kc, t0:t0 + n], kc == 0, kc == 7, [wqb, XNT],
                           [bankbuf[bk][0]], kc == 7)
                    tt('dve', kdT[:, t0:t0 + n], banks[bk][:, 0:n], tmpD[k][:, 0:n], ALU.mult, [bankbuf[bk][0], tDb[k]], [HB])
                for bi, (t0, n) in enumerate(BLOCKS):
                    k = bi % NT
                    for et in range(2):
                        bk = next_bank(0, 4)
                        for kc in range(8):
                            mm(banks[bk][:, 0:n], wg_[:, kc, et * 128:(et + 1) * 128], xnT[:, kc, t0:t0 + n], kc == 0, kc == 7,
                               [wgb, XNT], [bankbuf[bk][0]], kc == 7)
                        tgt = tmpC if et == 0 else tmpD
                        tgb = tCb if et == 0 else tDb
                        act(tgt[k][:, 0:n], banks[bk][:, 0:n], AF.Silu, [bankbuf[bk][0]], [tgb[k]])
                        ts('dve', gsil[:, et, t0:t0 + n], tgt[k][:, 0:n], ggla_t[:, et:et + 1], ALU.mult, [tgb[k]], [HB])
                for i, (t0, n) in enumerate(TILES):
                    bk = next_bank(0, 4)
                    for kc in range(8):
                        mm(banks[bk][0:n, 0:256], xnT[:, kc, t0:t0 + n], wv[:, kc, 0:256], kc == 0, kc == 7, [wvb, XNT],
                           [bankbuf[bk][0]], kc == 7)
                    cp('act', vtok[0:n, i, :], banks[bk][0:n, 0:256], [bankbuf[bk][0]], [HB])
                act(ebl[:, :], ncl[:, :], AF.Exp, [HB], [HB])
                if h + 1 < 4:
                    h1 = h + 1
                    gpre = (load_w([(C_GQ + h1 * 128, 128), (C_GK + h1 * 128, 128)]), load_w([(C_GG + h1 * 256, 256)]),
                            load_w([(C_GV + h1 * 256, 256)]))

                def pre(ci, t0, n):
                    a = ci % 2
                    mm(banks[2][0:n, 0:n], kdT[:, t0:t0 + n], qdT[:, t0:t0 + n], True, True, [HB],
                       [bankbuf[2][0]], True)
                    tt('dve', Am[a][0:n, 0:n], banks[2][0:n, 0:n], trimask[0:n, 0:n], ALU.mult,
                       [bankbuf[2][0], CB], [Amb[a]])
                    pT = banks[4][:].bitcast(BF16)
                    tr(pT[0:n, 0:128], kdT[:, t0:t0 + n], ident[:, :], [HB], [bankbuf[4][0]])
                    cp('act', kd2[a][0:n, :], pT[0:n, 0:128], [bankbuf[4][0]], [kd2b[a]])
                    mm(banks[6][:, 0:256], kd2[a][0:n, :], vtok[0:n, ci, :], True, True, [kd2b[a], HB],
                       [bankbuf[6][0]], True)
                    act(Uc[a][:, :], banks[6][:, 0:256], AF.Copy, [bankbuf[6][0], HB], [Ucb[a]], scale=ebl[:, ci:ci + 1])

                def dep(ci, t0, n, si, pO, pOb, col):
                    a = ci % 2
                    for et in range(2):
                        mm(pO[et][:, col:col + n], Sbf[si][:, et * 128:(et + 1) * 128], qdT[:, t0:t0 + n], True, False,
                           [Sbb[si], HB], [pOb[et]], False)
                        mm(pO[et][:, col:col + n], vtok[0:n, ci, et * 128:(et + 1) * 128], Am[a][0:n, 0:n], False, True,
                           [Amb[a], HB], [pOb[et]], True)
                    stt('dve', Sst[1 - si][:, :], Sst[si][:, :], ebl[:, ci:ci + 1], Uc[a][:, :],
                        ALU.mult, ALU.add, [Sb[si], HB, Ucb[a]], [Sb[1 - si]])
                    cp('act', Sbf[1 - si][:, :], Sst[1 - si][:, :], [Sb[1 - si]], [Sbb[1 - si]])

                def finish_block(bi, t0, n, pO, pOb):
                    k = bi % 2
                    for et in range(2):
                        act(sqg[k][:, et, 0:n], pO[et][:, 0:n], AF.Square, [pOb[et]], [sqgb[k]])
                    for et in range(2):
                        mm(banks[7][:, 0:n], ones_bf[:, :], sqg[k][:, et, 0:n], et == 0, et == 1, [sqgb[k], CB],
                           [bankbuf[7][0]], et == 1)
                    act(rsg[k][:, 0:n], banks[7][:, 0:n], AF.Ln, [bankbuf[7][0]], [rsgb[k]], scale=1.0 / 256, bias=EPS)
                    act(rsg[k][:, 0:n], rsg[k][:, 0:n], AF.Exp, [rsgb[k]], [rsgb[k]], scale=-0.5)
                    for et in range(2):
                        tt('dve', tmpB[et][:, 0:n], pO[et][:, 0:n], rsg[k][:, 0:n], ALU.mult, [pOb[et], rsgb[k]], [tBb[et]])
                        tt('dve', ygo[k][:, et, 0:n], tmpB[et][:, 0:n], gsil[:, et, t0:t0 + n], ALU.mult, [tBb[et], HB], [ygob[k]])
                    dst = ygT_d[h * 256:(h + 1) * 256, t0:t0 + n].rearrange("(a p) t -> p a t", p=128)
                    S.dma('sp', lambda e, o=dst, s=ygo[k][:, :, 0:n]: e.dma_start(out=o, in_=s), ygsem[k], reads=[ygob[k]])

                S.op('pool', lambda e: e.memset(Sst[0][:, :], 0.0), writes=[Sb[0]])
                S.op('pool', lambda e: e.memset(Sbf[0][:, :], 0.0), writes=[Sbb[0]])
                si = 0
                pOs = ([banks[0], banks[1]], [banks[3], banks[5]])
                pObs = ([bankbuf[0][0], bankbuf[1][0]], [bankbuf[3][0], bankbuf[5][0]])
                chunks = [(ci, ci * 128, 128) for ci in range(32)] + [(32, T, TS)]
                pre(*chunks[0])
                for idx, (ci, t0, n) in enumerate(chunks):
                    pO, pOb = pOs[(ci // 4) % 2], pObs[(ci // 4) % 2]
                    if idx + 1 < len(chunks):
                        pre(*chunks[idx + 1])
                    if ci == 32:
                        S.dma('sp', lambda e, o=ngp[h], s=Sst[si][:, :]: e.dma_start(out=o, in_=s), sosem[0], reads=[Sb[si]])
                        si = 1 - si
                        S.dma('sp', lambda e, o=Sst[si][:, :], s=sg[h]: e.dma_start(out=o, in_=s), ssem, writes=[Sb[si]])
                        cp('act', Sbf[si][:, :], Sst[si][:, :], [Sb[si]], [Sbb[si]])
                        dep(ci, t0, n, si, pO, pOb, 0)
                        si = 1 - si
                        finish_block(8, T, TS, pO, pOb)
                        S.dma('sp', lambda e, o=ngs[h], s=Sst[si][:, :]: e.dma_start(out=o, in_=s), sosem[1], reads=[Sb[si]])
                    else:
                        dep(ci, t0, n, si, pO, pOb, (ci % 4) * 128)
                        si = 1 - si
                        if ci % 4 == 3:
                            finish_block(ci // 4, (ci // 4) * 512, 512, pO, pOb)
            S.barrier()

        aoff[0] = mark_mid
        with ExitStack() as ds:
          if stop is None or stop >= 3:
            QT = sb("QT", [128, TT], BF16, ds)
            KT = sb("KT", [128, TT + PAST], BF16, ds)
            Vall = sb("Vall", [128, 49, 128], BF16, ds)
            kcb = sb("kcb", [128, 16, 128], BF16, ds)
            dgs = sb("dgs", [128, TT], BF16, ds)
            HB = Buf()
            KCB = Buf()
            kcsem = dsem()
            vcsem = dsem()
            stg = [sb("stg%d" % i, [128, 384], F32, ds) for i in range(4)]
            stgb = [Buf() for _ in range(4)]
            stgs = [dsem() for _ in range(4)]
            qkc = [sb("qkc%d" % i, [128, 128], BF16, ds) for i in range(3)]
            qkcb = [Buf() for _ in range(3)]
            rtA = [sb("rtA%d" % i, [128, 4, 16], F32, ds) for i in range(2)]
            rtB = [sb("rtB%d" % i, [128, 4, 16], F32, ds) for i in range(2)]
            rtb = [Buf() for _ in range(2)]
            qkbf = [sb("qkbf%d" % i, [128, 256], BF16, ds) for i in range(3)]
            qkbfb = [Buf() for _ in range(3)]
            sqn = [sb("sqn%d" % i, [128, 256], F32, ds) for i in range(2)]
            red = [sb("red%d" % i, [128, 4], F32, ds) for i in range(2)]
            sqnb = [Buf() for _ in range(2)]
            nmax = sb("nmax", [128, 4], F32, ds)
            nmb = Buf()
            bnd = sb("bnd", [128, 4], F32, ds)
            negc = sb("negc", [128, 1], F32, ds)
            bndb = Buf()
            tS = [sb("atS%d" % i, [128, 512], F32, ds) for i in range(2)]
            tSb = [Buf() for _ in range(2)]
            racc = [sb("racc%d" % i, [128, 2, 2, 256], F32, ds) for i in range(2)]
            raccb = [Buf() for _ in range(2)]
            rbf = [sb("rbf%d" % i, [128, 2, 256], BF16, ds) for i in range(2)]
            rbfb = [Buf() for _ in range(2)]
            NP = 3
            PT = [sb("PT%d" % i, [128, 2, 2, 256], BF16, ds) for i in range(NP)]
            PTb = [Buf() for _ in range(NP)]
            rs = [sb("ars%d" % i, [128, 512], F32, ds) for i in range(2)]
            o1 = [sb("ao1%d" % i, [128, 2, 256], F32, ds) for i in range(2)]
            od = [sb("aod%d" % i, [128, 256], F32, ds) for i in range(2)]
            sqd = [sb("asq%d" % i, [128, 256], BF16, ds) for i in range(2)]
            rsd = [sb("arsd%d" % i, [128, 256], F32, ds) for i in range(2)]
            ydo = [sb("ydo%d" % i, [128, 256], BF16, ds) for i in range(2)]
            nb_ = [Buf() for _ in range(2)]
            ydob = [Buf() for _ in range(2)]
            ydsem = [dsem() for _ in range(2)]
            pctr = [0]
            actr = [0]

            PE_ = 'dve'
            RE_ = 'dve' if 'dverope' in DBG['skip'] else 'pool'
            for h in range(DBG['nh']):
                def issue_loads(hh):
                    a = load_w([(C_DQ + hh * 128, 128), (C_DK + hh * 128, 128), (C_DV + hh * 128, 128)])
                    b = load_w([(C_DG + hh * 128, 128)])
                    for q4 in range(4):
                        rs_ = slice(q4 * 512, (q4 + 1) * 512)
                        S.dma('pool', lambda e, o=kcb[:, q4 * 4:(q4 + 1) * 4, :], s=ck[rs_, hh * 128:(hh + 1) * 128].rearrange("(n p) c -> p n c", p=128):
                              e.dma_start(out=o, in_=s), kcsem, writes=[KCB])
                    return a, b

                if h == 0:
                    apre = issue_loads(0)
                (wqkv, wqkvb), (wdg, wdgb) = apre
                for q4 in range(4):
                    rs_ = slice(q4 * 512, (q4 + 1) * 512)
                    S.dma('pool', lambda e, o=Vall[:, 33 + q4 * 4:33 + (q4 + 1) * 4, :], s=cv[rs_, h * 128:(h + 1) * 128].rearrange("(n p) c -> p n c", p=128):
                          e.dma_start(out=o, in_=s), vcsem, writes=[HB])
                S.op('pool', lambda e: e.memset(nmax[:, :], 0.0), writes=[nmb])

                def norms(src, n, k, srcb):
                    G = src.shape[1] // 64
                    tt('dve', sqn[k][0:n, 0:G * 64], src, src, ALU.mult, [srcb], [sqnb[k]])
                    S.op('dve', lambda e: e.tensor_reduce(out=red[k][0:n, 0:G], in_=sqn[k][0:n, 0:G * 64].rearrange("p (g d) -> p g d", g=G),
                                                          axis=AX.X, op=ALU.add), reads=[sqnb[k]], writes=[sqnb[k]])
                    tt('dve', nmax[0:n, 0:G], nmax[0:n, 0:G], red[k][0:n, 0:G], ALU.max, [sqnb[k], nmb], [nmb])

                def stA(i):
                    t0, n = TILES[i]
                    k4 = i % 4
                    bk = i % 2
                    for kc in range(8):
                        mm(banks[bk][0:n, 0:384], xnT[:, kc, t0:t0 + n], wqkv[:, kc, 0:384], kc == 0, kc == 7, [wqkvb, XNT],
                           [bankbuf[bk][0]], kc == 7)
                    cp('act', stg[k4][0:n, :], banks[bk][0:n, 0:384], [bankbuf[bk][0]], [stgb[k4]])

                def stB(i):
                    t0, n = TILES[i]
                    k4 = i % 4
                    k = i % 2
                    k3 = i % 3
                    cp('act', Vall[0:n, i, :], stg[k4][0:n, 256:384], [stgb[k4]], [HB])
                    pq = stg[k4][0:n, 0:256].rearrange("p (g d) -> p g d", g=4)
                    c16 = rope_t[0:n, i, 0:16].unsqueeze(1).to_broadcast([n, 4, 16])
                    sa8 = rope_t[0:n, i, 16:24].unsqueeze(1).to_broadcast([n, 4, 8])
                    sb8 = rope_t[0:n, i, 24:32].unsqueeze(1).to_broadcast([n, 4, 8])
                    S.op(RE_, lambda e, o=rtA[k][0:n], a=pq[:, :, 0:16], b=c16: e.tensor_tensor(out=o, in0=a, in1=b, op=ALU.mult),
                         reads=[stgb[k4], CB], writes=[rtb[k]], signal=False)
                    S.op(RE_, lambda e, o=rtB[k][0:n, :, 0:8], a=pq[:, :, 8:16], b=sa8: e.tensor_tensor(out=o, in0=a, in1=b, op=ALU.mult),
                         reads=[stgb[k4], CB], writes=[rtb[k]], signal=False)
                    tt(RE_, rtB[k][0:n, :, 8:16], pq[:, :, 0:8], sb8, ALU.mult, [stgb[k4], CB], [rtb[k]])
                    tt(RE_, pq[:, :, 0:16], rtA[k][0:n], rtB[k][0:n], ALU.add, [rtb[k]], [stgb[k4]])
                    vdst = nvp[t0:t0 + n, h * 128:(h + 1) * 128] if t0 < T else nvs[0:n, h * 128:(h + 1) * 128]
                    kdst = nkp[t0:t0 + n, h * 128:(h + 1) * 128] if t0 < T else nks[0:n, h * 128:(h + 1) * 128]
                    S.dma('sp', lambda e, o=vdst, s=stg[k4][0:n, 256:384]: e.dma_start(out=o, in_=s), stgs[k4], reads=[stgb[k4]])
                    S.dma('sp', lambda e, o=kdst, s=stg[k4][0:n, 128:256]: e.dma_start(out=o, in_=s), stgs[k4], reads=[stgb[k4]])
                    cp('act', qkbf[k3][0:n, :], stg[k4][0:n, 0:256], [stgb[k4]], [qkbfb[k3]])
                    norms(qkbf[k3][0:n, :], n, k, qkbfb[k3])

                def stC(i):
                    t0, n = TILES[i]
                    k = i % 2
                    k3 = i % 3
                    pT = banks[2 + k].bitcast(BF16)
                    tr(pT[:, 0:n], qkbf[k3][0:n, 0:128], ident[0:n, 0:n], [qkbfb[k3]], [bankbuf[2 + k][0]], last=False)
                    tr(pT[:, 128:128 + n], qkbf[k3][0:n, 128:256], ident[0:n, 0:n], [qkbfb[k3]], [bankbuf[2 + k][0]])
                    cp('act', QT[:, t0:t0 + n], pT[:, 0:n], [bankbuf[2 + k][0]], [HB])
                    cp('act', KT[:, t0:t0 + n], pT[:, 128:128 + n], [bankbuf[2 + k][0]], [HB])

                def dgblk(bi):
                    t0, n = BLOCKS[bi]
                    k = bi % 2
                    for kc in range(8):
                        mm(banks[7][:, 0:n], wdg[:, kc, 0:128], xnT[:, kc, t0:t0 + n], kc == 0, kc == 7, [wdgb, XNT],
                           [bankbuf[7][0]], kc == 7)
                    act(tS[k][:, 0:n], banks[7][:, 0:n], AF.Silu, [bankbuf[7][0]], [tSb[k]])
                    ts('dve', dgs[:, t0:t0 + n], tS[k][:, 0:n], gsub08[:, 0:1], ALU.mult, [tSb[k], CB], [HB])

                NTL = len(TILES)
                ndg = 0
                for st_ in range(NTL + 2):
                    if st_ < NTL:
                        stA(st_)
                    if 0 <= st_ - 1 < NTL:
                        stB(st_ - 1)
                    if 0 <= st_ - 2 < NTL:
                        stC(st_ - 2)
                    if st_ % 4 == 3 and ndg < len(BLOCKS):
                        dgblk(ndg)
                        ndg += 1
                while ndg < len(BLOCKS):
                    dgblk(ndg)
                    ndg += 1

                def caA(j):
                    k3 = j % 3
                    cp('dve', qkc[k3][:, :], kcb[:, j, :], [KCB], [qkcb[k3]])
                    norms(qkc[k3][:, :], 128, j % 2, qkcb[k3])

                def caC(j):
                    k = j % 2
                    k3 = j % 3
                    pT = banks[2 + k].bitcast(BF16)
                    tr(pT[:, 0:128], qkc[k3][:, :], ident[:, :], [qkcb[k3]], [bankbuf[2 + k][0]])
                    cp('act', KT[:, TT + j * 128:TT + (j + 1) * 128], pT[:, 0:128], [bankbuf[2 + k][0]], [HB])

                for st_ in range(17):
                    if st_ < 16:
                        caA(st_)
                    if st_ >= 1:
                        caC(st_ - 1)
                if DBG['stage'] < 1:
                    continue
                S.op('dve', lambda e: e.tensor_reduce(out=bnd[:, 0:1], in_=nmax[:, :], axis=AX.X, op=ALU.max), reads=[nmb], writes=[bndb])
                S.op('pe', lambda e: e.transpose(out=banks[7][0:1, 0:128], in_=bnd[:, 0:1], identity=identf[:, :]),
                     reads=[bndb, CB], writes=[bankbuf[7][0]])
                S.op('dve', lambda e: e.tensor_reduce(out=bnd[0:1, 1:2], in_=banks[7][0:1, 0:128], axis=AX.X, op=ALU.max),
                     reads=[bankbuf[7][0]], writes=[bndb])
                S.op('pe', lambda e: e.matmul(banks[7][:, 256:257], lhsT=ones_f[0:1, :], rhs=bnd[0:1, 1:2], start=True, stop=True),
                     reads=[bndb, CB], writes=[bankbuf[7][1]])
                ts('dve', negc[:, 0:1], banks[7][:, 256:257], -0.125, ALU.mult, [bankbuf[7][1]], [bndb])

                if h + 1 < DBG['nh']:
                    apre = issue_loads(h + 1)
                jobs = []

                def add_block(q0, nq, ktiles):
                    w = actr[0] % 2
                    actr[0] += 1
                    pairs = [ktiles[a:a + 2] for a in range(0, len(ktiles), 2)]
                    for g, pr in enumerate(pairs):
                        jobs.append(dict(w=w, q0=q0, nq=nq, pr=pr, g=g, np=len(pairs)))

                if DBG['stage'] >= 2:
                    for qb in (DBG['qbl'] if DBG.get('qbl') else range(DBG['qb0'], DBG['nqb'])):
                        kts = [(j * 128, 128, j, None) for j in range(2 * qb)]
                        kts += [((2 * qb) * 128, 128, 2 * qb, 0), ((2 * qb + 1) * 128, 128, 2 * qb + 1, 1)]
                        add_block(qb * 256, 256, kts)
                if DBG['stage'] >= 3:
                    add_block(T, TS, [(TT + j * 128, 128, 33 + j, None) for j in range(16)] + [(T, TS, 32, None)])

                def qk(ji):
                    jb = jobs[ji]
                    b0 = 2 + 2 * (ji % 2)
                    pr, q0, nq = jb['pr'], jb['q0'], jb['nq']
                    for t, (kc0, nk, vt, r) in enumerate(pr):
                        for m in range(2):
                            mm(banks[b0 + m][0:nk, t * 256:t * 256 + nq], KT[m * 64:(m + 1) * 64, kc0:kc0 + nk],
                               QT[m * 64:(m + 1) * 64, q0:q0 + nq], True, True, [HB], [bankbuf[b0 + m][0]],
                               t == len(pr) - 1 and m == 1)

                def norm_closures(jb):
                    w, q0, nq = jb['w'], jb['q0'], jb['nq']
                    full = (nq == 256)
                    pvi, psi = (6, 7) if w == 0 else (0, 1)
                    pv, psm = banks[pvi], banks[psi]
                    pvb, psb = bankbuf[pvi][0], bankbuf[psi][0]
                    pv3 = pv[:, 0:512].rearrange("p (m q) -> p m q", m=2)[:, :, 0:nq]
                    ps3 = psm[:, 0:512].rearrange("p (m q) -> p m q", m=2)[:, :, 0:nq]
                    rs3 = rs[w][:, :].rearrange("p (m q) -> p m q", m=2)[:, :, 0:nq]

                    def n1():
                        tt('dve', rbf[w][:, :, 0:nq], racc[w][:, 0, :, 0:nq], racc[w][:, 1, :, 0:nq], ALU.add, [raccb[w]], [rbfb[w]])
                        if full:
                            mm(psm[:, 0:512], ones_bf[:, :], rbf[w][:, :, :].rearrange("p m q -> p (m q)"), True, True,
                               [rbfb[w], CB], [psb], True)
                        else:
                            for m in range(2):
                                mm(psm[:, m * 256:m * 256 + nq], ones_bf[:, :], rbf[w][:, m, 0:nq], m == 0, m == 1, [rbfb[w], CB], [psb], m == 1)

                    def n2():
                        act(rs3, ps3, AF.Ln, [psb], [nb_[w]])
                        act(rs3, rs3, AF.Exp, [nb_[w]], [nb_[w]], scale=-1.0)

                    def n3():
                        tt('dve', o1[w][:, :, 0:nq], pv3, rs3, ALU.mult, [pvb, nb_[w]], [nb_[w]])
                        stt('dve', od[w][:, 0:nq], o1[w][:, 1, 0:nq], neg_lam[:, 0:1], o1[w][:, 0, 0:nq], ALU.mult, ALU.add,
                            [nb_[w], CB], [nb_[w]])

                    def n4():
                        act(sqd[w][:, 0:nq], od[w][:, 0:nq], AF.Square, [nb_[w]], [nb_[w]])
                        mm(psm[:, 0:nq], ones_bf[:, :], sqd[w][:, 0:nq], True, True, [nb_[w], CB], [psb], True)

                    def n5():
                        act(rsd[w][:, 0:nq], psm[:, 0:nq], AF.Ln, [psb], [nb_[w]], scale=1.0 / 128, bias=EPS)
                        act(rsd[w][:, 0:nq], rsd[w][:, 0:nq], AF.Exp, [nb_[w]], [nb_[w]], scale=-0.5)

                    def n6():
                        tt('dve', od[w][:, 0:nq], od[w][:, 0:nq], rsd[w][:, 0:nq], ALU.mult, [nb_[w]], [nb_[w]])
                        tt('dve', ydo[w][:, 0:nq], od[w][:, 0:nq], dgs[:, q0:q0 + nq], ALU.mult, [nb_[w], HB], [ydob[w]])
                        S.dma('sp', lambda e, o=ydT_d[h * 128:(h + 1) * 128, q0:q0 + nq], s=ydo[w][:, 0:nq]: e.dma_start(out=o, in_=s),
                              ydsem[w], reads=[ydob[w]])

                    return [n1, n2, n3, n4, n5, n6]

                deferred = []
                if jobs:
                    qk(0)
                for ji, jb in enumerate(jobs):
                    w, q0, nq, pr, g, npairs = jb['w'], jb['q0'], jb['nq'], jb['pr'], jb['g'], jb['np']
                    full = (nq == 256)
                    pvi = 6 if w == 0 else 0
                    pv, pvb = banks[pvi], bankbuf[pvi][0]
                    nt = len(pr)
                    if ji + 1 < len(jobs):
                        qk(ji + 1)
                    pi = pctr[0] % NP
                    pctr[0] += 1
                    b0 = 2 + 2 * (ji % 2)
                    sbufs = [bankbuf[b0][0], bankbuf[b0 + 1][0], bndb]
                    same_nk = all(p_[1] == pr[0][1] for p_ in pr)
                    if full and same_nk:
                        nk = pr[0][1]
                        src = psum_all[0:nk, b0:b0 + 2, 0:nt * 256].rearrange("p m (t q) -> p m t q", t=nt)
                        dst = PT[pi][0:nk, 0:nt, :, :].rearrange("p t m q -> p m t q")
                        act(dst, src, AF.Exp, sbufs, [PTb[pi]], scale=0.125, bias=negc[0:nk, 0:1])
                    else:
                        for t, (kc0, nk, vt, r) in enumerate(pr):
                            act(PT[pi][0:nk, t, :, 0:nq], psum_all[0:nk, b0:b0 + 2, t * 256:t * 256 + nq], AF.Exp, sbufs,
                                [PTb[pi]], scale=0.125, bias=negc[0:nk, 0:1])
                    if pr[0][3] is not None:
                        for mi_, ap_ in enumerate((PT[pi][64:128, 0, :, 0:64], PT[pi][0:64, 1, :, 0:128], PT[pi][64:128, 1, :, 0:192])):
                            S.op('dve', lambda e, ap_=ap_: e.memset(ap_, 0.0), reads=[], writes=[PTb[pi]], signal=(mi_ == 2))
                    if same_nk:
                        nk = pr[0][1]
                        if g == 0:
                            cp('dve', racc[w][0:nk, 0:nt, :, 0:nq], PT[pi][0:nk, 0:nt, :, 0:nq], [PTb[pi]], [raccb[w]])
                        else:
                            tt('dve', racc[w][0:nk, 0:nt, :, 0:nq], racc[w][0:nk, 0:nt, :, 0:nq], PT[pi][0:nk, 0:nt, :, 0:nq],
                               ALU.add, [PTb[pi], raccb[w]], [raccb[w]])
                    else:
                        for t, (kc0, nk, vt, r) in enumerate(pr):
                            tt('dve', racc[w][0:nk, t, :, 0:nq], racc[w][0:nk, t, :, 0:nq], PT[pi][0:nk, t, :, 0:nq],
                               ALU.add, [PTb[pi], raccb[w]], [raccb[w]])
                    for t, (kc0, nk, vt, r) in enumerate(pr):
                        first = (g == 0 and t == 0)
                        last = (g == npairs - 1 and t == nt - 1)
                        if full:
                            mm(pv[:, 0:512], Vall[0:nk, vt, :], PT[pi][0:nk, t, :, :].rearrange("p m q -> p (m q)"), first, last,
                               [PTb[pi], HB], [pvb], t == nt - 1)
                        else:
                            for m in range(2):
                                mm(pv[:, m * 256:m * 256 + nq], Vall[0:nk, vt, :], PT[pi][0:nk, t, m, 0:nq], first and m == 0,
                                   last and m == 1, [PTb[pi], HB], [pvb], t == nt - 1 and m == 1)
                    if g == npairs - 1:
                        while deferred:
                            deferred.pop(0)()
                        deferred = norm_closures(jb)
                    elif deferred:
                        deferred.pop(0)()
                while deferred:
                    deferred.pop(0)()
            S.barrier()

        aoff[0] = mark_const
        with ExitStack() as ms:
          if stop is None or stop >= 4:
            wgs = sb("wgs", [128, 8, D], BF16, ms)
            wds = sb("wds", [128, 8, D], BF16, ms)
            wos = sb("wos", [128, 8, D], BF16, ms)
            wms = sb("wms", [128, 8, 2 * D], BF16, ms)
            WB = Buf()
            wrs = dsem()
            for q4 in range(4):
                c = slice(q4 * 256, (q4 + 1) * 256)
                for dst, src in ((wgs, wpg), (wds, wpd), (wos, wo)):
                    S.dma('pool', lambda e, o=dst[:, :, c], s=src[:, :, c]: e.dma_start(out=o, in_=s), wrs, writes=[WB])
            for q4 in range(8):
                S.dma('pool', lambda e, o=wms[:, :, q4 * 256:(q4 + 1) * 256], s=wl[:, :, C_MA + q4 * 256:C_MA + (q4 + 1) * 256]:
                      e.dma_start(out=o, in_=s), wrs, writes=[WB])
            xblk = [sb("xblk%d" % i, [128, 4, D], F32, ms) for i in range(2)]
            xblkb = [[Buf() for _ in range(4)] for _ in range(2)]
            xbs = [[dsem() for _ in range(4)] for _ in range(2)]
            xnTb = [sb("xnTb%d" % i, [128, 8, 512], BF16, ms) for i in range(2)]
            xnTbb = [Buf() for _ in range(2)]
            ygb = [sb("ygb%d" % i, [128, 8, 512], BF16, ms) for i in range(1)] * 2
            ydb = [sb("ydb%d" % i, [128, 8, 512], BF16, ms) for i in range(1)] * 2
            ygbb = [Buf()] * 2
            ydbb = [Buf()] * 2
            ygs_ = [dsem()] * 2
            yds_ = [dsem()] * 2
            mT = [sb("mT%d" % i, [128, 8, 512], BF16, ms) for i in range(1)] * 2
            mTb = [Buf()] * 2
            sga = [sb("sga%d" % i, [128, 512], F32, ms) for i in range(2)]
            sgb = [sb("sgb%d" % i, [128, 512], F32, ms) for i in range(2)]
            sgab = [Buf() for _ in range(2)]
            sgbb = [Buf() for _ in range(2)]
            ost = [sb("ost%d" % i, [128, D], F32, ms) for i in range(2)]
            ostb = [Buf() for _ in range(2)]
            osts = [dsem() for _ in range(2)]
            fst2 = sb("fst", [128, 16], F32, ms)
            fstb2 = [Buf(), Buf()]
            tctr = [0]

            def stage_load(bi):
                t0, n = BLOCKS[bi]
                p = bi % 2
                for tl in range((n + 127) // 128):
                    tn = min(128, n - tl * 128)
                    S.dma('sp', lambda e, o=xblk[p][0:tn, tl, :], s=xrows(t0 + tl * 128, tn): e.dma_start(out=o, in_=s),
                          xbs[p][tl], writes=[xblkb[p][tl]])
                    norm_tile(tctr[0], t0 + tl * 128, tn, xblk[p][:, tl, :], xblkb[p][tl], xnTb[p], xnTbb[p], tl * 128)
                    tctr[0] += 1

            stage_load(0)
            for bi, (t0, n) in enumerate(BLOCKS):
                p = bi % 2
                ntl = (n + 127) // 128
                S.dma('sp', lambda e, o=ygb[p][:, :, 0:n], s=ygT_d[:, t0:t0 + n].rearrange("(k p) t -> p k t", p=128):
                      e.dma_start(out=o, in_=s), ygs_[p], writes=[ygbb[p]])
                S.dma('sp', lambda e, o=ydb[p][:, :, 0:n], s=ydT_d[:, t0:t0 + n].rearrange("(k p) t -> p k t", p=128):
                      e.dma_start(out=o, in_=s), yds_[p], writes=[ydbb[p]])
                for ct in range(8):
                    k = ct % 2
                    cs_ = slice(ct * 128, (ct + 1) * 128)
                    for kc in range(8):
                        mm(banks[2][:, 0:n], wms[:, kc, cs_], xnTb[p][:, kc, 0:n], kc == 0, kc == 7, [WB, xnTbb[p]], [bankbuf[2][0]], kc == 7)
                    for kc in range(8):
                        mm(banks[3][:, 0:n], wms[:, kc, D + ct * 128:D + (ct + 1) * 128], xnTb[p][:, kc, 0:n], kc == 0, kc == 7,
                           [WB, xnTbb[p]], [bankbuf[3][0]], kc == 7)
                    for kc in range(8):
                        mm(banks[4][:, 0:n], wgs[:, kc, cs_], ygb[p][:, kc, 0:n], kc == 0, kc == 7, [WB, ygbb[p]], [bankbuf[4][0]], kc == 7)
                    for kc in range(8):
                        mm(banks[5][:, 0:n], wds[:, kc, cs_], ydb[p][:, kc, 0:n], kc == 0, kc == 7, [WB, ydbb[p]], [bankbuf[5][0]], kc == 7)
                    act(sga[k][:, 0:n], banks[2][:, 0:n], AF.Sigmoid, [bankbuf[2][0]], [sgab[k]])
                    act(sgb[k][:, 0:n], banks[3][:, 0:n], AF.Sigmoid, [bankbuf[3][0]], [sgbb[k]])
                    tt('dve', sga[k][:, 0:n], banks[4][:, 0:n], sga[k][:, 0:n], ALU.mult, [bankbuf[4][0], sgab[k]], [sgab[k]])
                    tt('dve', sgb[k][:, 0:n], banks[5][:, 0:n], sgb[k][:, 0:n], ALU.mult, [bankbuf[5][0], sgbb[k]], [sgbb[k]])
                    tt('dve', mT[p][:, ct, 0:n], sga[k][:, 0:n], sgb[k][:, 0:n], ALU.add, [sgab[k], sgbb[k]], [mTb[p]])
                    if ct == 3 and bi + 1 < len(BLOCKS):
                        stage_load(bi + 1)
                for tl in range(ntl):
                    tn = min(128, n - tl * 128)
                    k = tl % 2
                    fb = (6, 7) if tl % 2 == 0 else (0, 1)
                    fst = fst2[:, 8 * k:8 * k + 8]
                    fstb = fstb2[k]
                    for hf in range(2):
                        for ct in range(8):
                            mm(banks[fb[hf]][0:tn, :], mT[p][:, ct, tl * 128:tl * 128 + tn], wos[:, ct, hf * 512:(hf + 1) * 512],
                               ct == 0, ct == 7, [WB, mTb[p]], [bankbuf[fb[hf]][0]], ct == 7)
                    for hf in range(2):
                        act(junk[0:tn, hf * 512:(hf + 1) * 512], banks[fb[hf]][0:tn, :], AF.Square, [bankbuf[fb[hf]][0]], [junkb, fstb],
                            accum_out=fst[0:tn, hf:hf + 1])
                    tt('dve', fst[0:tn, 2:3], fst[0:tn, 0:1], fst[0:tn, 1:2], ALU.add, [fstb], [fstb])
                    act(fst[0:tn, 3:4], fst[0:tn, 2:3], AF.Ln, [fstb], [fstb], scale=1.0 / D, bias=EPS)
                    act(fst[0:tn, 4:5], fst[0:tn, 3:4], AF.Exp, [fstb], [fstb], scale=-0.5)
                    for hf in range(2):
                        hs = slice(hf * 512, (hf + 1) * 512)
                        stt('dve', ost[k][0:tn, hs], banks[fb[hf]][0:tn, :], fst[0:tn, 4:5], gpost_bc[0:tn, hs], ALU.mult, ALU.mult,
                            [bankbuf[fb[hf]][0], fstb, CB], [ostb[k]])
                        tt('dve', ost[k][0:tn, hs], ost[k][0:tn, hs], xblk[p][0:tn, tl, hs], ALU.add, [ostb[k], xblkb[p][tl]], [ostb[k]])
                    r0 = t0 + tl * 128
                    ydst = yp[r0:r0 + tn, :] if r0 < T else ys[0:tn, :]
                    S.dma('sp', lambda e, o=ydst, s=ost[k][0:tn, :]: e.dma_start(out=o, in_=s), osts[k], reads=[ostb[k]])
            S.barrier()

        with nc.Block() as block:
            S.emit(block)
    return nc


_NC = [None]


def _rope_table():
    half = 8
    inv = 500000.0 ** (-np.arange(0, 16, 2, dtype=np.float32) / np.float32(16))
    pos = np.concatenate([np.arange(T), PAST + np.arange(TS)]).astype(np.float32)
    ang = (pos[:, None] * inv[None, :].astype(np.float32)).astype(np.float32)
    c = np.cos(ang).astype(np.float32)
    s = np.sin(ang).astype(np.float32)
    return np.ascontiguousarray(np.concatenate([c, c, -s, s], axis=1).astype(np.float32))


def _lay(w):
    return np.ascontiguousarray(w.reshape(8, 128, w.shape[1]).transpose(1, 0, 2))


def kernel(x_prompt, x_sample, cache_diff_k, cache_diff_v, state_gla, pre_norm_g, w_in, gla_w_a2, gla_b_a, gla_norm_g,
           diff_lambda_q1, diff_lambda_k1, diff_lambda_q2, diff_lambda_k2, diff_subln_g, w_proj_gla, w_proj_diff, w_out,
           post_norm_g):
    f = lambda a: np.ascontiguousarray(np.asarray(a, dtype=np.float32))
    if _NC[0] is None:
        _NC[0] = build()
    nc = _NC[0]
    shared = {
        "wl": _lay(f(w_in)[0]), "wa2": f(gla_w_a2)[0], "ba": np.ascontiguousarray(f(gla_b_a)[0].reshape(4, 128).T),
        "g_pre": f(pre_norm_g), "g_post": f(post_norm_g),
        "g_gla": np.ascontiguousarray(f(gla_norm_g)[0].reshape(2, 128).T), "g_sub": f(diff_subln_g)[0].reshape(128, 1),
        "lam4": np.concatenate([f(diff_lambda_q1), f(diff_lambda_k1), f(diff_lambda_q2), f(diff_lambda_k2)], axis=1),
        "wpg": _lay(f(w_proj_gla)[0]), "wpd": _lay(f(w_proj_diff)[0]), "wo": _lay(f(w_out)[0]), "rope": _rope_table(),
    }
    xpf, xsf, ckf, cvf, sgf = f(x_prompt), f(x_sample), f(cache_diff_k), f(cache_diff_v), f(state_gla)
    in_maps = []
    for i in range(8):
        m = dict(shared)
        m.update({"xp": xpf[i], "xs": xsf[i], "ck": ckf[0, i].reshape(PAST, D), "cv": cvf[0, i].reshape(PAST, D), "sg": sgf[0, i]})
        in_maps.append(m)
    res = run_bass_kernel_spmd(nc, in_maps, core_ids=list(range(8)))
    R = res.results
    st = lambda k: np.stack([np.asarray(R[i][k], dtype=np.float32) for i in range(8)])
    return (st("yp"), st("ys"), st("nkp").reshape(1, 8, T, 8, 128), st("nvp").reshape(1, 8, T, 8, 128),
            st("ngp").reshape(1, 8, 4, 128, 256), st("nks").reshape(1, 8, TS, 8, 128), st("nvs").reshape(1, 8, TS, 8, 128),
            st("ngs").reshape(1, 8, 4, 128, 256))
```

```python
import math
from contextlib import ExitStack
import numpy as np
import concourse.bass as bass
import concourse.mybir as mybir
from concourse.bass_utils import run_bass_kernel_spmd

F32 = mybir.dt.float32
BF16 = mybir.dt.bfloat16
AF = mybir.ActivationFunctionType
ALU = mybir.AluOpType
AX = mybir.AxisListType

T = 4096
TS = 32
TT = T + TS
PAST = 2048
D = 1024
C_GQ, C_GK, C_GV, C_GR, C_GG = 0, 512, 1024, 2048, 2064
C_DQ, C_DK, C_DV, C_DG, C_MA, C_MB = 3088, 4112, 5136, 6160, 7184, 8208
IN_TOTAL = 9232
EPS = 1e-6
TILES = [(i * 128, 128) for i in range(32)] + [(T, TS)]
BLOCKS = [(i * 512, 512) for i in range(8)] + [(T, TS)]
NQB = 16
DBG = {'nh': 8, 'stage': 9, 'nqb': 16, 'skip': '', 'qb0': 0}


class Buf:
    __slots__ = ('w', 'r', 'excl')

    def __init__(self, excl=False):
        self.w = None
        self.r = {}
        self.excl = excl


class Sched:
    def __init__(self, nc, esems):
        self.nc = nc
        self.names = ['pe', 'act', 'dve', 'pool', 'sp']
        self.q = {k: [] for k in self.names}
        self.esem = esems
        self.cnt = {k: 0 for k in self.names}
        self.seen = {k: {} for k in self.names}
        self.pend = {k: ([], []) for k in self.names}
        self.last = {}

    def _waits(self, eng, toks):
        ws = []
        for t in toks:
            if t is None:
                continue
            if eng == 'pe' and t[2] == 'e_pe':
                continue
            if t[2] == 'e_' + eng and t[1] <= self.cnt[eng] - 3:
                continue
            if self.seen[eng].get(t[2], 0) < t[1]:
                self.seen[eng][t[2]] = t[1]
                ws.append(t)
        return ws

    def _deps(self, reads, writes):
        toks = []
        for b in reads:
            toks.append(b.w)
            if b.excl:
                toks.extend(b.r.values())
        for b in writes:
            toks.append(b.w)
            toks.extend(b.r.values())
        return toks

    def _commit(self, eng, tok, reads, writes):
        pr, pw = self.pend[eng]
        pr.extend(reads)
        pw.extend(writes)
        if tok is not None:
            for b in pr:
                if b.excl:
                    b.w = tok
                    b.r = {}
                else:
                    b.r[tok[2]] = tok
            for b in pw:
                b.w = tok
                b.r = {}
            self.pend[eng] = ([], [])
            self.last[tok[2]] = tok

    def op(self, eng, fn, reads=(), writes=(), signal=True, extra=()):
        ws = self._waits(eng, self._deps(reads, writes) + list(extra))
        tok = None
        if signal:
            self.cnt[eng] += 1
            tok = (self.esem[eng], self.cnt[eng], 'e_' + eng)
        self.q[eng].append((ws, fn, tok, 1))
        self._commit(eng, tok, reads, writes)
        return tok

    def dma(self, eng, fn, dsem, reads=(), writes=(), extra=()):
        ws = self._waits(eng, self._deps(reads, writes) + list(extra))
        dsem[1] += 16
        tok = (dsem[0], dsem[1], dsem[2])
        self.q[eng].append((ws, fn, tok, 16))
        for b in reads:
            b.r[tok[2]] = tok
        for b in writes:
            b.w = tok
            b.r = {}
        self.last[tok[2]] = tok
        return tok

    def barrier(self):
        toks = list(self.last.values())
        for e in self.names:
            ws = self._waits(e, toks)
            if ws:
                self.q[e].append((ws, None, None, 0))

    def emit(self, block):
        def mk(name):
            def body(e):
                for ws, fn, tok, inc in self.q[name]:
                    for w in ws:
                        e.wait_ge(w[0], w[1])
                    if fn is None:
                        continue
                    ins = fn(e)
                    if tok is not None:
                        ins.then_inc(tok[0], inc)
            return body
        block.tensor(mk('pe'))
        block.scalar(mk('act'))
        block.vector(mk('dve'))
        block.gpsimd(mk('pool'))
        block.sync(mk('sp'))


def build(stop=None):
    nc = bass.Bass("TRN2", target_bir_lowering=False)

    def din(name, shape, dt=F32):
        return nc.dram_tensor(name, shape, dt, kind="ExternalInput").ap()

    def dout(name, shape, dt=F32):
        return nc.dram_tensor(name, shape, dt, kind="ExternalOutput").ap()

    xp = din("xp", [T, D])
    xs = din("xs", [TS, D])
    ck = din("ck", [PAST, D])
    cv = din("cv", [PAST, D])
    sg = din("sg", [4, 128, 256])
    wl = din("wl", [128, 8, IN_TOTAL])
    wa2 = din("wa2", [16, 512])
    ba = din("ba", [128, 4])
    g_pre = din("g_pre", [1, D])
    g_post = din("g_post", [1, D])
    g_gla = din("g_gla", [128, 2])
    g_sub = din("g_sub", [128, 1])
    lam4 = din("lam4", [1, 256])
    wpg = din("wpg", [128, 8, D])
    wpd = din("wpd", [128, 8, D])
    wo = din("wo", [128, 8, D])
    rope = din("rope", [TT, 32])
    yp = dout("yp", [T, D])
    ys = dout("ys", [TS, D])
    nkp = dout("nkp", [T, D])
    nvp = dout("nvp", [T, D])
    ngp = dout("ngp", [4, 128, 256])
    nks = dout("nks", [TS, D])
    nvs = dout("nvs", [TS, D])
    ngs = dout("ngs", [4, 128, 256])
    if stop is None:
        ygT_d = nc.dram_tensor("ygT_scr", [D, TT], BF16).ap()
        ydT_d = nc.dram_tensor("ydT_scr", [D, TT], BF16).ap()
    else:
        ygT_d = dout("ygT_scr", [D, TT], BF16)
        ydT_d = dout("ydT_scr", [D, TT], BF16)
        dbg_xnT = dout("dbg_xnT", [128, 8, TT], BF16)

    def xrows(t0, n):
        return xp[t0:t0 + n, :] if t0 < T else xs[0:n, :]

    with ExitStack() as es:
        ARENA = 106200
        arena = es.enter_context(nc.sbuf_tensor("arena", [128, ARENA], BF16))
        aoff = [0]

        def sb(name, shape, dt, stack=None):
            n = 1
            for d_ in shape[1:]:
                n *= d_
            nb = n * (4 if dt == F32 else 2)
            nb = (nb + 31) // 32 * 32
            o = aoff[0]
            aoff[0] += nb // 2
            assert aoff[0] <= ARENA, (name, aoff[0])
            ap = arena[:, o:o + nb // 2]
            if dt == F32:
                ap = ap.bitcast(F32)
            ap = ap[:, 0:n]
            if len(shape) == 3:
                ap = ap.rearrange("p (a b) -> p a b", a=shape[1])
            elif len(shape) == 4:
                ap = ap.rearrange("p (a b c) -> p a b c", a=shape[1], b=shape[2])
            return ap

        def sem(name):
            return es.enter_context(nc.semaphore(name))

        nds = [0]

        def dsem():
            nds[0] += 1
            return [sem("ds%d" % nds[0]), 0, "ds%d" % nds[0]]

        esems = {k: sem("es_" + k) for k in ['pe', 'act', 'dve', 'pool', 'sp']}
        S = Sched(nc, esems)
        psum_all = es.enter_context(nc.psum_tensor("psum_all", [128, 8, 512], F32))
        banks = [psum_all[:, i, :] for i in range(8)]
        BK = [Buf(excl=True) for _ in range(8)]
        bankbuf = [[BK[i], BK[i]] for i in range(8)]

        ident = sb("ident", [128, 128], BF16)
        identf = sb("identf", [128, 128], F32)
        ones_bf = sb("ones_bf", [128, 128], BF16)
        ones_f = sb("ones_f", [128, 128], F32)
        trimask = sb("trimask", [128, 128], BF16)
        dmask = sb("dmask", [128, 2, 2, 256], BF16)
        g_bc = sb("g_bc", [128, D], F32)
        gpost_bc = sb("gpost_bc", [128, D], F32)
        ba_t = sb("ba_t", [128, 4], F32)
        nba_t = sb("nba_t", [128, 4], F32)
        wa2_bf = sb("wa2_bf", [128, 512], BF16)
        ggla_t = sb("ggla_t", [128, 2], F32)
        gsub_t = sb("gsub_t", [128, 1], F32)
        gsub08 = sb("gsub08", [128, 1], F32)
        lam_t = sb("lam_t", [128, 256], F32)
        lam_p = sb("lam_p", [128, 128], F32)
        lam_s = sb("lam_s", [128, 2], F32)
        lam_e = sb("lam_e", [128, 2], F32)
        neg_lam = sb("neg_lam", [128, 1], F32)
        rope_t = sb("rope_t", [128, 33, 32], F32)
        junk = sb("junk", [128, D], BF16)
        junkb = Buf()
        xn_tok = [sb("xn_tok%d" % i, [128, D], BF16) for i in range(2)]
        xnb = [Buf() for _ in range(2)]
        nstat2 = sb("nstat", [128, 8], F32)
        nstb2 = [Buf(), Buf()]
        mark_const = aoff[0]
        xnT = sb("xnT", [128, 8, TT], BF16)
        CB = Buf()
        XNT = Buf()

        d_c = dsem()
        cons = [
            ('sp', g_bc[:], g_pre.partition_broadcast(128)),
            ('sp', gpost_bc[:], g_post.partition_broadcast(128)),
            ('sp', ba_t[:], ba),
            ('sp', ggla_t[:], g_gla),
            ('sp', gsub_t[:], g_sub),
            ('sp', lam_t[:], lam4.partition_broadcast(128)),
            ('sp', rope_t[:, 0:32, :], rope[0:T, :].rearrange("(n p) c -> p n c", p=128)),
            ('sp', rope_t[0:TS, 32, :], rope[T:TT, :]),
            ('pool', wa2_bf[0:16, :], wa2),
        ]
        d_c2 = dsem()
        for eng, o, i in cons:
            S.dma(eng, lambda e, o=o, i=i: e.dma_start(out=o, in_=i), d_c if eng == 'sp' else d_c2, writes=[CB])
        P = lambda fn, **kw: S.op('pool', fn, **kw)
        P(lambda e: e.memset(ones_bf[:], 1.0), writes=[CB])
        P(lambda e: e.memset(ones_f[:], 1.0), writes=[CB])
        P(lambda e: e.affine_select(out=ident[:], in_=ones_bf[:], pattern=[[-1, 128]], compare_op=ALU.is_equal,
                                    fill=0.0, base=0, channel_multiplier=1), reads=[CB], writes=[CB])
        P(lambda e: e.affine_select(out=identf[:], in_=ones_f[:], pattern=[[-1, 128]], compare_op=ALU.is_equal,
                                    fill=0.0, base=0, channel_multiplier=1), reads=[CB], writes=[CB])
        P(lambda e: e.affine_select(out=trimask[:], in_=ones_bf[:], pattern=[[1, 128]], compare_op=ALU.is_ge,
                                    fill=0.0, base=0, channel_multiplier=-1), reads=[CB], writes=[CB])
        P(lambda e: e.memset(dmask[:], 1.0), writes=[CB])
        P(lambda e: e.memset(dmask[64:128, 0, :, 0:64], 0.0), reads=[CB], writes=[CB])
        P(lambda e: e.memset(dmask[0:64, 1, :, 0:128], 0.0), reads=[CB], writes=[CB])
        P(lambda e: e.memset(dmask[64:128, 1, :, 0:192], 0.0), reads=[CB], writes=[CB])
        V = lambda fn, **kw: S.op('dve', fn, **kw)
        A = lambda fn, **kw: S.op('act', fn, **kw)
        V(lambda e: e.tensor_scalar(out=nba_t[:], in0=ba_t[:], scalar1=-1.0, scalar2=None, op0=ALU.mult), reads=[CB], writes=[CB])
        V(lambda e: e.tensor_scalar(out=gsub08[:], in0=gsub_t[:], scalar1=0.8, scalar2=None, op0=ALU.mult), reads=[CB], writes=[CB])
        lam_v = lam_t[:].rearrange("p (a b d) -> p a b d", a=2, b=2)
        V(lambda e: e.tensor_tensor(out=lam_p[:].rearrange("p (a d) -> p a d", a=2), in0=lam_v[:, :, 0, :], in1=lam_v[:, :, 1, :],
                                    op=ALU.mult), reads=[CB], writes=[CB])
        V(lambda e: e.tensor_reduce(out=lam_s[:], in_=lam_p[:].rearrange("p (a d) -> p a d", a=2), axis=AX.X, op=ALU.add),
          reads=[CB], writes=[CB])
        A(lambda e: e.activation(out=lam_e[:], in_=lam_s[:], func=AF.Exp), reads=[CB], writes=[CB])
        V(lambda e: e.scalar_tensor_tensor(out=neg_lam[:], in0=lam_e[:, 1:2], scalar=-0.2, in1=lam_e[:, 0:1],
                                           op0=ALU.add, op1=ALU.subtract), reads=[CB], writes=[CB])
        S.barrier()

        def mm(out, lhsT, rhs, start, stop, reads, writes, last):
            return S.op('pe', lambda e: e.matmul(out, lhsT=lhsT, rhs=rhs, start=start, stop=stop),
                        reads=reads, writes=writes, signal=last)

        def tr(out, in_, idn, reads, writes, last=True):
            return S.op('pe', lambda e: e.transpose(out=out, in_=in_, identity=idn), reads=reads, writes=writes, signal=last)

        def act(out, in_, func, reads, writes, **kw):
            return S.op('act', lambda e: e.activation(out=out, in_=in_, func=func, **kw), reads=reads, writes=writes)

        def tt(eng, out, in0, in1, op, reads, writes):
            return S.op(eng, lambda e: e.tensor_tensor(out=out, in0=in0, in1=in1, op=op), reads=reads, writes=writes)

        def ts(eng, out, in0, s1, op0, reads, writes, s2=None, op1=None):
            if op1 is None:
                return S.op(eng, lambda e: e.tensor_scalar(out=out, in0=in0, scalar1=s1, scalar2=None, op0=op0),
                            reads=reads, writes=writes)
            return S.op(eng, lambda e: e.tensor_scalar(out=out, in0=in0, scalar1=s1, scalar2=s2, op0=op0, op1=op1),
                        reads=reads, writes=writes)

        def stt(eng, out, in0, scalar, in1, op0, op1, reads, writes):
            return S.op(eng, lambda e: e.scalar_tensor_tensor(out=out, in0=in0, scalar=scalar, in1=in1, op0=op0, op1=op1),
                        reads=reads, writes=writes)

        def cp(eng, out, in_, reads, writes):
            if eng == 'act':
                return S.op('act', lambda e: e.copy(out=out, in_=in_), reads=reads, writes=writes)
            return S.op(eng, lambda e: e.tensor_copy(out=out, in_=in_), reads=reads, writes=writes)

        NW = 3
        wslot = [sb("wslot%d" % i, [128, 8, 384], BF16) for i in range(NW)]
        wbuf = [Buf() for _ in range(NW)]
        wsem = [dsem() for _ in range(NW)]
        wctr = [0]

        def load_w(cols):
            i = wctr[0] % NW
            wctr[0] += 1
            off = 0
            for c0, n in cols:
                S.dma('pool', lambda e, o=wslot[i][:, :, off:off + n], s=wl[:, :, c0:c0 + n]: e.dma_start(out=o, in_=s),
                      wsem[i], writes=[wbuf[i]])
                off += n
            return wslot[i], wbuf[i]

        prot = [0]

        def next_bank(lo, hi):
            b = lo + prot[0] % (hi - lo)
            prot[0] += 1
            return b

        mark_mid = aoff[0]
        xin = [sb("xin%d" % i, [128, D], F32) for i in range(2)]
        xinb = [Buf() for _ in range(2)]
        xsem = [dsem() for _ in range(2)]
        def norm_a(i, n, xtile, xbuf):
            k = i % 2
            nstat, nstb = nstat2[:, 4 * k:4 * k + 4], nstb2[k]
            act(junk[0:n, :], xtile[0:n, :], AF.Square, [xbuf], [junkb, nstb], accum_out=nstat[0:n, 0:1])
            act(nstat[0:n, 1:2], nstat[0:n, 0:1], AF.Ln, [nstb], [nstb], scale=1.0 / D, bias=EPS)
            act(nstat[0:n, 2:3], nstat[0:n, 1:2], AF.Exp, [nstb], [nstb], scale=-0.5)

        def norm_b(i, n, xtile, xbuf):
            k = i % 2
            nstat, nstb = nstat2[:, 4 * k:4 * k + 4], nstb2[k]
            stt('dve', xn_tok[k][0:n, :], xtile[0:n, :], nstat[0:n, 2:3], g_bc[0:n, :], ALU.mult, ALU.mult,
                [xbuf, nstb], [xnb[k]])

        def norm_c(i, n, dstT, dstbuf, dcol):
            k = i % 2
            bk = next_bank(0, 2)
            pT = banks[bk][:].bitcast(BF16).rearrange("p (k t) -> p k t", k=8)
            for kc in range(8):
                tr(pT[:, kc, 0:n], xn_tok[k][0:n, kc * 128:(kc + 1) * 128], ident[0:n, 0:n], [xnb[k]], [bankbuf[bk][0]],
                   last=(kc == 7))
            cp('dve', dstT[:, :, dcol:dcol + n], pT[:, :, 0:n], [bankbuf[bk][0]], [dstbuf])

        def norm_tile(i, t0, n, xtile, xbuf, dstT, dstbuf, dcol):
            norm_a(i, n, xtile, xbuf)
            norm_b(i, n, xtile, xbuf)
            norm_c(i, n, dstT, dstbuf, dcol)

        NTL0 = len(TILES)
        for st_ in range(NTL0 + 2):
            if st_ < NTL0:
                t0, n = TILES[st_]
                k = st_ % 2
                S.dma('sp', lambda e, o=xin[k][0:n, :], s=xrows(t0, n): e.dma_start(out=o, in_=s), xsem[k], writes=[xinb[k]])
                norm_a(st_, n, xin[k], xinb[k])
            if 0 <= st_ - 1 < NTL0:
                t0, n = TILES[st_ - 1]
                norm_b(st_ - 1, n, xin[(st_ - 1) % 2], xinb[(st_ - 1) % 2])
            if 0 <= st_ - 2 < NTL0:
                t0, n = TILES[st_ - 2]
                norm_c(st_ - 2, n, xnT, XNT, t0)
        S.barrier()
        if stop is not None:
            dd = dsem()
            S.dma('sp', lambda e: e.dma_start(out=dbg_xnT, in_=xnT), dd, reads=[XNT])
            S.barrier()

        aoff[0] = mark_mid
        with ExitStack() as gs:
          if stop is None or stop >= 2:
            grT = sb("grT", [128, TT], BF16, gs)
            GRT = Buf()
            qdT = sb("qdT", [128, TT], BF16, gs)
            kdT = sb("kdT", [128, TT], BF16, gs)
            vtok = sb("vtok", [128, 33, 256], BF16, gs)
            gsil = sb("gsil", [128, 2, TT], BF16, gs)
            ncl = sb("ncl", [128, 33], F32, gs)
            ebl = sb("ebl", [128, 33], F32, gs)
            HB = Buf()
            NT = 2
            tmpA = [sb("gtA%d" % i, [128, 512], F32, gs) for i in range(NT)]
            tmpB = [sb("gtB%d" % i, [128, 512], F32, gs) for i in range(NT)]
            tmpC = [sb("gtC%d" % i, [128, 512], F32, gs) for i in range(NT)]
            tmpD = [sb("gtD%d" % i, [128, 512], F32, gs) for i in range(NT)]
            tAb = [Buf() for _ in range(NT)]
            tBb = [Buf() for _ in range(NT)]
            tCb = [Buf() for _ in range(NT)]
            tDb = [Buf() for _ in range(NT)]
            Sst = [sb("Sst%d" % i, [128, 256], F32, gs) for i in range(2)]
            Sbf = [sb("Sbf%d" % i, [128, 256], BF16, gs) for i in range(2)]
            Sb = [Buf() for _ in range(2)]
            Sbb = [Buf() for _ in range(2)]
            Am = [sb("Am%d" % i, [128, 128], BF16, gs) for i in range(2)]
            Amb = [Buf() for _ in range(2)]
            kd2 = [sb("kd2_%d" % i, [128, 128], BF16, gs) for i in range(2)]
            kd2b = [Buf() for _ in range(2)]
            Uc = [sb("Uc%d" % i, [128, 256], F32, gs) for i in range(2)]
            Ucb = [Buf() for _ in range(2)]
            sqg = [sb("sqg%d" % i, [128, 2, 512], BF16, gs) for i in range(2)]
            sqgb = [Buf() for _ in range(2)]
            rsg = [sb("rsg%d" % i, [128, 512], F32, gs) for i in range(2)]
            rsgb = [Buf() for _ in range(2)]
            ygo = [sb("ygo%d" % i, [128, 2, 512], BF16, gs) for i in range(2)]
            ygob = [Buf() for _ in range(2)]
            ygsem = [dsem() for _ in range(2)]
            ssem = dsem()
            sosem = [dsem() for _ in range(2)]

            wt, wb_ = load_w([(C_GR, 16)])
            for (t0, n) in BLOCKS:
                bk = next_bank(0, 4)
                for kc in range(8):
                    mm(banks[bk][0:16, 0:n], wt[:, kc, 0:16], xnT[:, kc, t0:t0 + n], kc == 0, kc == 7,
                       [wb_, XNT], [bankbuf[bk][0]], kc == 7)
                cp('act', grT[0:16, t0:t0 + n], banks[bk][0:16, 0:n], [bankbuf[bk][0]], [GRT])

            for h in range(4):
                if h == 0:
                    gpre = (load_w([(C_GQ + h * 128, 128), (C_GK + h * 128, 128)]), load_w([(C_GG + h * 256, 256)]),
                            load_w([(C_GV + h * 256, 256)]))
                (wq, wqb), (wg_, wgb), (wv, wvb) = gpre
                for bi, (t0, n) in enumerate(BLOCKS):
                    k = bi % NT
                    bk = next_bank(0, 4)
                    mm(banks[bk][:, 0:n], wa2_bf[0:16, h * 128:(h + 1) * 128], grT[0:16, t0:t0 + n], True, True,
                       [GRT, CB], [bankbuf[bk][0]], True)
                    act(tmpA[k][:, 0:n], banks[bk][:, 0:n], AF.Exp, [bankbuf[bk][0]], [tAb[k]], scale=-1.0, bias=nba_t[:, h:h + 1])
                    act(tmpA[k][:, 0:n], tmpA[k][:, 0:n], AF.Ln, [tAb[k]], [tAb[k]], bias=1.0)
                    for c0 in range(0, n, 128):
                        cn = min(128, n - c0)
                        ci = (t0 + c0) // 128
                        S.op('dve', lambda e, o=tmpB[k][:, c0:c0 + cn], d1=tmpA[k][:, c0:c0 + cn], d0=ones_bf[:, 0:cn]:
                             e.tensor_tensor_scan(out=o, data0=d0, data1=d1, initial=0.0, op0=ALU.mult, op1=ALU.add),
                             reads=[tAb[k]], writes=[tBb[k]])
                        ts('dve', ncl[:, ci:ci + 1], tmpB[k][:, c0 + cn - 1:c0 + cn], -1.0 / 16, ALU.mult, [tBb[k]], [HB])
                    act(tmpC[k][:, 0:n], tmpB[k][:, 0:n], AF.Exp, [tBb[k]], [tCb[k]], scale=-1.0 / 16)
                    act(tmpD[k][:, 0:n], tmpB[k][:, 0:n], AF.Exp, [tBb[k]], [tDb[k]], scale=1.0 / 16)
                    bk = next_bank(0, 4)
                    for kc in range(8):
                        mm(banks[bk][:, 0:n], wq[:, kc, 0:128], xnT[:, kc, t0:t0 + n], kc == 0, kc == 7, [wqb, XNT],
                           [bankbuf[bk][0]], kc == 7)
                    stt('dve', qdT[:, t0:t0 + n], banks[bk][:, 0:n], 128 ** -0.5, tmpC[k][:, 0:n], ALU.mult, ALU.mult,
                        [bankbuf[bk][0], tCb[k]], [HB])
                    bk = next_bank(0, 4)
                    for kc in range(8):
                        mm(banks[bk][:, 0:n], wq[:, kc, 128:256], xnT[:, kc, t0:t0 + n], kc == 0, kc == 7, [wqb, XNT],
                           [bankbuf[bk][0]], kc == 7)
                    tt('dve', kdT[:, t0:t0 + n], banks[bk][:, 0:n], tmpD[k][:, 0:n], ALU.mult, [bankbuf[bk][0], tDb[k]], [HB])
                for bi, (t0, n) in enumerate(BLOCKS):
                    k = bi % NT
                    for et in range(2):
                        bk = next_bank(0, 4)
                        for kc in range(8):
                            mm(banks[bk][:, 0:n], wg_[:, kc, et * 128:(et + 1) * 128], xnT[:, kc, t0:t0 + n], kc == 0, kc == 7,
                               [wgb, XNT], [bankbuf[bk][0]], kc == 7)
                        tgt = tmpC if et == 0 else tmpD
                        tgb = tCb if et == 0 else tDb
                        act(tgt[k][:, 0:n], banks[bk][:, 0:n], AF.Silu, [bankbuf[bk][0]], [tgb[k]])
                        ts('dve', gsil[:, et, t0:t0 + n], tgt[k][:, 0:n], ggla_t[:, et:et + 1], ALU.mult, [tgb[k]], [HB])
                for i, (t0, n) in enumerate(TILES):
                    bk = next_bank(0, 4)
                    for kc in range(8):
                        mm(banks[bk][0:n, 0:256], xnT[:, kc, t0:t0 + n], wv[:, kc, 0:256], kc == 0, kc == 7, [wvb, XNT],
                           [bankbuf[bk][0]], kc == 7)
                    cp('act', vtok[0:n, i, :], banks[bk][0:n, 0:256], [bankbuf[bk][0]], [HB])
                act(ebl[:, :], ncl[:, :], AF.Exp, [HB], [HB])
                if h + 1 < 4:
                    h1 = h + 1
                    gpre = (load_w([(C_GQ + h1 * 128, 128), (C_GK + h1 * 128, 128)]), load_w([(C_GG + h1 * 256, 256)]),
                            load_w([(C_GV + h1 * 256, 256)]))

                def pre(ci, t0, n):
                    a = ci % 2
                    mm(banks[2][0:n, 0:n], kdT[:, t0:t0 + n], qdT[:, t0:t0 + n], True, True, [HB],
                       [bankbuf[2][0]], True)
                    tt('dve', Am[a][0:n, 0:n], banks[2][0:n, 0:n], trimask[0:n, 0:n], ALU.mult,
                       [bankbuf[2][0], CB], [Amb[a]])
                    pT = banks[4][:].bitcast(BF16)
                    tr(pT[0:n, 0:128], kdT[:, t0:t0 + n], ident[:, :], [HB], [bankbuf[4][0]])
                    cp('act', kd2[a][0:n, :], pT[0:n, 0:128], [bankbuf[4][0]], [kd2b[a]])
                    mm(banks[6][:, 0:256], kd2[a][0:n, :], vtok[0:n, ci, :], True, True, [kd2b[a], HB],
                       [bankbuf[6][0]], True)
                    act(Uc[a][:, :], banks[6][:, 0:256], AF.Copy, [bankbuf[6][0], HB], [Ucb[a]], scale=ebl[:, ci:ci + 1])

                def dep(ci, t0, n, si, pO, pOb, col):
                    a = ci % 2
                    for et in range(2):
                        mm(pO[et][:, col:col + n], Sbf[si][:, et * 128:(et + 1) * 128], qdT[:, t0:t0 + n], True, False,
                           [Sbb[si], HB], [pOb[et]], False)
                        mm(pO[et][:, col:col + n], vtok[0:n, ci, et * 128:(et + 1) * 128], Am[a][0:n, 0:n], False, True,
                           [Amb[a], HB], [pOb[et]], True)
                    stt('dve', Sst[1 - si][:, :], Sst[si][:, :], ebl[:, ci:ci + 1], Uc[a][:, :],
                        ALU.mult, ALU.add, [Sb[si], HB, Ucb[a]], [Sb[1 - si]])
                    cp('act', Sbf[1 - si][:, :], Sst[1 - si][:, :], [Sb[1 - si]], [Sbb[1 - si]])

                def finish_block(bi, t0, n, pO, pOb):
                    k = bi % 2
                    for et in range(2):
                        act(sqg[k][:, et, 0:n], pO[et][:, 0:n], AF.Square, [pOb[et]], [sqgb[k]])
                    for et in range(2):
                        mm(banks[7][:, 0:n], ones_bf[:, :], sqg[k][:, et, 0:n], et == 0, et == 1, [sqgb[k], CB],
                           [bankbuf[7][0]], et == 1)
                    act(rsg[k][:, 0:n], banks[7][:, 0:n], AF.Ln, [bankbuf[7][0]], [rsgb[k]], scale=1.0 / 256, bias=EPS)
                    act(rsg[k][:, 0:n], rsg[k][:, 0:n], AF.Exp, [rsgb[k]], [rsgb[k]], scale=-0.5)
                    for et in range(2):
                        tt('dve', tmpB[et][:, 0:n], pO[et][:, 0:n], rsg[k][:, 0:n], ALU.mult, [pOb[et], rsgb[k]], [tBb[et]])
                        tt('dve', ygo[k][:, et, 0:n], tmpB[et][:, 0:n], gsil[:, et, t0:t0 + n], ALU.mult, [tBb[et], HB], [ygob[k]])
                    dst = ygT_d[h * 256:(h + 1) * 256, t0:t0 + n].rearrange("(a p) t -> p a t", p=128)
                    S.dma('sp', lambda e, o=dst, s=ygo[k][:, :, 0:n]: e.dma_start(out=o, in_=s), ygsem[k], reads=[ygob[k]])

                S.op('pool', lambda e: e.memset(Sst[0][:, :], 0.0), writes=[Sb[0]])
                S.op('pool', lambda e: e.memset(Sbf[0][:, :], 0.0), writes=[Sbb[0]])
                si = 0
                pOs = ([banks[0], banks[1]], [banks[3], banks[5]])
                pObs = ([bankbuf[0][0], bankbuf[1][0]], [bankbuf[3][0], bankbuf[5][0]])
                chunks = [(ci, ci * 128, 128) for ci in range(32)] + [(32, T, TS)]
                pre(*chunks[0])
                for idx, (ci, t0, n) in enumerate(chunks):
                    pO, pOb = pOs[(ci // 4) % 2], pObs[(ci // 4) % 2]
                    if idx + 1 < len(chunks):
                        pre(*chunks[idx + 1])
                    if ci == 32:
                        S.dma('sp', lambda e, o=ngp[h], s=Sst[si][:, :]: e.dma_start(out=o, in_=s), sosem[0], reads=[Sb[si]])
                        si = 1 - si
                        S.dma('sp', lambda e, o=Sst[si][:, :], s=sg[h]: e.dma_start(out=o, in_=s), ssem, writes=[Sb[si]])
                        cp('act', Sbf[si][:, :], Sst[si][:, :], [Sb[si]], [Sbb[si]])
                        dep(ci, t0, n, si, pO, pOb, 0)
                        si = 1 - si
                        finish_block(8, T, TS, pO, pOb)
                        S.dma('sp', lambda e, o=ngs[h], s=Sst[si][:, :]: e.dma_start(out=o, in_=s), sosem[1], reads=[Sb[si]])
                    else:
                        dep(ci, t0, n, si, pO, pOb, (ci % 4) * 128)
                        si = 1 - si
                        if ci % 4 == 3:
                            finish_block(ci // 4, (ci // 4) * 512, 512, pO, pOb)
            S.barrier()

        aoff[0] = mark_mid
        with ExitStack() as ds:
          if stop is None or stop >= 3:
            QT = sb("QT", [128, TT], BF16, ds)
            KT = sb("KT", [128, TT + PAST], BF16, ds)
            Vall = sb("Vall", [128, 49, 128], BF16, ds)
            kcb = sb("kcb", [128, 16, 128], BF16, ds)
            dgs = sb("dgs", [128, TT], BF16, ds)
            HB = Buf()
            KCB = Buf()
            kcsem = dsem()
            vcsem = dsem()
            stg = [sb("stg%d" % i, [128, 384], F32, ds) for i in range(4)]
            stgb = [Buf() for _ in range(4)]
            stgs = [dsem() for _ in range(4)]
            qkc = [sb("qkc%d" % i, [128, 128], BF16, ds) for i in range(3)]
            qkcb = [Buf() for _ in range(3)]
            rtA = [sb("rtA%d" % i, [128, 4, 16], F32, ds) for i in range(2)]
            rtB = [sb("rtB%d" % i, [128, 4, 16], F32, ds) for i in range(2)]
            rtb = [Buf() for _ in range(2)]
            qkbf = [sb("qkbf%d" % i, [128, 256], BF16, ds) for i in range(3)]
            qkbfb = [Buf() for _ in range(3)]
            sqn = [sb("sqn%d" % i, [128, 256], F32, ds) for i in range(2)]
            red = [sb("red%d" % i, [128, 4], F32, ds) for i in range(2)]
            sqnb = [Buf() for _ in range(2)]
            nmax = sb("nmax", [128, 4], F32, ds)
            nmb = Buf()
            bnd = sb("bnd", [128, 4], F32, ds)
            negc = sb("negc", [128, 1], F32, ds)
            bndb = Buf()
            tS = [sb("atS%d" % i, [128, 512], F32, ds) for i in range(2)]
            tSb = [Buf() for _ in range(2)]
            racc = [sb("racc%d" % i, [128, 2, 2, 256], F32, ds) for i in range(2)]
            raccb = [Buf() for _ in range(2)]
            rbf = [sb("rbf%d" % i, [128, 2, 256], BF16, ds) for i in range(2)]
            rbfb = [Buf() for _ in range(2)]
            NP = 3
            PT = [sb("PT%d" % i, [128, 2, 2, 256], BF16, ds) for i in range(NP)]
            PTb = [Buf() for _ in range(NP)]
            rs = [sb("ars%d" % i, [128, 512], F32, ds) for i in range(2)]
            o1 = [sb("ao1%d" % i, [128, 2, 256], F32, ds) for i in range(2)]
            od = [sb("aod%d" % i, [128, 256], F32, ds) for i in range(2)]
            sqd = [sb("asq%d" % i, [128, 256], BF16, ds) for i in range(2)]
            rsd = [sb("arsd%d" % i, [128, 256], F32, ds) for i in range(2)]
            ydo = [sb("ydo%d" % i, [128, 256], BF16, ds) for i in range(2)]
            nb_ = [Buf() for _ in range(2)]
            ydob = [Buf() for _ in range(2)]
            ydsem = [dsem() for _ in range(2)]
            pctr = [0]
            actr = [0]

            PE_ = 'dve'
            RE_ = 'dve' if 'dverope' in DBG['skip'] else 'pool'
            for h in range(DBG['nh']):
                def issue_loads(hh):
                    a = load_w([(C_DQ + hh * 128, 128), (C_DK + hh * 128, 128), (C_DV + hh * 128, 128)])
                    b = load_w([(C_DG + hh * 128, 128)])
                    for q4 in range(4):
                        rs_ = slice(q4 * 512, (q4 + 1) * 512)
                        S.dma('pool', lambda e, o=kcb[:, q4 * 4:(q4 + 1) * 4, :], s=ck[rs_, hh * 128:(hh + 1) * 128].rearrange("(n p) c -> p n c", p=128):
                              e.dma_start(out=o, in_=s), kcsem, writes=[KCB])
                    return a, b

                if h == 0:
                    apre = issue_loads(0)
                (wqkv, wqkvb), (wdg, wdgb) = apre
                for q4 in range(4):
                    rs_ = slice(q4 * 512, (q4 + 1) * 512)
                    S.dma('pool', lambda e, o=Vall[:, 33 + q4 * 4:33 + (q4 + 1) * 4, :], s=cv[rs_, h * 128:(h + 1) * 128].rearrange("(n p) c -> p n c", p=128):
                          e.dma_start(out=o, in_=s), vcsem, writes=[HB])
                S.op('pool', lambda e: e.memset(nmax[:, :], 0.0), writes=[nmb])

                def norms(src, n, k, srcb):
                    G = src.shape[1] // 64
                    tt('dve', sqn[k][0:n, 0:G * 64], src, src, ALU.mult, [srcb], [sqnb[k]])
                    S.op('dve', lambda e: e.tensor_reduce(out=red[k][0:n, 0:G], in_=sqn[k][0:n, 0:G * 64].rearrange("p (g d) -> p g d", g=G),
                                                          axis=AX.X, op=ALU.add), reads=[sqnb[k]], writes=[sqnb[k]])
                    tt('dve', nmax[0:n, 0:G], nmax[0:n, 0:G], red[k][0:n, 0:G], ALU.max, [sqnb[k], nmb], [nmb])

                def stA(i):
                    t0, n = TILES[i]
                    k4 = i % 4
                    bk = i % 2
                    for kc in range(8):
                        mm(banks[bk][0:n, 0:384], xnT[:, kc, t0:t0 + n], wqkv[:, kc, 0:384], kc == 0, kc == 7, [wqkvb, XNT],
                           [bankbuf[bk][0]], kc == 7)
                    cp('act', stg[k4][0:n, :], banks[bk][0:n, 0:384], [bankbuf[bk][0]], [stgb[k4]])

                def stB(i):
                    t0, n = TILES[i]
                    k4 = i % 4
                    k = i % 2
                    k3 = i % 3
                    cp('act', Vall[0:n, i, :], stg[k4][0:n, 256:384], [stgb[k4]], [HB])
                    pq = stg[k4][0:n, 0:256].rearrange("p (g d) -> p g d", g=4)
                    c16 = rope_t[0:n, i, 0:16].unsqueeze(1).to_broadcast([n, 4, 16])
                    sa8 = rope_t[0:n, i, 16:24].unsqueeze(1).to_broadcast([n, 4, 8])
                    sb8 = rope_t[0:n, i, 24:32].unsqueeze(1).to_broadcast([n, 4, 8])
                    S.op(RE_, lambda e, o=rtA[k][0:n], a=pq[:, :, 0:16], b=c16: e.tensor_tensor(out=o, in0=a, in1=b, op=ALU.mult),
                         reads=[stgb[k4], CB], writes=[rtb[k]], signal=False)
                    S.op(RE_, lambda e, o=rtB[k][0:n, :, 0:8], a=pq[:, :, 8:16], b=sa8: e.tensor_tensor(out=o, in0=a, in1=b, op=ALU.mult),
                         reads=[stgb[k4], CB], writes=[rtb[k]], signal=False)
                    tt(RE_, rtB[k][0:n, :, 8:16], pq[:, :, 0:8], sb8, ALU.mult, [stgb[k4], CB], [rtb[k]])
                    tt(RE_, pq[:, :, 0:16], rtA[k][0:n], rtB[k][0:n], ALU.add, [rtb[k]], [stgb[k4]])
                    vdst = nvp[t0:t0 + n, h * 128:(h + 1) * 128] if t0 < T else nvs[0:n, h * 128:(h + 1) * 128]
                    kdst = nkp[t0:t0 + n, h * 128:(h + 1) * 128] if t0 < T else nks[0:n, h * 128:(h + 1) * 128]
                    S.dma('sp', lambda e, o=vdst, s=stg[k4][0:n, 256:384]: e.dma_start(out=o, in_=s), stgs[k4], reads=[stgb[k4]])
                    S.dma('sp', lambda e, o=kdst, s=stg[k4][0:n, 128:256]: e.dma_start(out=o, in_=s), stgs[k4], reads=[stgb[k4]])
                    cp('act', qkbf[k3][0:n, :], stg[k4][0:n, 0:256], [stgb[k4]], [qkbfb[k3]])
                    norms(qkbf[k3][0:n, :], n, k, qkbfb[k3])

                def stC(i):
                    t0, n = TILES[i]
                    k = i % 2
                    k3 = i % 3
                    pT = banks[2 + k].bitcast(BF16)
                    tr(pT[:, 0:n], qkbf[k3][0:n, 0:128], ident[0:n, 0:n], [qkbfb[k3]], [bankbuf[2 + k][0]], last=False)
                    tr(pT[:, 128:128 + n], qkbf[k3][0:n, 128:256], ident[0:n, 0:n], [qkbfb[k3]], [bankbuf[2 + k][0]])
                    cp('act', QT[:, t0:t0 + n], pT[:, 0:n], [bankbuf[2 + k][0]], [HB])
                    cp('act', KT[:, t0:t0 + n], pT[:, 128:128 + n], [bankbuf[2 + k][0]], [HB])

                def dgblk(bi):
                    t0, n = BLOCKS[bi]
                    k = bi % 2
                    for kc in range(8):
                        mm(banks[7][:, 0:n], wdg[:, kc, 0:128], xnT[:, kc, t0:t0 + n], kc == 0, kc == 7, [wdgb, XNT],
                           [bankbuf[7][0]], kc == 7)
                    act(tS[k][:, 0:n], banks[7][:, 0:n], AF.Silu, [bankbuf[7][0]], [tSb[k]])
                    ts('dve', dgs[:, t0:t0 + n], tS[k][:, 0:n], gsub08[:, 0:1], ALU.mult, [tSb[k], CB], [HB])

                NTL = len(TILES)
                ndg = 0
                for st_ in range(NTL + 2):
                    if st_ < NTL:
                        stA(st_)
                    if 0 <= st_ - 1 < NTL:
                        stB(st_ - 1)
                    if 0 <= st_ - 2 < NTL:
                        stC(st_ - 2)
                    if st_ % 4 == 3 and ndg < len(BLOCKS):
                        dgblk(ndg)
                        ndg += 1
                while ndg < len(BLOCKS):
                    dgblk(ndg)
                    ndg += 1

                def caA(j):
                    k3 = j % 3
                    cp('dve', qkc[k3][:, :], kcb[:, j, :], [KCB], [qkcb[k3]])
                    norms(qkc[k3][:, :], 128, j % 2, qkcb[k3])

                def caC(j):
                    k = j % 2
                    k3 = j % 3
                    pT = banks[2 + k].bitcast(BF16)
                    tr(pT[:, 0:128], qkc[k3][:, :], ident[:, :], [qkcb[k3]], [bankbuf[2 + k][0]])
                    cp('act', KT[:, TT + j * 128:TT + (j + 1) * 128], pT[:, 0:128], [bankbuf[2 + k][0]], [HB])

                for st_ in range(17):
                    if st_ < 16:
                        caA(st_)
                    if st_ >= 1:
                        caC(st_ - 1)
                if DBG['stage'] < 1:
                    continue
                S.op('dve', lambda e: e.tensor_reduce(out=bnd[:, 0:1], in_=nmax[:, :], axis=AX.X, op=ALU.max), reads=[nmb], writes=[bndb])
                S.op('pe', lambda e: e.transpose(out=banks[7][0:1, 0:128], in_=bnd[:, 0:1], identity=identf[:, :]),
                     reads=[bndb, CB], writes=[bankbuf[7][0]])
                S.op('dve', lambda e: e.tensor_reduce(out=bnd[0:1, 1:2], in_=banks[7][0:1, 0:128], axis=AX.X, op=ALU.max),
                     reads=[bankbuf[7][0]], writes=[bndb])
                S.op('pe', lambda e: e.matmul(banks[7][:, 256:257], lhsT=ones_f[0:1, :], rhs=bnd[0:1, 1:2], start=True, stop=True),
                     reads=[bndb, CB], writes=[bankbuf[7][1]])
                ts('dve', negc[:, 0:1], banks[7][:, 256:257], -0.125, ALU.mult, [bankbuf[7][1]], [bndb])

                if h + 1 < DBG['nh']:
                    apre = issue_loads(h + 1)
                jobs = []

                def add_block(q0, nq, ktiles):
                    w = actr[0] % 2
                    actr[0] += 1
                    pairs = [ktiles[a:a + 2] for a in range(0, len(ktiles), 2)]
                    for g, pr in enumerate(pairs):
                        jobs.append(dict(w=w, q0=q0, nq=nq, pr=pr, g=g, np=len(pairs)))

                if DBG['stage'] >= 2:
                    for qb in (DBG['qbl'] if DBG.get('qbl') else range(DBG['qb0'], DBG['nqb'])):
                        kts = [(j * 128, 128, j, None) for j in range(2 * qb)]
                        kts += [((2 * qb) * 128, 128, 2 * qb, 0), ((2 * qb + 1) * 128, 128, 2 * qb + 1, 1)]
                        add_block(qb * 256, 256, kts)
                if DBG['stage'] >= 3:
                    add_block(T, TS, [(TT + j * 128, 128, 33 + j, None) for j in range(16)] + [(T, TS, 32, None)])

                def qk(ji):
                    jb = jobs[ji]
                    b0 = 2 + 2 * (ji % 2)
                    pr, q0, nq = jb['pr'], jb['q0'], jb['nq']
                    for t, (kc0, nk, vt, r) in enumerate(pr):
                        for m in range(2):
                            mm(banks[b0 + m][0:nk, t * 256:t * 256 + nq], KT[m * 64:(m + 1) * 64, kc0:kc0 + nk],
                               QT[m * 64:(m + 1) * 64, q0:q0 + nq], True, True, [HB], [bankbuf[b0 + m][0]],
                               t == len(pr) - 1 and m == 1)

                def norm_closures(jb):
                    w, q0, nq = jb['w'], jb['q0'], jb['nq']
                    full = (nq == 256)
                    pvi, psi = (6, 7) if w == 0 else (0, 1)
                    pv, psm = banks[pvi], banks[psi]
                    pvb, psb = bankbuf[pvi][0], bankbuf[psi][0]
                    pv3 = pv[:, 0:512].rearrange("p (m q) -> p m q", m=2)[:, :, 0:nq]
                    ps3 = psm[:, 0:512].rearrange("p (m q) -> p m q", m=2)[:, :, 0:nq]
                    rs3 = rs[w][:, :].rearrange("p (m q) -> p m q", m=2)[:, :, 0:nq]

                    def n1():
                        tt('dve', rbf[w][:, :, 0:nq], racc[w][:, 0, :, 0:nq], racc[w][:, 1, :, 0:nq], ALU.add, [raccb[w]], [rbfb[w]])
                        if full:
                            mm(psm[:, 0:512], ones_bf[:, :], rbf[w][:, :, :].rearrange("p m q -> p (m q)"), True, True,
                               [rbfb[w], CB], [psb], True)
                        else:
                            for m in range(2):
                                mm(psm[:, m * 256:m * 256 + nq], ones_bf[:, :], rbf[w][:, m, 0:nq], m == 0, m == 1, [rbfb[w], CB], [psb], m == 1)

                    def n2():
                        act(rs3, ps3, AF.Ln, [psb], [nb_[w]])
                        act(rs3, rs3, AF.Exp, [nb_[w]], [nb_[w]], scale=-1.0)

                    def n3():
                        tt('dve', o1[w][:, :, 0:nq], pv3, rs3, ALU.mult, [pvb, nb_[w]], [nb_[w]])
                        stt('dve', od[w][:, 0:nq], o1[w][:, 1, 0:nq], neg_lam[:, 0:1], o1[w][:, 0, 0:nq], ALU.mult, ALU.add,
                            [nb_[w], CB], [nb_[w]])

                    def n4():
                        act(sqd[w][:, 0:nq], od[w][:, 0:nq], AF.Square, [nb_[w]], [nb_[w]])
                        mm(psm[:, 0:nq], ones_bf[:, :], sqd[w][:, 0:nq], True, True, [nb_[w], CB], [psb], True)

                    def n5():
                        act(rsd[w][:, 0:nq], psm[:, 0:nq], AF.Ln, [psb], [nb_[w]], scale=1.0 / 128, bias=EPS)
                        act(rsd[w][:, 0:nq], rsd[w][:, 0:nq], AF.Exp, [nb_[w]], [nb_[w]], scale=-0.5)

                    def n6():
                        tt('dve', od[w][:, 0:nq], od[w][:, 0:nq], rsd[w][:, 0:nq], ALU.mult, [nb_[w]], [nb_[w]])
                        tt('dve', ydo[w][:, 0:nq], od[w][:, 0:nq], dgs[:, q0:q0 + nq], ALU.mult, [nb_[w], HB], [ydob[w]])
                        S.dma('sp', lambda e, o=ydT_d[h * 128:(h + 1) * 128, q0:q0 + nq], s=ydo[w][:, 0:nq]: e.dma_start(out=o, in_=s),
                              ydsem[w], reads=[ydob[w]])

                    return [n1, n2, n3, n4, n5, n6]

                deferred = []
                if jobs:
                    qk(0)
                for ji, jb in enumerate(jobs):
                    w, q0, nq, pr, g, npairs = jb['w'], jb['q0'], jb['nq'], jb['pr'], jb['g'], jb['np']
                    full = (nq == 256)
                    pvi = 6 if w == 0 else 0
                    pv, pvb = banks[pvi], bankbuf[pvi][0]
                    nt = len(pr)
                    if ji + 1 < len(jobs):
                        qk(ji + 1)
                    pi = pctr[0] % NP
                    pctr[0] += 1
                    b0 = 2 + 2 * (ji % 2)
                    sbufs = [bankbuf[b0][0], bankbuf[b0 + 1][0], bndb]
                    same_nk = all(p_[1] == pr[0][1] for p_ in pr)
                    if full and same_nk:
                        nk = pr[0][1]
                        src = psum_all[0:nk, b0:b0 + 2, 0:nt * 256].rearrange("p m (t q) -> p m t q", t=nt)
                        dst = PT[pi][0:nk, 0:nt, :, :].rearrange("p t m q -> p m t q")
                        act(dst, src, AF.Exp, sbufs, [PTb[pi]], scale=0.125, bias=negc[0:nk, 0:1])
                    else:
                        for t, (kc0, nk, vt, r) in enumerate(pr):
                            act(PT[pi][0:nk, t, :, 0:nq], psum_all[0:nk, b0:b0 + 2, t * 256:t * 256 + nq], AF.Exp, sbufs,
                                [PTb[pi]], scale=0.125, bias=negc[0:nk, 0:1])
                    if pr[0][3] is not None:
                        for mi_, ap_ in enumerate((PT[pi][64:128, 0, :, 0:64], PT[pi][0:64, 1, :, 0:128], PT[pi][64:128, 1, :, 0:192])):
                            S.op('dve', lambda e, ap_=ap_: e.memset(ap_, 0.0), reads=[], writes=[PTb[pi]], signal=(mi_ == 2))
                    if same_nk:
                        nk = pr[0][1]
                        if g == 0:
                            cp('dve', racc[w][0:nk, 0:nt, :, 0:nq], PT[pi][0:nk, 0:nt, :, 0:nq], [PTb[pi]], [raccb[w]])
                        else:
                            tt('dve', racc[w][0:nk, 0:nt, :, 0:nq], racc[w][0:nk, 0:nt, :, 0:nq], PT[pi][0:nk, 0:nt, :, 0:nq],
                               ALU.add, [PTb[pi], raccb[w]], [raccb[w]])
                    else:
                        for t, (kc0, nk, vt, r) in enumerate(pr):
                            tt('dve', racc[w][0:nk, t, :, 0:nq], racc[w][0:nk, t, :, 0:nq], PT[pi][0:nk, t, :, 0:nq],
                               ALU.add, [PTb[pi], raccb[w]], [raccb[w]])
                    for t, (kc0, nk, vt, r) in enumerate(pr):
                        first = (g == 0 and t == 0)
                        last = (g == npairs - 1 and t == nt - 1)
                        if full:
                            mm(pv[:, 0:512], Vall[0:nk, vt, :], PT[pi][0:nk, t, :, :].rearrange("p m q -> p (m q)"), first, last,
                               [PTb[pi], HB], [pvb], t == nt - 1)
                        else:
                            for m in range(2):
                                mm(pv[:, m * 256:m * 256 + nq], Vall[0:nk, vt, :], PT[pi][0:nk, t, m, 0:nq], first and m == 0,
                                   last and m == 1, [PTb[pi], HB], [pvb], t == nt - 1 and m == 1)
                    if g == npairs - 1:
                        while deferred:
                            deferred.pop(0)()
                        deferred = norm_closures(jb)
                    elif deferred:
                        deferred.pop(0)()
                while deferred:
                    deferred.pop(0)()
            S.barrier()

        aoff[0] = mark_const
        with ExitStack() as ms:
          if stop is None or stop >= 4:
            wgs = sb("wgs", [128, 8, D], BF16, ms)
            wds = sb("wds", [128, 8, D], BF16, ms)
            wos = sb("wos", [128, 8, D], BF16, ms)
            wms = sb("wms", [128, 8, 2 * D], BF16, ms)
            WB = Buf()
            wrs = dsem()
            for q4 in range(4):
                c = slice(q4 * 256, (q4 + 1) * 256)
                for dst, src in ((wgs, wpg), (wds, wpd), (wos, wo)):
                    S.dma('pool', lambda e, o=dst[:, :, c], s=src[:, :, c]: e.dma_start(out=o, in_=s), wrs, writes=[WB])
            for q4 in range(8):
                S.dma('pool', lambda e, o=wms[:, :, q4 * 256:(q4 + 1) * 256], s=wl[:, :, C_MA + q4 * 256:C_MA + (q4 + 1) * 256]:
                      e.dma_start(out=o, in_=s), wrs, writes=[WB])
            xblk = [sb("xblk%d" % i, [128, 4, D], F32, ms) for i in range(2)]
            xblkb = [[Buf() for _ in range(4)] for _ in range(2)]
            xbs = [[dsem() for _ in range(4)] for _ in range(2)]
            xnTb = [sb("xnTb%d" % i, [128, 8, 512], BF16, ms) for i in range(2)]
            xnTbb = [Buf() for _ in range(2)]
            ygb = [sb("ygb%d" % i, [128, 8, 512], BF16, ms) for i in range(1)] * 2
            ydb = [sb("ydb%d" % i, [128, 8, 512], BF16, ms) for i in range(1)] * 2
            ygbb = [Buf()] * 2
            ydbb = [Buf()] * 2
            ygs_ = [dsem()] * 2
            yds_ = [dsem()] * 2
            mT = [sb("mT%d" % i, [128, 8, 512], BF16, ms) for i in range(1)] * 2
            mTb = [Buf()] * 2
            sga = [sb("sga%d" % i, [128, 512], F32, ms) for i in range(2)]
            sgb = [sb("sgb%d" % i, [128, 512], F32, ms) for i in range(2)]
            sgab = [Buf() for _ in range(2)]
            sgbb = [Buf() for _ in range(2)]
            ost = [sb("ost%d" % i, [128, D], F32, ms) for i in range(2)]
            ostb = [Buf() for _ in range(2)]
            osts = [dsem() for _ in range(2)]
            fst2 = sb("fst", [128, 16], F32, ms)
            fstb2 = [Buf(), Buf()]
            tctr = [0]

            def stage_load(bi):
                t0, n = BLOCKS[bi]
                p = bi % 2
                for tl in range((n + 127) // 128):
                    tn = min(128, n - tl * 128)
                    S.dma('sp', lambda e, o=xblk[p][0:tn, tl, :], s=xrows(t0 + tl * 128, tn): e.dma_start(out=o, in_=s),
                          xbs[p][tl], writes=[xblkb[p][tl]])
                    norm_tile(tctr[0], t0 + tl * 128, tn, xblk[p][:, tl, :], xblkb[p][tl], xnTb[p], xnTbb[p], tl * 128)
                    tctr[0] += 1

            stage_load(0)
            for bi, (t0, n) in enumerate(BLOCKS):
                p = bi % 2
                ntl = (n + 127) // 128
                S.dma('sp', lambda e, o=ygb[p][:, :, 0:n], s=ygT_d[:, t0:t0 + n].rearrange("(k p) t -> p k t", p=128):
                      e.dma_start(out=o, in_=s), ygs_[p], writes=[ygbb[p]])
                S.dma('sp', lambda e, o=ydb[p][:, :, 0:n], s=ydT_d[:, t0:t0 + n].rearrange("(k p) t -> p k t", p=128):
                      e.dma_start(out=o, in_=s), yds_[p], writes=[ydbb[p]])
                for ct in range(8):
                    k = ct % 2
                    cs_ = slice(ct * 128, (ct + 1) * 128)
                    for kc in range(8):
                        mm(banks[2][:, 0:n], wms[:, kc, cs_], xnTb[p][:, kc, 0:n], kc == 0, kc == 7, [WB, xnTbb[p]], [bankbuf[2][0]], kc == 7)
                    for kc in range(8):
                        mm(banks[3][:, 0:n], wms[:, kc, D + ct * 128:D + (ct + 1) * 128], xnTb[p][:, kc, 0:n], kc == 0, kc == 7,
                           [WB, xnTbb[p]], [bankbuf[3][0]], kc == 7)
                    for kc in range(8):
                        mm(banks[4][:, 0:n], wgs[:, kc, cs_], ygb[p][:, kc, 0:n], kc == 0, kc == 7, [WB, ygbb[p]], [bankbuf[4][0]], kc == 7)
                    for kc in range(8):
                        mm(banks[5][:, 0:n], wds[:, kc, cs_], ydb[p][:, kc, 0:n], kc == 0, kc == 7, [WB, ydbb[p]], [bankbuf[5][0]], kc == 7)
                    act(sga[k][:, 0:n], banks[2][:, 0:n], AF.Sigmoid, [bankbuf[2][0]], [sgab[k]])
                    act(sgb[k][:, 0:n], banks[3][:, 0:n], AF.Sigmoid, [bankbuf[3][0]], [sgbb[k]])
                    tt('dve', sga[k][:, 0:n], banks[4][:, 0:n], sga[k][:, 0:n], ALU.mult, [bankbuf[4][0], sgab[k]], [sgab[k]])
                    tt('dve', sgb[k][:, 0:n], banks[5][:, 0:n], sgb[k][:, 0:n], ALU.mult, [bankbuf[5][0], sgbb[k]], [sgbb[k]])
                    tt('dve', mT[p][:, ct, 0:n], sga[k][:, 0:n], sgb[k][:, 0:n], ALU.add, [sgab[k], sgbb[k]], [mTb[p]])
                    if ct == 7 and bi + 1 < len(BLOCKS):
                        stage_load(bi + 1)
                for tl in range(ntl):
                    tn = min(128, n - tl * 128)
                    k = tl % 2
                    fb = (6, 7) if tl % 2 == 0 else (0, 1)
                    fst = fst2[:, 8 * k:8 * k + 8]
                    fstb = fstb2[k]
                    for hf in range(2):
                        for ct in range(8):
                            mm(banks[fb[hf]][0:tn, :], mT[p][:, ct, tl * 128:tl * 128 + tn], wos[:, ct, hf * 512:(hf + 1) * 512],
                               ct == 0, ct == 7, [WB, mTb[p]], [bankbuf[fb[hf]][0]], ct == 7)
                    for hf in range(2):
                        act(junk[0:tn, hf * 512:(hf + 1) * 512], banks[fb[hf]][0:tn, :], AF.Square, [bankbuf[fb[hf]][0]], [junkb, fstb],
                            accum_out=fst[0:tn, hf:hf + 1])
                    tt('dve', fst[0:tn, 2:3], fst[0:tn, 0:1], fst[0:tn, 1:2], ALU.add, [fstb], [fstb])
                    act(fst[0:tn, 3:4], fst[0:tn, 2:3], AF.Ln, [fstb], [fstb], scale=1.0 / D, bias=EPS)
                    act(fst[0:tn, 4:5], fst[0:tn, 3:4], AF.Exp, [fstb], [fstb], scale=-0.5)
                    for hf in range(2):
                        hs = slice(hf * 512, (hf + 1) * 512)
                        stt('dve', ost[k][0:tn, hs], banks[fb[hf]][0:tn, :], fst[0:tn, 4:5], gpost_bc[0:tn, hs], ALU.mult, ALU.mult,
                            [bankbuf[fb[hf]][0], fstb, CB], [ostb[k]])
                        tt('dve', ost[k][0:tn, hs], ost[k][0:tn, hs], xblk[p][0:tn, tl, hs], ALU.add, [ostb[k], xblkb[p][tl]], [ostb[k]])
                    r0 = t0 + tl * 128
                    ydst = yp[r0:r0 + tn, :] if r0 < T else ys[0:tn, :]
                    S.dma('sp', lambda e, o=ydst, s=ost[k][0:tn, :]: e.dma_start(out=o, in_=s), osts[k], reads=[ostb[k]])
            S.barrier()

        with nc.Block() as block:
            S.emit(block)
    return nc


_NC = [None]


def _rope_table():
    half = 8
    inv = 500000.0 ** (-np.arange(0, 16, 2, dtype=np.float32) / np.float32(16))
    pos = np.concatenate([np.arange(T), PAST + np.arange(TS)]).astype(np.float32)
    ang = (pos[:, None] * inv[None, :].astype(np.float32)).astype(np.float32)
    c = np.cos(ang).astype(np.float32)
    s = np.sin(ang).astype(np.float32)
    return np.ascontiguousarray(np.concatenate([c, c, -s, s], axis=1).astype(np.float32))


def _lay(w):
    return np.ascontiguousarray(w.reshape(8, 128, w.shape[1]).transpose(1, 0, 2))


def kernel(x_prompt, x_sample, cache_diff_k, cache_diff_v, state_gla, pre_norm_g, w_in, gla_w_a2, gla_b_a, gla_norm_g,
           diff_lambda_q1, diff_lambda_k1, diff_lambda_q2, diff_lambda_k2, diff_subln_g, w_proj_gla, w_proj_diff, w_out,
           post_norm_g):
    f = lambda a: np.ascontiguousarray(np.asarray(a, dtype=np.float32))
    if _NC[0] is None:
        _NC[0] = build()
    nc = _NC[0]
    shared = {
        "wl": _lay(f(w_in)[0]), "wa2": f(gla_w_a2)[0], "ba": np.ascontiguousarray(f(gla_b_a)[0].reshape(4, 128).T),
        "g_pre": f(pre_norm_g), "g_post": f(post_norm_g),
        "g_gla": np.ascontiguousarray(f(gla_norm_g)[0].reshape(2, 128).T), "g_sub": f(diff_subln_g)[0].reshape(128, 1),
        "lam4": np.concatenate([f(diff_lambda_q1), f(diff_lambda_k1), f(diff_lambda_q2), f(diff_lambda_k2)], axis=1),
        "wpg": _lay(f(w_proj_gla)[0]), "wpd": _lay(f(w_proj_diff)[0]), "wo": _lay(f(w_out)[0]), "rope": _rope_table(),
    }
    xpf, xsf, ckf, cvf, sgf = f(x_prompt), f(x_sample), f(cache_diff_k), f(cache_diff_v), f(state_gla)
    in_maps = []
    for i in range(8):
        m = dict(shared)
        m.update({"xp": xpf[i], "xs": xsf[i], "ck": ckf[0, i].reshape(PAST, D), "cv": cvf[0, i].reshape(PAST, D), "sg": sgf[0, i]})
        in_maps.append(m)
    res = run_bass_kernel_spmd(nc, in_maps, core_ids=list(range(8)))
    R = res.results
    st = lambda k: np.stack([np.asarray(R[i][k], dtype=np.float32) for i in range(8)])
    return (st("yp"), st("ys"), st("nkp").reshape(1, 8, T, 8, 128), st("nvp").reshape(1, 8, T, 8, 128),
            st("ngp").reshape(1, 8, 4, 128, 256), st("nks").reshape(1, 8, TS, 8, 128), st("nvs").reshape(1, 8, TS, 8, 128),
            st("ngs").reshape(1, 8, 4, 128, 256))
```

```python
import math
from contextlib import ExitStack
import numpy as np
import concourse.bass as bass
import concourse.mybir as mybir
from concourse.bass_utils import run_bass_kernel_spmd

F32 = mybir.dt.float32
BF16 = mybir.dt.bfloat16
AF = mybir.ActivationFunctionType
ALU = mybir.AluOpType
AX = mybir.AxisListType

T = 4096
TS = 32
TT = T + TS
PAST = 2048
D = 1024
C_GQ, C_GK, C_GV, C_GR, C_GG = 0, 512, 1024, 2048, 2064
C_DQ, C_DK, C_DV, C_DG, C_MA, C_MB = 3088, 4112, 5136, 6160, 7184, 8208
IN_TOTAL = 9232
EPS = 1e-6
TILES = [(i * 128, 128) for i in range(32)] + [(T, TS)]
BLOCKS = [(i * 512, 512) for i in range(8)] + [(T, TS)]
NQB = 16
DBG = {'nh': 8, 'stage': 9, 'nqb': 16, 'skip': '', 'qb0': 0}


class Buf:
    __slots__ = ('w', 'r', 'excl')

    def __init__(self, excl=False):
        self.w = None
        self.r = {}
        self.excl = excl


class Sched:
    def __init__(self, nc, esems):
        self.nc = nc
        self.names = ['pe', 'act', 'dve', 'pool', 'sp']
        self.q = {k: [] for k in self.names}
        self.esem = esems
        self.cnt = {k: 0 for k in self.names}
        self.seen = {k: {} for k in self.names}
        self.pend = {k: ([], []) for k in self.names}
        self.last = {}

    def _waits(self, eng, toks):
        ws = []
        for t in toks:
            if t is None:
                continue
            if eng == 'pe' and t[2] == 'e_pe':
                continue
            if t[2] == 'e_' + eng and t[1] <= self.cnt[eng] - 3:
                continue
            if self.seen[eng].get(t[2], 0) < t[1]:
                self.seen[eng][t[2]] = t[1]
                ws.append(t)
        return ws

    def _deps(self, reads, writes):
        toks = []
        for b in reads:
            toks.append(b.w)
            if b.excl:
                toks.extend(b.r.values())
        for b in writes:
            toks.append(b.w)
            toks.extend(b.r.values())
        return toks

    def _commit(self, eng, tok, reads, writes):
        pr, pw = self.pend[eng]
        pr.extend(reads)
        pw.extend(writes)
        if tok is not None:
            for b in pr:
                if b.excl:
                    b.w = tok
                    b.r = {}
                else:
                    b.r[tok[2]] = tok
            for b in pw:
                b.w = tok
                b.r = {}
            self.pend[eng] = ([], [])
            self.last[tok[2]] = tok

    def op(self, eng, fn, reads=(), writes=(), signal=True, extra=()):
        ws = self._waits(eng, self._deps(reads, writes) + list(extra))
        tok = None
        if signal:
            self.cnt[eng] += 1
            tok = (self.esem[eng], self.cnt[eng], 'e_' + eng)
        self.q[eng].append((ws, fn, tok, 1))
        self._commit(eng, tok, reads, writes)
        return tok

    def dma(self, eng, fn, dsem, reads=(), writes=(), extra=()):
        ws = self._waits(eng, self._deps(reads, writes) + list(extra))
        dsem[1] += 16
        tok = (dsem[0], dsem[1], dsem[2])
        self.q[eng].append((ws, fn, tok, 16))
        for b in reads:
            b.r[tok[2]] = tok
        for b in writes:
            b.w = tok
            b.r = {}
        self.last[tok[2]] = tok
        return tok

    def barrier(self):
        toks = list(self.last.values())
        for e in self.names:
            ws = self._waits(e, toks)
            if ws:
                self.q[e].append((ws, None, None, 0))

    def emit(self, block):
        def mk(name):
            def body(e):
                for ws, fn, tok, inc in self.q[name]:
                    for w in ws:
                        e.wait_ge(w[0], w[1])
                    if fn is None:
                        continue
                    ins = fn(e)
                    if tok is not None:
                        ins.then_inc(tok[0], inc)
            return body
        block.tensor(mk('pe'))
        block.scalar(mk('act'))
        block.vector(mk('dve'))
        block.gpsimd(mk('pool'))
        block.sync(mk('sp'))


def build(stop=None):
    nc = bass.Bass("TRN2", target_bir_lowering=False)

    def din(name, shape, dt=F32):
        return nc.dram_tensor(name, shape, dt, kind="ExternalInput").ap()

    def dout(name, shape, dt=F32):
        return nc.dram_tensor(name, shape, dt, kind="ExternalOutput").ap()

    xp = din("xp", [T, D])
    xs = din("xs", [TS, D])
    ck = din("ck", [PAST, D])
    cv = din("cv", [PAST, D])
    sg = din("sg", [4, 128, 256])
    wl = din("wl", [128, 8, IN_TOTAL])
    wa2 = din("wa2", [16, 512])
    ba = din("ba", [128, 4])
    g_pre = din("g_pre", [1, D])
    g_post = din("g_post", [1, D])
    g_gla = din("g_gla", [128, 2])
    g_sub = din("g_sub", [128, 1])
    lam4 = din("lam4", [1, 256])
    wpg = din("wpg", [128, 8, D])
    wpd = din("wpd", [128, 8, D])
    wo = din("wo", [128, 8, D])
    rope = din("rope", [TT, 32])
    yp = dout("yp", [T, D])
    ys = dout("ys", [TS, D])
    nkp = dout("nkp", [T, D])
    nvp = dout("nvp", [T, D])
    ngp = dout("ngp", [4, 128, 256])
    nks = dout("nks", [TS, D])
    nvs = dout("nvs", [TS, D])
    ngs = dout("ngs", [4, 128, 256])
    if stop is None:
        ygT_d = nc.dram_tensor("ygT_scr", [D, TT], BF16).ap()
        ydT_d = nc.dram_tensor("ydT_scr", [D, TT], BF16).ap()
    else:
        ygT_d = dout("ygT_scr", [D, TT], BF16)
        ydT_d = dout("ydT_scr", [D, TT], BF16)
        dbg_xnT = dout("dbg_xnT", [128, 8, TT], BF16)

    def xrows(t0, n):
        return xp[t0:t0 + n, :] if t0 < T else xs[0:n, :]

    with ExitStack() as es:
        ARENA = 106200
        arena = es.enter_context(nc.sbuf_tensor("arena", [128, ARENA], BF16))
        aoff = [0]

        def sb(name, shape, dt, stack=None):
            n = 1
            for d_ in shape[1:]:
                n *= d_
            nb = n * (4 if dt == F32 else 2)
            nb = (nb + 31) // 32 * 32
            o = aoff[0]
            aoff[0] += nb // 2
            assert aoff[0] <= ARENA, (name, aoff[0])
            ap = arena[:, o:o + nb // 2]
            if dt == F32:
                ap = ap.bitcast(F32)
            ap = ap[:, 0:n]
            if len(shape) == 3:
                ap = ap.rearrange("p (a b) -> p a b", a=shape[1])
            elif len(shape) == 4:
                ap = ap.rearrange("p (a b c) -> p a b c", a=shape[1], b=shape[2])
            return ap

        def sem(name):
            return es.enter_context(nc.semaphore(name))

        nds = [0]

        def dsem():
            nds[0] += 1
            return [sem("ds%d" % nds[0]), 0, "ds%d" % nds[0]]

        esems = {k: sem("es_" + k) for k in ['pe', 'act', 'dve', 'pool', 'sp']}
        S = Sched(nc, esems)
        psum_all = es.enter_context(nc.psum_tensor("psum_all", [128, 8, 512], F32))
        banks = [psum_all[:, i, :] for i in range(8)]
        BK = [Buf(excl=True) for _ in range(8)]
        bankbuf = [[BK[i], BK[i]] for i in range(8)]

        ident = sb("ident", [128, 128], BF16)
        identf = sb("identf", [128, 128], F32)
        ones_bf = sb("ones_bf", [128, 128], BF16)
        ones_f = sb("ones_f", [128, 128], F32)
        trimask = sb("trimask", [128, 128], BF16)
        dmask = sb("dmask", [128, 2, 2, 256], BF16)
        g_bc = sb("g_bc", [128, D], F32)
        gpost_bc = sb("gpost_bc", [128, D], F32)
        ba_t = sb("ba_t", [128, 4], F32)
        nba_t = sb("nba_t", [128, 4], F32)
        wa2_bf = sb("wa2_bf", [128, 512], BF16)
        ggla_t = sb("ggla_t", [128, 2], F32)
        gsub_t = sb("gsub_t", [128, 1], F32)
        gsub08 = sb("gsub08", [128, 1], F32)
        lam_t = sb("lam_t", [128, 256], F32)
        lam_p = sb("lam_p", [128, 128], F32)
        lam_s = sb("lam_s", [128, 2], F32)
        lam_e = sb("lam_e", [128, 2], F32)
        neg_lam = sb("neg_lam", [128, 1], F32)
        rope_t = sb("rope_t", [128, 33, 32], F32)
        junk = sb("junk", [128, D], BF16)
        junkb = Buf()
        xn_tok = [sb("xn_tok%d" % i, [128, D], BF16) for i in range(2)]
        xnb = [Buf() for _ in range(2)]
        nstat2 = sb("nstat", [128, 8], F32)
        nstb2 = [Buf(), Buf()]
        mark_const = aoff[0]
        xnT = sb("xnT", [128, 8, TT], BF16)
        CB = Buf()
        XNT = Buf()

        d_c = dsem()
        cons = [
            ('sp', g_bc[:], g_pre.partition_broadcast(128)),
            ('sp', gpost_bc[:], g_post.partition_broadcast(128)),
            ('sp', ba_t[:], ba),
            ('sp', ggla_t[:], g_gla),
            ('sp', gsub_t[:], g_sub),
            ('sp', lam_t[:], lam4.partition_broadcast(128)),
            ('sp', rope_t[:, 0:32, :], rope[0:T, :].rearrange("(n p) c -> p n c", p=128)),
            ('sp', rope_t[0:TS, 32, :], rope[T:TT, :]),
            ('pool', wa2_bf[0:16, :], wa2),
        ]
        d_c2 = dsem()
        for eng, o, i in cons:
            S.dma(eng, lambda e, o=o, i=i: e.dma_start(out=o, in_=i), d_c if eng == 'sp' else d_c2, writes=[CB])
        P = lambda fn, **kw: S.op('pool', fn, **kw)
        P(lambda e: e.memset(ones_bf[:], 1.0), writes=[CB])
        P(lambda e: e.memset(ones_f[:], 1.0), writes=[CB])
        P(lambda e: e.affine_select(out=ident[:], in_=ones_bf[:], pattern=[[-1, 128]], compare_op=ALU.is_equal,
                                    fill=0.0, base=0, channel_multiplier=1), reads=[CB], writes=[CB])
        P(lambda e: e.affine_select(out=identf[:], in_=ones_f[:], pattern=[[-1, 128]], compare_op=ALU.is_equal,
                                    fill=0.0, base=0, channel_multiplier=1), reads=[CB], writes=[CB])
        P(lambda e: e.affine_select(out=trimask[:], in_=ones_bf[:], pattern=[[1, 128]], compare_op=ALU.is_ge,
                                    fill=0.0, base=0, channel_multiplier=-1), reads=[CB], writes=[CB])
        P(lambda e: e.memset(dmask[:], 1.0), writes=[CB])
        P(lambda e: e.memset(dmask[64:128, 0, :, 0:64], 0.0), reads=[CB], writes=[CB])
        P(lambda e: e.memset(dmask[0:64, 1, :, 0:128], 0.0), reads=[CB], writes=[CB])
        P(lambda e: e.memset(dmask[64:128, 1, :, 0:192], 0.0), reads=[CB], writes=[CB])
        V = lambda fn, **kw: S.op('dve', fn, **kw)
        A = lambda fn, **kw: S.op('act', fn, **kw)
        V(lambda e: e.tensor_scalar(out=nba_t[:], in0=ba_t[:], scalar1=-1.0, scalar2=None, op0=ALU.mult), reads=[CB], writes=[CB])
        V(lambda e: e.tensor_scalar(out=gsub08[:], in0=gsub_t[:], scalar1=0.8, scalar2=None, op0=ALU.mult), reads=[CB], writes=[CB])
        lam_v = lam_t[:].rearrange("p (a b d) -> p a b d", a=2, b=2)
        V(lambda e: e.tensor_tensor(out=lam_p[:].rearrange("p (a d) -> p a d", a=2), in0=lam_v[:, :, 0, :], in1=lam_v[:, :, 1, :],
                                    op=ALU.mult), reads=[CB], writes=[CB])
        V(lambda e: e.tensor_reduce(out=lam_s[:], in_=lam_p[:].rearrange("p (a d) -> p a d", a=2), axis=AX.X, op=ALU.add),
          reads=[CB], writes=[CB])
        A(lambda e: e.activation(out=lam_e[:], in_=lam_s[:], func=AF.Exp), reads=[CB], writes=[CB])
        V(lambda e: e.scalar_tensor_tensor(out=neg_lam[:], in0=lam_e[:, 1:2], scalar=-0.2, in1=lam_e[:, 0:1],
                                           op0=ALU.add, op1=ALU.subtract), reads=[CB], writes=[CB])
        S.barrier()

        def mm(out, lhsT, rhs, start, stop, reads, writes, last):
            return S.op('pe', lambda e: e.matmul(out, lhsT=lhsT, rhs=rhs, start=start, stop=stop),
                        reads=reads, writes=writes, signal=last)

        def tr(out, in_, idn, reads, writes, last=True):
            return S.op('pe', lambda e: e.transpose(out=out, in_=in_, identity=idn), reads=reads, writes=writes, signal=last)

        def act(out, in_, func, reads, writes, **kw):
            return S.op('act', lambda e: e.activation(out=out, in_=in_, func=func, **kw), reads=reads, writes=writes)

        def tt(eng, out, in0, in1, op, reads, writes):
            return S.op(eng, lambda e: e.tensor_tensor(out=out, in0=in0, in1=in1, op=op), reads=reads, writes=writes)

        def ts(eng, out, in0, s1, op0, reads, writes, s2=None, op1=None):
            if op1 is None:
                return S.op(eng, lambda e: e.tensor_scalar(out=out, in0=in0, scalar1=s1, scalar2=None, op0=op0),
                            reads=reads, writes=writes)
            return S.op(eng, lambda e: e.tensor_scalar(out=out, in0=in0, scalar1=s1, scalar2=s2, op0=op0, op1=op1),
                        reads=reads, writes=writes)

        def stt(eng, out, in0, scalar, in1, op0, op1, reads, writes):
            return S.op(eng, lambda e: e.scalar_tensor_tensor(out=out, in0=in0, scalar=scalar, in1=in1, op0=op0, op1=op1),
                        reads=reads, writes=writes)

        def cp(eng, out, in_, reads, writes):
            if eng == 'act':
                return S.op('act', lambda e: e.copy(out=out, in_=in_), reads=reads, writes=writes)
            return S.op(eng, lambda e: e.tensor_copy(out=out, in_=in_), reads=reads, writes=writes)

        NW = 3
        wslot = [sb("wslot%d" % i, [128, 8, 384], BF16) for i in range(NW)]
        wbuf = [Buf() for _ in range(NW)]
        wsem = [dsem() for _ in range(NW)]
        wctr = [0]

        def load_w(cols):
            i = wctr[0] % NW
            wctr[0] += 1
            off = 0
            for c0, n in cols:
                S.dma('pool', lambda e, o=wslot[i][:, :, off:off + n], s=wl[:, :, c0:c0 + n]: e.dma_start(out=o, in_=s),
                      wsem[i], writes=[wbuf[i]])
                off += n
            return wslot[i], wbuf[i]

        prot = [0]

        def next_bank(lo, hi):
            b = lo + prot[0] % (hi - lo)
            prot[0] += 1
            return b

        mark_mid = aoff[0]
        xin = [sb("xin%d" % i, [128, D], F32) for i in range(2)]
        xinb = [Buf() for _ in range(2)]
        xsem = [dsem() for _ in range(2)]
        def norm_a(i, n, xtile, xbuf):
            k = i % 2
            nstat, nstb = nstat2[:, 4 * k:4 * k + 4], nstb2[k]
            act(junk[0:n, :], xtile[0:n, :], AF.Square, [xbuf], [junkb, nstb], accum_out=nstat[0:n, 0:1])
            act(nstat[0:n, 1:2], nstat[0:n, 0:1], AF.Ln, [nstb], [nstb], scale=1.0 / D, bias=EPS)
            act(nstat[0:n, 2:3], nstat[0:n, 1:2], AF.Exp, [nstb], [nstb], scale=-0.5)

        def norm_b(i, n, xtile, xbuf):
            k = i % 2
            nstat, nstb = nstat2[:, 4 * k:4 * k + 4], nstb2[k]
            stt('dve', xn_tok[k][0:n, :], xtile[0:n, :], nstat[0:n, 2:3], g_bc[0:n, :], ALU.mult, ALU.mult,
                [xbuf, nstb], [xnb[k]])

        def norm_c(i, n, dstT, dstbuf, dcol):
            k = i % 2
            bk = next_bank(0, 2)
            pT = banks[bk][:].bitcast(BF16).rearrange("p (k t) -> p k t", k=8)
            for kc in range(8):
                tr(pT[:, kc, 0:n], xn_tok[k][0:n, kc * 128:(kc + 1) * 128], ident[0:n, 0:n], [xnb[k]], [bankbuf[bk][0]],
                   last=(kc == 7))
            cp('dve', dstT[:, :, dcol:dcol + n], pT[:, :, 0:n], [bankbuf[bk][0]], [dstbuf])

        def norm_tile(i, t0, n, xtile, xbuf, dstT, dstbuf, dcol):
            norm_a(i, n, xtile, xbuf)
            norm_b(i, n, xtile, xbuf)
            norm_c(i, n, dstT, dstbuf, dcol)

        NTL0 = len(TILES)
        for st_ in range(NTL0 + 2):
            if st_ < NTL0:
                t0, n = TILES[st_]
                k = st_ % 2
                S.dma('sp', lambda e, o=xin[k][0:n, :], s=xrows(t0, n): e.dma_start(out=o, in_=s), xsem[k], writes=[xinb[k]])
                norm_a(st_, n, xin[k], xinb[k])
            if 0 <= st_ - 1 < NTL0:
                t0, n = TILES[st_ - 1]
                norm_b(st_ - 1, n, xin[(st_ - 1) % 2], xinb[(st_ - 1) % 2])
            if 0 <= st_ - 2 < NTL0:
                t0, n = TILES[st_ - 2]
                norm_c(st_ - 2, n, xnT, XNT, t0)
        S.barrier()
        if stop is not None:
            dd = dsem()
            S.dma('sp', lambda e: e.dma_start(out=dbg_xnT, in_=xnT), dd, reads=[XNT])
            S.barrier()

        aoff[0] = mark_mid
        with ExitStack() as gs:
          if stop is None or stop >= 2:
            grT = sb("grT", [128, TT], BF16, gs)
            GRT = Buf()
            qdT = sb("qdT", [128, TT], BF16, gs)
            kdT = sb("kdT", [128, TT], BF16, gs)
            vtok = sb("vtok", [128, 33, 256], BF16, gs)
            gsil = sb("gsil", [128, 2, TT], BF16, gs)
            ncl = sb("ncl", [128, 33], F32, gs)
            ebl = sb("ebl", [128, 33], F32, gs)
            HB = Buf()
            NT = 2
            tmpA = [sb("gtA%d" % i, [128, 512], F32, gs) for i in range(NT)]
            tmpB = [sb("gtB%d" % i, [128, 512], F32, gs) for i in range(NT)]
            tmpC = [sb("gtC%d" % i, [128, 512], F32, gs) for i in range(NT)]
            tmpD = [sb("gtD%d" % i, [128, 512], F32, gs) for i in range(NT)]
            tAb = [Buf() for _ in range(NT)]
            tBb = [Buf() for _ in range(NT)]
            tCb = [Buf() for _ in range(NT)]
            tDb = [Buf() for _ in range(NT)]
            Sst = [sb("Sst%d" % i, [128, 256], F32, gs) for i in range(2)]
            Sbf = [sb("Sbf%d" % i, [128, 256], BF16, gs) for i in range(2)]
            Sb = [Buf() for _ in range(2)]
            Sbb = [Buf() for _ in range(2)]
            Am = [sb("Am%d" % i, [128, 128], BF16, gs) for i in range(2)]
            Amb = [Buf() for _ in range(2)]
            kd2 = [sb("kd2_%d" % i, [128, 128], BF16, gs) for i in range(2)]
            kd2b = [Buf() for _ in range(2)]
            Uc = [sb("Uc%d" % i, [128, 256], F32, gs) for i in range(2)]
            Ucb = [Buf() for _ in range(2)]
            sqg = [sb("sqg%d" % i, [128, 2, 512], BF16, gs) for i in range(2)]
            sqgb = [Buf() for _ in range(2)]
            rsg = [sb("rsg%d" % i, [128, 512], F32, gs) for i in range(2)]
            rsgb = [Buf() for _ in range(2)]
            ygo = [sb("ygo%d" % i, [128, 2, 512], BF16, gs) for i in range(2)]
            ygob = [Buf() for _ in range(2)]
            ygsem = [dsem() for _ in range(2)]
            ssem = dsem()
            sosem = [dsem() for _ in range(2)]

            wt, wb_ = load_w([(C_GR, 16)])
            for (t0, n) in BLOCKS:
                bk = next_bank(0, 4)
                for kc in range(8):
                    mm(banks[bk][0:16, 0:n], wt[:, kc, 0:16], xnT[:, kc, t0:t0 + n], kc == 0, kc == 7,
                       [wb_, XNT], [bankbuf[bk][0]], kc == 7)
                cp('act', grT[0:16, t0:t0 + n], banks[bk][0:16, 0:n], [bankbuf[bk][0]], [GRT])

            for h in range(4):
                if h == 0:
                    gpre = (load_w([(C_GQ + h * 128, 128), (C_GK + h * 128, 128)]), load_w([(C_GG + h * 256, 256)]),
                            load_w([(C_GV + h * 256, 256)]))
                (wq, wqb), (wg_, wgb), (wv, wvb) = gpre
                for bi, (t0, n) in enumerate(BLOCKS):
                    k = bi % NT
                    bk = next_bank(0, 4)
                    mm(banks[bk][:, 0:n], wa2_bf[0:16, h * 128:(h + 1) * 128], grT[0:16, t0:t0 + n], True, True,
                       [GRT, CB], [bankbuf[bk][0]], True)
                    act(tmpA[k][:, 0:n], banks[bk][:, 0:n], AF.Exp, [bankbuf[bk][0]], [tAb[k]], scale=-1.0, bias=nba_t[:, h:h + 1])
                    act(tmpA[k][:, 0:n], tmpA[k][:, 0:n], AF.Ln, [tAb[k]], [tAb[k]], bias=1.0)
                    for c0 in range(0, n, 128):
                        cn = min(128, n - c0)
                        ci = (t0 + c0) // 128
                        S.op('dve', lambda e, o=tmpB[k][:, c0:c0 + cn], d1=tmpA[k][:, c0:c0 + cn], d0=ones_bf[:, 0:cn]:
                             e.tensor_tensor_scan(out=o, data0=d0, data1=d1, initial=0.0, op0=ALU.mult, op1=ALU.add),
                             reads=[tAb[k]], writes=[tBb[k]])
                        ts('dve', ncl[:, ci:ci + 1], tmpB[k][:, c0 + cn - 1:c0 + cn], -1.0 / 16, ALU.mult, [tBb[k]], [HB])
                    act(tmpC[k][:, 0:n], tmpB[k][:, 0:n], AF.Exp, [tBb[k]], [tCb[k]], scale=-1.0 / 16)
                    act(tmpD[k][:, 0:n], tmpB[k][:, 0:n], AF.Exp, [tBb[k]], [tDb[k]], scale=1.0 / 16)
                    bk = next_bank(0, 4)
                    for kc in range(8):
                        mm(banks[bk][:, 0:n], wq[:, kc, 0:128], xnT[:, kc, t0:t0 + n], kc == 0, kc == 7, [wqb, XNT],
                           [bankbuf[bk][0]], kc == 7)
                    stt('dve', qdT[:, t0:t0 + n], banks[bk][:, 0:n], 128 ** -0.5, tmpC[k][:, 0:n], ALU.mult, ALU.mult,
                        [bankbuf[bk][0], tCb[k]], [HB])
                    bk = next_bank(0, 4)
                    for kc in range(8):
                        mm(banks[bk][:, 0:n], wq[:, kc, 128:256], xnT[:, kc, t0:t0 + n], kc == 0, kc == 7, [wqb, XNT],
                           [bankbuf[bk][0]], kc == 7)
                    tt('dve', kdT[:, t0:t0 + n], banks[bk][:, 0:n], tmpD[k][:, 0:n], ALU.mult, [bankbuf[bk][0], tDb[k]], [HB])
                for bi, (t0, n) in enumerate(BLOCKS):
                    k = bi % NT
                    for et in range(2):
                        bk = next_bank(0, 4)
                        for kc in range(8):
                            mm(banks[bk][:, 0:n], wg_[:, kc, et * 128:(et + 1) * 128], xnT[:, kc, t0:t0 + n], kc == 0, kc == 7,
                               [wgb, XNT], [bankbuf[bk][0]], kc == 7)
                        tgt = tmpC if et == 0 else tmpD
                        tgb = tCb if et == 0 else tDb
                        act(tgt[k][:, 0:n], banks[bk][:, 0:n], AF.Silu, [bankbuf[bk][0]], [tgb[k]])
                        ts('dve', gsil[:, et, t0:t0 + n], tgt[k][:, 0:n], ggla_t[:, et:et + 1], ALU.mult, [tgb[k]], [HB])
                for i, (t0, n) in enumerate(TILES):
                    bk = next_bank(0, 4)
                    for kc in range(8):
                        mm(banks[bk][0:n, 0:256], xnT[:, kc, t0:t0 + n], wv[:, kc, 0:256], kc == 0, kc == 7, [wvb, XNT],
                           [bankbuf[bk][0]], kc == 7)
                    cp('act', vtok[0:n, i, :], banks[bk][0:n, 0:256], [bankbuf[bk][0]], [HB])
                act(ebl[:, :], ncl[:, :], AF.Exp, [HB], [HB])
                if h + 1 < 4:
                    h1 = h + 1
                    gpre = (load_w([(C_GQ + h1 * 128, 128), (C_GK + h1 * 128, 128)]), load_w([(C_GG + h1 * 256, 256)]),
                            load_w([(C_GV + h1 * 256, 256)]))

                def pre(ci, t0, n):
                    a = ci % 2
                    mm(banks[2][0:n, 0:n], kdT[:, t0:t0 + n], qdT[:, t0:t0 + n], True, True, [HB],
                       [bankbuf[2][0]], True)
                    tt('dve', Am[a][0:n, 0:n], banks[2][0:n, 0:n], trimask[0:n, 0:n], ALU.mult,
                       [bankbuf[2][0], CB], [Amb[a]])
                    pT = banks[4][:].bitcast(BF16)
                    tr(pT[0:n, 0:128], kdT[:, t0:t0 + n], ident[:, :], [HB], [bankbuf[4][0]])
                    cp('act', kd2[a][0:n, :], pT[0:n, 0:128], [bankbuf[4][0]], [kd2b[a]])
                    mm(banks[6][:, 0:256], kd2[a][0:n, :], vtok[0:n, ci, :], True, True, [kd2b[a], HB],
                       [bankbuf[6][0]], True)
                    act(Uc[a][:, :], banks[6][:, 0:256], AF.Copy, [bankbuf[6][0], HB], [Ucb[a]], scale=ebl[:, ci:ci + 1])

                def dep(ci, t0, n, si, pO, pOb, col):
                    a = ci % 2
                    for et in range(2):
                        mm(pO[et][:, col:col + n], Sbf[si][:, et * 128:(et + 1) * 128], qdT[:, t0:t0 + n], True, False,
                           [Sbb[si], HB], [pOb[et]], False)
                        mm(pO[et][:, col:col + n], vtok[0:n, ci, et * 128:(et + 1) * 128], Am[a][0:n, 0:n], False, True,
                           [Amb[a], HB], [pOb[et]], True)
                    stt('dve', Sst[1 - si][:, :], Sst[si][:, :], ebl[:, ci:ci + 1], Uc[a][:, :],
                        ALU.mult, ALU.add, [Sb[si], HB, Ucb[a]], [Sb[1 - si]])
                    cp('act', Sbf[1 - si][:, :], Sst[1 - si][:, :], [Sb[1 - si]], [Sbb[1 - si]])

                def finish_block(bi, t0, n, pO, pOb):
                    k = bi % 2
                    for et in range(2):
                        act(sqg[k][:, et, 0:n], pO[et][:, 0:n], AF.Square, [pOb[et]], [sqgb[k]])
                    for et in range(2):
                        mm(banks[7][:, 0:n], ones_bf[:, :], sqg[k][:, et, 0:n], et == 0, et == 1, [sqgb[k], CB],
                           [bankbuf[7][0]], et == 1)
                    act(rsg[k][:, 0:n], banks[7][:, 0:n], AF.Ln, [bankbuf[7][0]], [rsgb[k]], scale=1.0 / 256, bias=EPS)
                    act(rsg[k][:, 0:n], rsg[k][:, 0:n], AF.Exp, [rsgb[k]], [rsgb[k]], scale=-0.5)
                    for et in range(2):
                        tt('dve', tmpB[et][:, 0:n], pO[et][:, 0:n], rsg[k][:, 0:n], ALU.mult, [pOb[et], rsgb[k]], [tBb[et]])
                        tt('dve', ygo[k][:, et, 0:n], tmpB[et][:, 0:n], gsil[:, et, t0:t0 + n], ALU.mult, [tBb[et], HB], [ygob[k]])
                    dst = ygT_d[h * 256:(h + 1) * 256, t0:t0 + n].rearrange("(a p) t -> p a t", p=128)
                    S.dma('sp', lambda e, o=dst, s=ygo[k][:, :, 0:n]: e.dma_start(out=o, in_=s), ygsem[k], reads=[ygob[k]])

                S.op('pool', lambda e: e.memset(Sst[0][:, :], 0.0), writes=[Sb[0]])
                S.op('pool', lambda e: e.memset(Sbf[0][:, :], 0.0), writes=[Sbb[0]])
                si = 0
                pOs = ([banks[0], banks[1]], [banks[3], banks[5]])
                pObs = ([bankbuf[0][0], bankbuf[1][0]], [bankbuf[3][0], bankbuf[5][0]])
                chunks = [(ci, ci * 128, 128) for ci in range(32)] + [(32, T, TS)]
                pre(*chunks[0])
                for idx, (ci, t0, n) in enumerate(chunks):
                    pO, pOb = pOs[(ci // 4) % 2], pObs[(ci // 4) % 2]
                    if idx + 1 < len(chunks):
                        pre(*chunks[idx + 1])
                    if ci == 32:
                        S.dma('sp', lambda e, o=ngp[h], s=Sst[si][:, :]: e.dma_start(out=o, in_=s), sosem[0], reads=[Sb[si]])
                        si = 1 - si
                        S.dma('sp', lambda e, o=Sst[si][:, :], s=sg[h]: e.dma_start(out=o, in_=s), ssem, writes=[Sb[si]])
                        cp('act', Sbf[si][:, :], Sst[si][:, :], [Sb[si]], [Sbb[si]])
                        dep(ci, t0, n, si, pO, pOb, 0)
                        si = 1 - si
                        finish_block(8, T, TS, pO, pOb)
                        S.dma('sp', lambda e, o=ngs[h], s=Sst[si][:, :]: e.dma_start(out=o, in_=s), sosem[1], reads=[Sb[si]])
                    else:
                        dep(ci, t0, n, si, pO, pOb, (ci % 4) * 128)
                        si = 1 - si
                        if ci % 4 == 3:
                            finish_block(ci // 4, (ci // 4) * 512, 512, pO, pOb)
            S.barrier()

        aoff[0] = mark_mid
        with ExitStack() as ds:
          if stop is None or stop >= 3:
            QT = sb("QT", [128, TT], BF16, ds)
            KT = sb("KT", [128, TT + PAST], BF16, ds)
            Vall = sb("Vall", [128, 49, 128], BF16, ds)
            kcb = sb("kcb", [128, 16, 128], BF16, ds)
            dgs = sb("dgs", [128, TT], BF16, ds)
            HB = Buf()
            KCB = Buf()
            kcsem = dsem()
            vcsem = dsem()
            stg = [sb("stg%d" % i, [128, 384], F32, ds) for i in range(4)]
            stgb = [Buf() for _ in range(4)]
            stgs = [dsem() for _ in range(4)]
            qkc = [sb("qkc%d" % i, [128, 128], BF16, ds) for i in range(3)]
            qkcb = [Buf() for _ in range(3)]
            rtA = [sb("rtA%d" % i, [128, 4, 16], F32, ds) for i in range(2)]
            rtB = [sb("rtB%d" % i, [128, 4, 16], F32, ds) for i in range(2)]
            rtb = [Buf() for _ in range(2)]
            qkbf = [sb("qkbf%d" % i, [128, 256], BF16, ds) for i in range(3)]
            qkbfb = [Buf() for _ in range(3)]
            sqn = [sb("sqn%d" % i, [128, 256], BF16, ds) for i in range(2)]
            red = [sb("red%d" % i, [128, 4], F32, ds) for i in range(2)]
            sqnb = [Buf() for _ in range(2)]
            nmax = sb("nmax", [128, 4], F32, ds)
            nmb = Buf()
            bnd = sb("bnd", [128, 4], F32, ds)
            negc = sb("negc", [128, 1], F32, ds)
            bndb = Buf()
            tS = [sb("atS%d" % i, [128, 512], F32, ds) for i in range(2)]
            tSb = [Buf() for _ in range(2)]
            racc = [sb("racc%d" % i, [128, 2, 2, 256], F32, ds) for i in range(2)]
            raccb = [Buf() for _ in range(2)]
            rbf = [sb("rbf%d" % i, [128, 2, 256], BF16, ds) for i in range(2)]
            rbfb = [Buf() for _ in range(2)]
            NP = 3
            PT = [sb("PT%d" % i, [128, 2, 2, 256], BF16, ds) for i in range(NP)]
            PTb = [Buf() for _ in range(NP)]
            rs = [sb("ars%d" % i, [128, 512], F32, ds) for i in range(2)]
            o1 = [sb("ao1%d" % i, [128, 2, 256], F32, ds) for i in range(2)]
            od = [sb("aod%d" % i, [128, 256], F32, ds) for i in range(2)]
            sqd = [sb("asq%d" % i, [128, 256], BF16, ds) for i in range(2)]
            rsd = [sb("arsd%d" % i, [128, 256], F32, ds) for i in range(2)]
            ydo = [sb("ydo%d" % i, [128, 256], BF16, ds) for i in range(2)]
            nb_ = [Buf() for _ in range(2)]
            ydob = [Buf() for _ in range(2)]
            ydsem = [dsem() for _ in range(2)]
            pctr = [0]
            actr = [0]

            PE_ = 'dve'
            RE_ = 'dve' if 'dverope' in DBG['skip'] else 'pool'
            for h in range(DBG['nh']):
                def issue_loads(hh):
                    a = load_w([(C_DQ + hh * 128, 128), (C_DK + hh * 128, 128), (C_DV + hh * 128, 128)])
                    b = load_w([(C_DG + hh * 128, 128)])
                    for q4 in range(4):
                        rs_ = slice(q4 * 512, (q4 + 1) * 512)
                        S.dma('pool', lambda e, o=kcb[:, q4 * 4:(q4 + 1) * 4, :], s=ck[rs_, hh * 128:(hh + 1) * 128].rearrange("(n p) c -> p n c", p=128):
                              e.dma_start(out=o, in_=s), kcsem, writes=[KCB])
                    return a, b

                if h == 0:
                    apre = issue_loads(0)
                (wqkv, wqkvb), (wdg, wdgb) = apre
                for q4 in range(4):
                    rs_ = slice(q4 * 512, (q4 + 1) * 512)
                    S.dma('pool', lambda e, o=Vall[:, 33 + q4 * 4:33 + (q4 + 1) * 4, :], s=cv[rs_, h * 128:(h + 1) * 128].rearrange("(n p) c -> p n c", p=128):
                          e.dma_start(out=o, in_=s), vcsem, writes=[HB])
                S.op('pool', lambda e: e.memset(nmax[:, :], 0.0), writes=[nmb])

                def norms(src, n, k, srcb):
                    G = src.shape[1] // 64
                    tt('dve', sqn[k][0:n, 0:G * 64], src, src, ALU.mult, [srcb], [sqnb[k]])
                    S.op('dve', lambda e: e.tensor_reduce(out=red[k][0:n, 0:G], in_=sqn[k][0:n, 0:G * 64].rearrange("p (g d) -> p g d", g=G),
                                                          axis=AX.X, op=ALU.add), reads=[sqnb[k]], writes=[sqnb[k]])
                    tt('dve', nmax[0:n, 0:G], nmax[0:n, 0:G], red[k][0:n, 0:G], ALU.max, [sqnb[k], nmb], [nmb])

                def stA(i):
                    t0, n = TILES[i]
                    k4 = i % 4
                    bk = i % 2
                    for kc in range(8):
                        mm(banks[bk][0:n, 0:384], xnT[:, kc, t0:t0 + n], wqkv[:, kc, 0:384], kc == 0, kc == 7, [wqkvb, XNT],
                           [bankbuf[bk][0]], kc == 7)
                    cp('act', stg[k4][0:n, :], banks[bk][0:n, 0:384], [bankbuf[bk][0]], [stgb[k4]])

                def stB(i):
                    t0, n = TILES[i]
                    k4 = i % 4
                    k = i % 2
                    k3 = i % 3
                    cp('act', Vall[0:n, i, :], stg[k4][0:n, 256:384], [stgb[k4]], [HB])
                    pq = stg[k4][0:n, 0:256].rearrange("p (g d) -> p g d", g=4)
                    c16 = rope_t[0:n, i, 0:16].unsqueeze(1).to_broadcast([n, 4, 16])
                    sa8 = rope_t[0:n, i, 16:24].unsqueeze(1).to_broadcast([n, 4, 8])
                    sb8 = rope_t[0:n, i, 24:32].unsqueeze(1).to_broadcast([n, 4, 8])
                    S.op(RE_, lambda e, o=rtA[k][0:n], a=pq[:, :, 0:16], b=c16: e.tensor_tensor(out=o, in0=a, in1=b, op=ALU.mult),
                         reads=[stgb[k4], CB], writes=[rtb[k]], signal=False)
                    S.op(RE_, lambda e, o=rtB[k][0:n, :, 0:8], a=pq[:, :, 8:16], b=sa8: e.tensor_tensor(out=o, in0=a, in1=b, op=ALU.mult),
                         reads=[stgb[k4], CB], writes=[rtb[k]], signal=False)
                    tt(RE_, rtB[k][0:n, :, 8:16], pq[:, :, 0:8], sb8, ALU.mult, [stgb[k4], CB], [rtb[k]])
                    tt(RE_, pq[:, :, 0:16], rtA[k][0:n], rtB[k][0:n], ALU.add, [rtb[k]], [stgb[k4]])
                    vdst = nvp[t0:t0 + n, h * 128:(h + 1) * 128] if t0 < T else nvs[0:n, h * 128:(h + 1) * 128]
                    kdst = nkp[t0:t0 + n, h * 128:(h + 1) * 128] if t0 < T else nks[0:n, h * 128:(h + 1) * 128]
                    S.dma('sp', lambda e, o=vdst, s=stg[k4][0:n, 256:384]: e.dma_start(out=o, in_=s), stgs[k4], reads=[stgb[k4]])
                    S.dma('sp', lambda e, o=kdst, s=stg[k4][0:n, 128:256]: e.dma_start(out=o, in_=s), stgs[k4], reads=[stgb[k4]])
                    cp('act', qkbf[k3][0:n, :], stg[k4][0:n, 0:256], [stgb[k4]], [qkbfb[k3]])
                    norms(qkbf[k3][0:n, :], n, k, qkbfb[k3])

                def stC(i):
                    t0, n = TILES[i]
                    k = i % 2
                    k3 = i % 3
                    pT = banks[2 + k].bitcast(BF16)
                    tr(pT[:, 0:n], qkbf[k3][0:n, 0:128], ident[0:n, 0:n], [qkbfb[k3]], [bankbuf[2 + k][0]], last=False)
                    tr(pT[:, 128:128 + n], qkbf[k3][0:n, 128:256], ident[0:n, 0:n], [qkbfb[k3]], [bankbuf[2 + k][0]])
                    cp('act', QT[:, t0:t0 + n], pT[:, 0:n], [bankbuf[2 + k][0]], [HB])
                    cp('act', KT[:, t0:t0 + n], pT[:, 128:128 + n], [bankbuf[2 + k][0]], [HB])

                def dgblk(bi):
                    t0, n = BLOCKS[bi]
                    k = bi % 2
                    for kc in range(8):
                        mm(banks[7][:, 0:n], wdg[:, kc, 0:128], xnT[:, kc, t0:t0 + n], kc == 0, kc == 7, [wdgb, XNT],
                           [bankbuf[7][0]], kc == 7)
                    act(tS[k][:, 0:n], banks[7][:, 0:n], AF.Silu, [bankbuf[7][0]], [tSb[k]])
                    ts('dve', dgs[:, t0:t0 + n], tS[k][:, 0:n], gsub08[:, 0:1], ALU.mult, [tSb[k], CB], [HB])

                NTL = len(TILES)
                ndg = 0
                for st_ in range(NTL + 2):
                    if st_ < NTL:
                        stA(st_)
                    if 0 <= st_ - 1 < NTL:
                        stB(st_ - 1)
                    if 0 <= st_ - 2 < NTL:
                        stC(st_ - 2)
                    if st_ % 4 == 3 and ndg < len(BLOCKS):
                        dgblk(ndg)
                        ndg += 1
                while ndg < len(BLOCKS):
                    dgblk(ndg)
                    ndg += 1

                def caA(j):
                    k3 = j % 3
                    cp('dve', qkc[k3][:, :], kcb[:, j, :], [KCB], [qkcb[k3]])
                    norms(qkc[k3][:, :], 128, j % 2, qkcb[k3])

                def caC(j):
                    k = j % 2
                    k3 = j % 3
                    pT = banks[2 + k].bitcast(BF16)
                    tr(pT[:, 0:128], qkc[k3][:, :], ident[:, :], [qkcb[k3]], [bankbuf[2 + k][0]])
                    cp('act', KT[:, TT + j * 128:TT + (j + 1) * 128], pT[:, 0:128], [bankbuf[2 + k][0]], [HB])

                for st_ in range(17):
                    if st_ < 16:
                        caA(st_)
                    if st_ >= 1:
                        caC(st_ - 1)
                if DBG['stage'] < 1:
                    continue
                S.op('dve', lambda e: e.tensor_reduce(out=bnd[:, 0:1], in_=nmax[:, :], axis=AX.X, op=ALU.max), reads=[nmb], writes=[bndb])
                S.op('pe', lambda e: e.transpose(out=banks[7][0:1, 0:128], in_=bnd[:, 0:1], identity=identf[:, :]),
                     reads=[bndb, CB], writes=[bankbuf[7][0]])
                S.op('dve', lambda e: e.tensor_reduce(out=bnd[0:1, 1:2], in_=banks[7][0:1, 0:128], axis=AX.X, op=ALU.max),
                     reads=[bankbuf[7][0]], writes=[bndb])
                S.op('pe', lambda e: e.matmul(banks[7][:, 256:257], lhsT=ones_f[0:1, :], rhs=bnd[0:1, 1:2], start=True, stop=True),
                     reads=[bndb, CB], writes=[bankbuf[7][1]])
                ts('dve', negc[:, 0:1], banks[7][:, 256:257], -0.125, ALU.mult, [bankbuf[7][1]], [bndb])

                if h + 1 < DBG['nh']:
                    apre = issue_loads(h + 1)
                jobs = []

                def add_block(q0, nq, ktiles):
                    w = actr[0] % 2
                    actr[0] += 1
                    pairs = [ktiles[a:a + 2] for a in range(0, len(ktiles), 2)]
                    for g, pr in enumerate(pairs):
                        jobs.append(dict(w=w, q0=q0, nq=nq, pr=pr, g=g, np=len(pairs)))

                if DBG['stage'] >= 2:
                    for qb in (DBG['qbl'] if DBG.get('qbl') else range(DBG['qb0'], DBG['nqb'])):
                        kts = [(j * 128, 128, j, None) for j in range(2 * qb)]
                        kts += [((2 * qb) * 128, 128, 2 * qb, 0), ((2 * qb + 1) * 128, 128, 2 * qb + 1, 1)]
                        add_block(qb * 256, 256, kts)
                if DBG['stage'] >= 3:
                    add_block(T, TS, [(TT + j * 128, 128, 33 + j, None) for j in range(16)] + [(T, TS, 32, None)])

                def qk(ji):
                    jb = jobs[ji]
                    b0 = 2 + 2 * (ji % 2)
                    pr, q0, nq = jb['pr'], jb['q0'], jb['nq']
                    for t, (kc0, nk, vt, r) in enumerate(pr):
                        for m in range(2):
                            mm(banks[b0 + m][0:nk, t * 256:t * 256 + nq], KT[m * 64:(m + 1) * 64, kc0:kc0 + nk],
                               QT[m * 64:(m + 1) * 64, q0:q0 + nq], True, True, [HB], [bankbuf[b0 + m][0]],
                               t == len(pr) - 1 and m == 1)

                def norm_closures(jb):
                    w, q0, nq = jb['w'], jb['q0'], jb['nq']
                    full = (nq == 256)
                    pvi, psi = (6, 7) if w == 0 else (0, 1)
                    pv, psm = banks[pvi], banks[psi]
                    pvb, psb = bankbuf[pvi][0], bankbuf[psi][0]
                    pv3 = pv[:, 0:512].rearrange("p (m q) -> p m q", m=2)[:, :, 0:nq]
                    ps3 = psm[:, 0:512].rearrange("p (m q) -> p m q", m=2)[:, :, 0:nq]
                    rs3 = rs[w][:, :].rearrange("p (m q) -> p m q", m=2)[:, :, 0:nq]

                    def n1():
                        tt('dve', rbf[w][:, :, 0:nq], racc[w][:, 0, :, 0:nq], racc[w][:, 1, :, 0:nq], ALU.add, [raccb[w]], [rbfb[w]])
                        if full:
                            mm(psm[:, 0:512], ones_bf[:, :], rbf[w][:, :, :].rearrange("p m q -> p (m q)"), True, True,
                               [rbfb[w], CB], [psb], True)
                        else:
                            for m in range(2):
                                mm(psm[:, m * 256:m * 256 + nq], ones_bf[:, :], rbf[w][:, m, 0:nq], m == 0, m == 1, [rbfb[w], CB], [psb], m == 1)

                    def n2():
                        act(rs3, ps3, AF.Ln, [psb], [nb_[w]])
                        act(rs3, rs3, AF.Exp, [nb_[w]], [nb_[w]], scale=-1.0)

                    def n3():
                        tt('dve', o1[w][:, :, 0:nq], pv3, rs3, ALU.mult, [pvb, nb_[w]], [nb_[w]])
                        stt('dve', od[w][:, 0:nq], o1[w][:, 1, 0:nq], neg_lam[:, 0:1], o1[w][:, 0, 0:nq], ALU.mult, ALU.add,
                            [nb_[w], CB], [nb_[w]])

                    def n4():
                        act(sqd[w][:, 0:nq], od[w][:, 0:nq], AF.Square, [nb_[w]], [nb_[w]])
                        mm(psm[:, 0:nq], ones_bf[:, :], sqd[w][:, 0:nq], True, True, [nb_[w], CB], [psb], True)

                    def n5():
                        act(rsd[w][:, 0:nq], psm[:, 0:nq], AF.Ln, [psb], [nb_[w]], scale=1.0 / 128, bias=EPS)
                        act(rsd[w][:, 0:nq], rsd[w][:, 0:nq], AF.Exp, [nb_[w]], [nb_[w]], scale=-0.5)

                    def n6():
                        tt('dve', od[w][:, 0:nq], od[w][:, 0:nq], rsd[w][:, 0:nq], ALU.mult, [nb_[w]], [nb_[w]])
                        tt('dve', ydo[w][:, 0:nq], od[w][:, 0:nq], dgs[:, q0:q0 + nq], ALU.mult, [nb_[w], HB], [ydob[w]])
                        S.dma('sp', lambda e, o=ydT_d[h * 128:(h + 1) * 128, q0:q0 + nq], s=ydo[w][:, 0:nq]: e.dma_start(out=o, in_=s),
                              ydsem[w], reads=[ydob[w]])

                    return [n1, n2, n3, n4, n5, n6]

                deferred = []
                if jobs:
                    qk(0)
                for ji, jb in enumerate(jobs):
                    w, q0, nq, pr, g, npairs = jb['w'], jb['q0'], jb['nq'], jb['pr'], jb['g'], jb['np']
                    full = (nq == 256)
                    pvi = 6 if w == 0 else 0
                    pv, pvb = banks[pvi], bankbuf[pvi][0]
                    nt = len(pr)
                    if ji + 1 < len(jobs):
                        qk(ji + 1)
                    pi = pctr[0] % NP
                    pctr[0] += 1
                    b0 = 2 + 2 * (ji % 2)
                    sbufs = [bankbuf[b0][0], bankbuf[b0 + 1][0], bndb]
                    same_nk = all(p_[1] == pr[0][1] for p_ in pr)
                    if full and same_nk:
                        nk = pr[0][1]
                        src = psum_all[0:nk, b0:b0 + 2, 0:nt * 256].rearrange("p m (t q) -> p m t q", t=nt)
                        dst = PT[pi][0:nk, 0:nt, :, :].rearrange("p t m q -> p m t q")
                        act(dst, src, AF.Exp, sbufs, [PTb[pi]], scale=0.125, bias=negc[0:nk, 0:1])
                    else:
                        for t, (kc0, nk, vt, r) in enumerate(pr):
                            act(PT[pi][0:nk, t, :, 0:nq], psum_all[0:nk, b0:b0 + 2, t * 256:t * 256 + nq], AF.Exp, sbufs,
                                [PTb[pi]], scale=0.125, bias=negc[0:nk, 0:1])
                    if pr[0][3] is not None:
                        for mi_, ap_ in enumerate((PT[pi][64:128, 0, :, 0:64], PT[pi][0:64, 1, :, 0:128], PT[pi][64:128, 1, :, 0:192])):
                            S.op('dve', lambda e, ap_=ap_: e.memset(ap_, 0.0), reads=[], writes=[PTb[pi]], signal=(mi_ == 2))
                    if same_nk:
                        nk = pr[0][1]
                        if g == 0:
                            cp('dve', racc[w][0:nk, 0:nt, :, 0:nq], PT[pi][0:nk, 0:nt, :, 0:nq], [PTb[pi]], [raccb[w]])
                        else:
                            tt('dve', racc[w][0:nk, 0:nt, :, 0:nq], racc[w][0:nk, 0:nt, :, 0:nq], PT[pi][0:nk, 0:nt, :, 0:nq],
                               ALU.add, [PTb[pi], raccb[w]], [raccb[w]])
                    else:
                        for t, (kc0, nk, vt, r) in enumerate(pr):
                            tt('dve', racc[w][0:nk, t, :, 0:nq], racc[w][0:nk, t, :, 0:nq], PT[pi][0:nk, t, :, 0:nq],
                               ALU.add, [PTb[pi], raccb[w]], [raccb[w]])
                    for t, (kc0, nk, vt, r) in enumerate(pr):
                        first = (g == 0 and t == 0)
                        last = (g == npairs - 1 and t == nt - 1)
                        if full:
                            mm(pv[:, 0:512], Vall[0:nk, vt, :], PT[pi][0:nk, t, :, :].rearrange("p m q -> p (m q)"), first, last,
                               [PTb[pi], HB], [pvb], t == nt - 1)
                        else:
                            for m in range(2):
                                mm(pv[:, m * 256:m * 256 + nq], Vall[0:nk, vt, :], PT[pi][0:nk, t, m, 0:nq], first and m == 0,
                                   last and m == 1, [PTb[pi], HB], [pvb], t == nt - 1 and m == 1)
                    if g == npairs - 1:
                        while deferred:
                            deferred.pop(0)()
                        deferred = norm_closures(jb)
                    elif deferred:
                        deferred.pop(0)()
                while deferred:
                    deferred.pop(0)()
            S.barrier()

        aoff[0] = mark_const
        with ExitStack() as ms:
          if stop is None or stop >= 4:
            wgs = sb("wgs", [128, 8, D], BF16, ms)
            wds = sb("wds", [128, 8, D], BF16, ms)
            wos = sb("wos", [128, 8, D], BF16, ms)
            wms = sb("wms", [128, 8, 2 * D], BF16, ms)
            WB = Buf()
            wrs = dsem()
            for q4 in range(4):
                c = slice(q4 * 256, (q4 + 1) * 256)
                for dst, src in ((wgs, wpg), (wds, wpd), (wos, wo)):
                    S.dma('pool', lambda e, o=dst[:, :, c], s=src[:, :, c]: e.dma_start(out=o, in_=s), wrs, writes=[WB])
            for q4 in range(8):
                S.dma('pool', lambda e, o=wms[:, :, q4 * 256:(q4 + 1) * 256], s=wl[:, :, C_MA + q4 * 256:C_MA + (q4 + 1) * 256]:
                      e.dma_start(out=o, in_=s), wrs, writes=[WB])
            xblk = [sb("xblk%d" % i, [128, 4, D], F32, ms) for i in range(2)]
            xblkb = [[Buf() for _ in range(4)] for _ in range(2)]
            xbs = [[dsem() for _ in range(4)] for _ in range(2)]
            xnTb = [sb("xnTb%d" % i, [128, 8, 512], BF16, ms) for i in range(2)]
            xnTbb = [Buf() for _ in range(2)]
            ygb = [sb("ygb%d" % i, [128, 8, 512], BF16, ms) for i in range(1)] * 2
            ydb = [sb("ydb%d" % i, [128, 8, 512], BF16, ms) for i in range(1)] * 2
            ygbb = [Buf()] * 2
            ydbb = [Buf()] * 2
            ygs_ = [dsem()] * 2
            yds_ = [dsem()] * 2
            mT = [sb("mT%d" % i, [128, 8, 512], BF16, ms) for i in range(1)] * 2
            mTb = [Buf()] * 2
            sga = [sb("sga%d" % i, [128, 512], F32, ms) for i in range(2)]
            sgb = [sb("sgb%d" % i, [128, 512], F32, ms) for i in range(2)]
            sgab = [Buf() for _ in range(2)]
            sgbb = [Buf() for _ in range(2)]
            ost = [sb("ost%d" % i, [128, D], F32, ms) for i in range(2)]
            ostb = [Buf() for _ in range(2)]
            osts = [dsem() for _ in range(2)]
            fst2 = sb("fst", [128, 16], F32, ms)
            fstb2 = [Buf(), Buf()]
            tctr = [0]

            def stage_load(bi):
                t0, n = BLOCKS[bi]
                p = bi % 2
                for tl in range((n + 127) // 128):
                    tn = min(128, n - tl * 128)
                    S.dma('sp', lambda e, o=xblk[p][0:tn, tl, :], s=xrows(t0 + tl * 128, tn): e.dma_start(out=o, in_=s),
                          xbs[p][tl], writes=[xblkb[p][tl]])
                    norm_tile(tctr[0], t0 + tl * 128, tn, xblk[p][:, tl, :], xblkb[p][tl], xnTb[p], xnTbb[p], tl * 128)
                    tctr[0] += 1

            stage_load(0)
            for bi, (t0, n) in enumerate(BLOCKS):
                p = bi % 2
                ntl = (n + 127) // 128
                S.dma('sp', lambda e, o=ygb[p][:, :, 0:n], s=ygT_d[:, t0:t0 + n].rearrange("(k p) t -> p k t", p=128):
                      e.dma_start(out=o, in_=s), ygs_[p], writes=[ygbb[p]])
                S.dma('sp', lambda e, o=ydb[p][:, :, 0:n], s=ydT_d[:, t0:t0 + n].rearrange("(k p) t -> p k t", p=128):
                      e.dma_start(out=o, in_=s), yds_[p], writes=[ydbb[p]])
                for ct in range(8):
                    k = ct % 2
                    cs_ = slice(ct * 128, (ct + 1) * 128)
                    for kc in range(8):
                        mm(banks[2][:, 0:n], wms[:, kc, cs_], xnTb[p][:, kc, 0:n], kc == 0, kc == 7, [WB, xnTbb[p]], [bankbuf[2][0]], kc == 7)
                    for kc in range(8):
                        mm(banks[3][:, 0:n], wms[:, kc, D + ct * 128:D + (ct + 1) * 128], xnTb[p][:, kc, 0:n], kc == 0, kc == 7,
                           [WB, xnTbb[p]], [bankbuf[3][0]], kc == 7)
                    for kc in range(8):
                        mm(banks[4][:, 0:n], wgs[:, kc, cs_], ygb[p][:, kc, 0:n], kc == 0, kc == 7, [WB, ygbb[p]], [bankbuf[4][0]], kc == 7)
                    for kc in range(8):
                        mm(banks[5][:, 0:n], wds[:, kc, cs_], ydb[p][:, kc, 0:n], kc == 0, kc == 7, [WB, ydbb[p]], [bankbuf[5][0]], kc == 7)
                    act(sga[k][:, 0:n], banks[2][:, 0:n], AF.Sigmoid, [bankbuf[2][0]], [sgab[k]])
                    act(sgb[k][:, 0:n], banks[3][:, 0:n], AF.Sigmoid, [bankbuf[3][0]], [sgbb[k]])
                    tt('dve', sga[k][:, 0:n], banks[4][:, 0:n], sga[k][:, 0:n], ALU.mult, [bankbuf[4][0], sgab[k]], [sgab[k]])
                    tt('dve', sgb[k][:, 0:n], banks[5][:, 0:n], sgb[k][:, 0:n], ALU.mult, [bankbuf[5][0], sgbb[k]], [sgbb[k]])
                    tt('dve', mT[p][:, ct, 0:n], sga[k][:, 0:n], sgb[k][:, 0:n], ALU.add, [sgab[k], sgbb[k]], [mTb[p]])
                    if ct == 7 and bi + 1 < len(BLOCKS):
                        stage_load(bi + 1)
                for tl in range(ntl):
                    tn = min(128, n - tl * 128)
                    k = tl % 2
                    fb = (6, 7) if tl % 2 == 0 else (0, 1)
                    fst = fst2[:, 8 * k:8 * k + 8]
                    fstb = fstb2[k]
                    for hf in range(2):
                        for ct in range(8):
                            mm(banks[fb[hf]][0:tn, :], mT[p][:, ct, tl * 128:tl * 128 + tn], wos[:, ct, hf * 512:(hf + 1) * 512],
                               ct == 0, ct == 7, [WB, mTb[p]], [bankbuf[fb[hf]][0]], ct == 7)
                    for hf in range(2):
                        act(junk[0:tn, hf * 512:(hf + 1) * 512], banks[fb[hf]][0:tn, :], AF.Square, [bankbuf[fb[hf]][0]], [junkb, fstb],
                            accum_out=fst[0:tn, hf:hf + 1])
                    tt('dve', fst[0:tn, 2:3], fst[0:tn, 0:1], fst[0:tn, 1:2], ALU.add, [fstb], [fstb])
                    act(fst[0:tn, 3:4], fst[0:tn, 2:3], AF.Ln, [fstb], [fstb], scale=1.0 / D, bias=EPS)
                    act(fst[0:tn, 4:5], fst[0:tn, 3:4], AF.Exp, [fstb], [fstb], scale=-0.5)
                    for hf in range(2):
                        hs = slice(hf * 512, (hf + 1) * 512)
                        stt('dve', ost[k][0:tn, hs], banks[fb[hf]][0:tn, :], fst[0:tn, 4:5], gpost_bc[0:tn, hs], ALU.mult, ALU.mult,
                            [bankbuf[fb[hf]][0], fstb, CB], [ostb[k]])
                        tt('dve', ost[k][0:tn, hs], ost[k][0:tn, hs], xblk[p][0:tn, tl, hs], ALU.add, [ostb[k], xblkb[p][tl]], [ostb[k]])
                    r0 = t0 + tl * 128
                    ydst = yp[r0:r0 + tn, :] if r0 < T else ys[0:tn, :]
                    S.dma('sp', lambda e, o=ydst, s=ost[k][0:tn, :]: e.dma_start(out=o, in_=s), osts[k], reads=[ostb[k]])
            S.barrier()

        with nc.Block() as block:
            S.emit(block)
    return nc


_NC = [None]


def _rope_table():
    half = 8
    inv = 500000.0 ** (-np.arange(0, 16, 2, dtype=np.float32) / np.float32(16))
    pos = np.concatenate([np.arange(T), PAST + np.arange(TS)]).astype(np.float32)
    ang = (pos[:, None] * inv[None, :].astype(np.float32)).astype(np.float32)
    c = np.cos(ang).astype(np.float32)
    s = np.sin(ang).astype(np.float32)
    return np.ascontiguousarray(np.concatenate([c, c, -s, s], axis=1).astype(np.float32))


def _lay(w):
    return np.ascontiguousarray(w.reshape(8, 128, w.shape[1]).transpose(1, 0, 2))


def kernel(x_prompt, x_sample, cache_diff_k, cache_diff_v, state_gla, pre_norm_g, w_in, gla_w_a2, gla_b_a, gla_norm_g,
           diff_lambda_q1, diff_lambda_k1, diff_lambda_q2, diff_lambda_k2, diff_subln_g, w_proj_gla, w_proj_diff, w_out,
           post_norm_g):
    f = lambda a: np.ascontiguousarray(np.asarray(a, dtype=np.float32))
    if _NC[0] is None:
        _NC[0] = build()
    nc = _NC[0]
    shared = {
        "wl": _lay(f(w_in)[0]), "wa2": f(gla_w_a2)[0], "ba": np.ascontiguousarray(f(gla_b_a)[0].reshape(4, 128).T),
        "g_pre": f(pre_norm_g), "g_post": f(post_norm_g),
        "g_gla": np.ascontiguousarray(f(gla_norm_g)[0].reshape(2, 128).T), "g_sub": f(diff_subln_g)[0].reshape(128, 1),
        "lam4": np.concatenate([f(diff_lambda_q1), f(diff_lambda_k1), f(diff_lambda_q2), f(diff_lambda_k2)], axis=1),
        "wpg": _lay(f(w_proj_gla)[0]), "wpd": _lay(f(w_proj_diff)[0]), "wo": _lay(f(w_out)[0]), "rope": _rope_table(),
    }
    xpf, xsf, ckf, cvf, sgf = f(x_prompt), f(x_sample), f(cache_diff_k), f(cache_diff_v), f(state_gla)
    in_maps = []
    for i in range(8):
        m = dict(shared)
        m.update({"xp": xpf[i], "xs": xsf[i], "ck": ckf[0, i].reshape(PAST, D), "cv": cvf[0, i].reshape(PAST, D), "sg": sgf[0, i]})
        in_maps.append(m)
    res = run_bass_kernel_spmd(nc, in_maps, core_ids=list(range(8)))
    R = res.results
    st = lambda k: np.stack([np.asarray(R[i][k], dtype=np.float32) for i in range(8)])
    return (st("yp"), st("ys"), st("nkp").reshape(1, 8, T, 8, 128), st("nvp").reshape(1, 8, T, 8, 128),
            st("ngp").reshape(1, 8, 4, 128, 256), st("nks").reshape(1, 8, TS, 8, 128), st("nvs").reshape(1, 8, TS, 8, 128),
            st("ngs").reshape(1, 8, 4, 128, 256))
```
